# Optimizing a Trainium2 kernel written in Bass

```python
import jax, jax.numpy as jnp
from jax import lax
import numpy as np

D_MODEL = 2048
BATCH = 2
SEQ = 8192
DEPTH = 2

HEAD_DIM = 128
N_SB_HEADS = 12
N_MEM_HEADS = 4
N_MLA_HEADS = 12
MEM_LEN = 256
Q_LORA_RANK = 512
KV_LORA_RANK = 512
QK_NOPE_DIM = 128
QK_ROPE_DIM = 64
V_HEAD_DIM = 128
ROPE_THETA = 10000.0
BLOCK_Q = 128
EPS = 1e-6
N_A_LAYERS = DEPTH // 2
N_B_LAYERS = DEPTH - N_A_LAYERS
SB_W = N_SB_HEADS * HEAD_DIM
MEM_W = N_MEM_HEADS * HEAD_DIM
MLA_W = N_MLA_HEADS * V_HEAD_DIM
A_IN_SIZES = (SB_W, SB_W, SB_W, SB_W, MEM_W, MEM_W)
B_IN_SIZES = (Q_LORA_RANK, MLA_W, MEM_W, MEM_W)
A_IN_W = sum(A_IN_SIZES)
B_IN_W = sum(B_IN_SIZES)
MIX_A_W = SB_W + MEM_W
MIX_B_W = MLA_W + MEM_W

kernel_name = "yoco_stickbreak_mla_memory_hybrid"


def _split(x, sizes):
    idx = [int(i) for i in np.cumsum(sizes)[:-1]]
    return jnp.split(x, idx, axis=-1)


def rmsnorm(x, g):
    xf = x.astype(jnp.float32)
    y = xf * lax.rsqrt(jnp.mean(xf * xf, axis=-1, keepdims=True) + EPS)
    return (y * g.astype(jnp.float32)).astype(x.dtype)


def rope_tables(positions):
    inv_freq = jnp.power(ROPE_THETA, -jnp.arange(0, QK_ROPE_DIM, 2, dtype=jnp.float32) / QK_ROPE_DIM)
    ang = positions.astype(jnp.float32)[..., None] * inv_freq
    return jnp.cos(ang), jnp.sin(ang)


def apply_rope(x, cos, sin):
    half = x.shape[-1] // 2
    x1, x2 = x[..., :half], x[..., half:]
    return jnp.concatenate([x1 * cos - x2 * sin, x2 * cos + x1 * sin], axis=-1).astype(x.dtype)


def _to_blocks(t):
    b, s = t.shape[0], t.shape[1]
    return t.reshape(b, s // BLOCK_Q, BLOCK_Q, *t.shape[2:]).swapaxes(0, 1)


def _from_blocks(t):
    t = t.swapaxes(0, 1)
    return t.reshape(t.shape[0], t.shape[1] * t.shape[2], *t.shape[3:])


def stick_breaking_attention(q, k, v):
    s_len, d = q.shape[1], q.shape[-1]
    scale = d ** -0.5
    key_pos = jnp.arange(s_len)

    def block(args):
        qi, bi = args
        z = jnp.einsum('bqhd,bkhd->bhqk', qi, k).astype(jnp.float32) * scale
        q_pos = bi * BLOCK_Q + jnp.arange(BLOCK_Q)
        causal = key_pos[None, :] < q_pos[:, None]
        log_1m_beta = jnp.where(causal, jax.nn.log_sigmoid(-z), 0.0)
        between = lax.cumsum(log_1m_beta, axis=3, reverse=True) - log_1m_beta
        a = jnp.where(causal, jnp.exp(jax.nn.log_sigmoid(z) + between), 0.0)
        return jnp.einsum('bhqk,bkhd->bqhd', a.astype(v.dtype), v)

    nblk = s_len // BLOCK_Q
    out = lax.map(block, (_to_blocks(q), jnp.arange(nblk)))
    return _from_blocks(out)


def mla_causal_attention(q_nope, q_rope, k_nope, k_rope, v):
    s_len = q_nope.shape[1]
    scale = (QK_NOPE_DIM + QK_ROPE_DIM) ** -0.5
    key_pos = jnp.arange(s_len)

    def block(args):
        qn, qr, bi = args
        s = (jnp.einsum('bqhd,bkhd->bhqk', qn, k_nope)
             + jnp.einsum('bqhr,bkr->bhqk', qr, k_rope)).astype(jnp.float32) * scale
        q_pos = bi * BLOCK_Q + jnp.arange(BLOCK_Q)
        mask = key_pos[None, :] <= q_pos[:, None]
        p = jax.nn.softmax(jnp.where(mask, s, -jnp.inf), axis=-1)
        return jnp.einsum('bhqk,bkhd->bqhd', p.astype(v.dtype), v)

    nblk = s_len // BLOCK_Q
    out = lax.map(block, (_to_blocks(q_nope), _to_blocks(q_rope), jnp.arange(nblk)))
    return _from_blocks(out)


def memory_attention(q_m, mem, g_norm, w_kv, g_q, g_k):
    b, s_len, _ = q_m.shape
    mk, mv = _split(rmsnorm(mem, g_norm) @ w_kv, (MEM_W, MEM_W))
    mk = rmsnorm(mk.reshape(b, -1, N_MEM_HEADS, HEAD_DIM), g_k)
    mv = mv.reshape(b, -1, N_MEM_HEADS, HEAD_DIM)
    q = rmsnorm(q_m.reshape(b, s_len, N_MEM_HEADS, HEAD_DIM), g_q)
    s = jnp.einsum('bqhd,bmhd->bhqm', q, mk).astype(jnp.float32) * HEAD_DIM ** -0.5
    p = jax.nn.softmax(s, axis=-1)
    return jnp.einsum('bhqm,bmhd->bqhd', p.astype(mv.dtype), mv).reshape(b, s_len, MEM_W)


def setup_inputs(seed: int = 0) -> dict:
    key = jax.random.key(seed)
    ks = iter(jax.random.split(key, 32))
    f32 = jnp.float32

    def w(shape, fan_in):
        return jax.random.normal(next(ks), shape, f32) * fan_in ** -0.5

    def gain(shape):
        return 1.0 + 0.02 * jax.random.normal(next(ks), shape, f32)

    x = jax.random.normal(next(ks), (BATCH, SEQ, D_MODEL), f32)
    mem = jax.random.normal(next(ks), (BATCH, MEM_LEN, D_MODEL), f32)
    positions = jnp.broadcast_to(jnp.arange(SEQ, dtype=jnp.int32)[None, :], (BATCH, SEQ))
    return {
        "x": x,
        "mem": mem,
        "positions": positions,
        "a_norm": gain((N_A_LAYERS, D_MODEL)),
        "a_w_in": w((N_A_LAYERS, D_MODEL, A_IN_W), D_MODEL),
        "a_w_out": w((N_A_LAYERS, MIX_A_W, D_MODEL), MIX_A_W),
        "kv_norm": gain((D_MODEL,)),
        "w_dkv": w((D_MODEL, KV_LORA_RANK + QK_ROPE_DIM), D_MODEL),
        "g_ckv": gain((KV_LORA_RANK,)),
        "w_ukv": w((KV_LORA_RANK, N_MLA_HEADS * (QK_NOPE_DIM + V_HEAD_DIM)), KV_LORA_RANK),
        "g_k_nope": gain((QK_NOPE_DIM,)),
        "g_k_rope": gain((QK_ROPE_DIM,)),
        "b_norm": gain((N_B_LAYERS, D_MODEL)),
        "b_w_in": w((N_B_LAYERS, D_MODEL, B_IN_W), D_MODEL),
        "b_g_q_lat": gain((N_B_LAYERS, Q_LORA_RANK)),
        "b_w_uq": w((N_B_LAYERS, Q_LORA_RANK, N_MLA_HEADS * (QK_NOPE_DIM + QK_ROPE_DIM)), Q_LORA_RANK),
        "b_g_q_nope": gain((N_B_LAYERS, QK_NOPE_DIM)),
        "b_g_q_rope": gain((N_B_LAYERS, QK_ROPE_DIM)),
        "b_w_out": w((N_B_LAYERS, MIX_B_W, D_MODEL), MIX_B_W),
        "mem_norm": gain((DEPTH, D_MODEL)),
        "w_mem_kv": w((DEPTH, D_MODEL, 2 * MEM_W), D_MODEL),
        "g_mem_q": gain((DEPTH, HEAD_DIM)),
        "g_mem_k": gain((DEPTH, HEAD_DIM)),
    }


def reference(x, mem, positions, a_norm, a_w_in, a_w_out, kv_norm, w_dkv, g_ckv, w_ukv,
              g_k_nope, g_k_rope, b_norm, b_w_in, b_g_q_lat, b_w_uq, b_g_q_nope, b_g_q_rope,
              b_w_out, mem_norm, w_mem_kv, g_mem_q, g_mem_k):
    b, s_len, _ = x.shape
    cos, sin = rope_tables(positions)
    cos_h, sin_h = cos[:, :, None, :], sin[:, :, None, :]
    shared = None
    for layer in range(DEPTH):
        if layer < N_A_LAYERS:
            i = layer
            h = rmsnorm(x, a_norm[i])
            q, k, v, g_sb, q_m, g_m = _split(h @ a_w_in[i], A_IN_SIZES)
            heads = lambda t: t.reshape(b, s_len, N_SB_HEADS, HEAD_DIM)
            sb = stick_breaking_attention(heads(q), heads(k), heads(v)).reshape(b, s_len, SB_W)
            mo = memory_attention(q_m, mem, mem_norm[layer], w_mem_kv[layer], g_mem_q[layer], g_mem_k[layer])
            mixed = jnp.concatenate([sb * jax.nn.silu(g_sb), mo * jax.nn.silu(g_m)], axis=-1)
            x = x + mixed @ a_w_out[i]
        else:
            j = layer - N_A_LAYERS
            if shared is None:
                c_kv, k_r = _split(rmsnorm(x, kv_norm) @ w_dkv, (KV_LORA_RANK, QK_ROPE_DIM))
                kv = (rmsnorm(c_kv, g_ckv) @ w_ukv).reshape(b, s_len, N_MLA_HEADS, QK_NOPE_DIM + V_HEAD_DIM)
                k_nope = rmsnorm(kv[..., :QK_NOPE_DIM], g_k_nope)
                v_mla = kv[..., QK_NOPE_DIM:]
                k_rope = apply_rope(rmsnorm(k_r, g_k_rope), cos, sin)
                shared = (k_nope, k_rope, v_mla)
            k_nope, k_rope, v_mla = shared
            h = rmsnorm(x, b_norm[j])
            q_lat, g_mla, q_m, g_m = _split(h @ b_w_in[j], B_IN_SIZES)
            q = (rmsnorm(q_lat, b_g_q_lat[j]) @ b_w_uq[j]).reshape(
                b, s_len, N_MLA_HEADS, QK_NOPE_DIM + QK_ROPE_DIM)
            q_nope = rmsnorm(q[..., :QK_NOPE_DIM], b_g_q_nope[j])
            q_rope = apply_rope(rmsnorm(q[..., QK_NOPE_DIM:], b_g_q_rope[j]), cos_h, sin_h)
            att = mla_causal_attention(q_nope, q_rope, k_nope, k_rope, v_mla).reshape(b, s_len, MLA_W)
            mo = memory_attention(q_m, mem, mem_norm[layer], w_mem_kv[layer], g_mem_q[layer], g_mem_k[layer])
            mixed = jnp.concatenate([att * jax.nn.silu(g_mla), mo * jax.nn.silu(g_m)], axis=-1)
            x = x + mixed @ b_w_out[j]
    return x
```

```python
import contextlib
import numpy as np
import ml_dtypes
import concourse.bass as bass
import concourse.mybir as mybir
from concourse.bass_utils import run_bass_kernel_spmd

F32 = mybir.dt.float32
BF16 = mybir.dt.bfloat16
I32 = mybir.dt.int32
AF = mybir.ActivationFunctionType
ALU = mybir.AluOpType
NPBF = ml_dtypes.bfloat16

ENGS = ("pe", "act", "dve", "pool", "sp")
NDSEM = 8
import os
SAFE_SAME_ENGINE = os.environ.get("UNSAFE_SE") is None


class Op:
    __slots__ = ("eng", "fn", "dma", "deps", "needed", "sig", "idx", "cc")

    def __init__(self, eng, fn, dma):
        self.eng, self.fn, self.dma = eng, fn, dma
        self.deps = []
        self.needed = False
        self.sig = None
        self.cc = False


class Prog:
    def __init__(self, nc):
        self.nc = nc
        self.ops = {e: [] for e in ENGS}
        self.last_w = {}
        self.readers = {}
        self.dmas = []
        self.out_dmas = []
        self.ccs = []
        self.bar_idx = 0

    def barrier(self, wait_cc=True):
        deps = []
        for e in ENGS:
            for op in reversed(self.ops[e]):
                if not op.dma and op.fn is not None:
                    deps.append(op)
                    break
        deps.extend(self.dmas[self.bar_idx:])
        self.bar_idx = len(self.dmas)
        if wait_cc:
            deps.extend(self.ccs)
        for d in deps:
            d.needed = True
        for e in ENGS:
            op = Op(e, None, False)
            op.deps = [d for d in deps if d.dma or d.cc or d.eng != e]
            self.ops[e].append(op)

    def add_cc(self, fn, reads=(), writes=(), deps=()):
        op = self.add("pool", fn, reads, writes, dma=True)
        self.dmas.pop()
        op.cc = True
        for d in deps:
            d.needed = True
            op.deps.append(d)
        if op.deps and op.deps[-1].dma and len(self.dmas) >= NDSEM and op.deps[-1] is self.dmas[len(self.dmas) - NDSEM]:
            pass
        self.ccs.append(op)
        return op

    def _need(self, d, op):
        if d is op:
            return False
        if d.dma or op.dma:
            return True
        if d.eng == op.eng:
            if d.eng == "pe":
                return False
            return SAFE_SAME_ENGINE
        return True

    def add(self, eng, fn, reads=(), writes=(), dma=False, out=False):
        op = Op(eng, fn, dma)
        deps = []
        for k in reads:
            w = self.last_w.get(k)
            if w is not None:
                deps.append(w)
        for k in writes:
            w = self.last_w.get(k)
            if w is not None:
                deps.append(w)
            deps.extend(self.readers.get(k, ()))
        seen = set()
        for d in deps:
            if id(d) in seen or not self._need(d, op):
                continue
            seen.add(id(d))
            d.needed = True
            op.deps.append(d)
        for k in writes:
            self.last_w[k] = op
            self.readers[k] = []
        for k in reads:
            lst = self.readers.setdefault(k, [])
            if not dma:
                lst[:] = [r for r in lst if r.dma or r.eng != eng]
            lst.append(op)
        if dma:
            n = len(self.dmas)
            if n >= NDSEM:
                op.deps.append(self.dmas[n - NDSEM])
            self.dmas.append(op)
            op.needed = True
            if out:
                self.out_dmas.append(op)
        self.ops[eng].append(op)
        return op

    def emit(self):
        nc = self.nc
        fin = Op("sp", None, False)
        fin.deps = list(self.out_dmas)
        self.ops["sp"].append(fin)
        with contextlib.ExitStack() as es:
            esem = {e: es.enter_context(nc.semaphore("s_" + e)) for e in ENGS}
            dsem = [es.enter_context(nc.semaphore("d_%d" % i)) for i in range(NDSEM)]
            for n, op in enumerate(self.dmas):
                op.sig = (dsem[n % NDSEM], 16 * (n // NDSEM + 1))
            for n, op in enumerate(self.ccs):
                op.sig = (es.enter_context(nc.semaphore("cc_%d" % n)), 1)
            for e in ENGS:
                c = 0
                for op in self.ops[e]:
                    if op.dma or op.cc:
                        continue
                    if op.needed:
                        c += 1
                        op.sig = (esem[e], c)
            block = es.enter_context(nc.Block())

            def run(engname):
                def body(eng):
                    waited = {}
                    for op in self.ops[engname]:
                        need = {}
                        for d in op.deps:
                            s, v = d.sig
                            if waited.get(s.num, 0) < v and need.get(s.num, (None, 0))[1] < v:
                                need[s.num] = (s, v)
                        for s, v in need.values():
                            eng.wait_ge(s, v)
                            waited[s.num] = v
                        if op.fn is None:
                            continue
                        ins = op.fn(eng)
                        if op.cc:
                            ins.then_inc(op.sig[0])
                        elif op.needed and ins is not None:
                            ins.then_inc(op.sig[0], 16 if op.dma else 1)
                return body

            block.tensor(run("pe"))
            block.scalar(run("act"))
            block.vector(run("dve"))
            block.gpsimd(run("pool"))
            block.sync(run("sp"))


class Ctx:
    def __init__(self, name="k"):
        self.nc = bass.Bass("TRN2", target_bir_lowering=False)
        self.es = contextlib.ExitStack()
        self.p = Prog(self.nc)
        self.n = 0
        self.ph = None
        self.banks = [self.es.enter_context(self.nc.psum_tensor("bank%d" % i, [128, 512], F32)) for i in range(8)]
        self.nps = 0

    def begin_phase(self):
        self.ph = contextlib.ExitStack()
        self.nps = 0

    def end_phase(self, wait_cc=True):
        self.p.barrier(wait_cc)
        self.ph.close()
        self.ph = None

    def gsb(self, shape, dt):
        self.n += 1
        return self.es.enter_context(self.nc.sbuf_tensor("gsb%d" % self.n, list(shape), dt))

    def dram(self, name, shape, dt, kind):
        return self.nc.dram_tensor(name, list(shape), dt, kind=kind).ap()

    def sb(self, shape, dt, name=None):
        self.n += 1
        st = self.ph if self.ph is not None else self.es
        return st.enter_context(self.nc.sbuf_tensor(name or "sb%d" % self.n, list(shape), dt))

    def ps(self, name=None):
        b = self.banks[self.nps]
        self.nps += 1
        return b

    def dma(self, out, in_, reads=(), writes=(), is_out=False):
        return self.p.add("sp", lambda e: e.dma_start(out=out, in_=in_), reads, writes, dma=True, out=is_out)

    def mm(self, out, lhsT, rhs, start, stop, reads=(), writes=()):
        return self.p.add("pe", lambda e: e.matmul(out, lhsT, rhs, start=start, stop=stop, skip_group_check=True),
                          reads, writes)

    def act(self, out, in_, func, reads=(), writes=(), scale=1.0, bias=0.0):
        return self.p.add("act", lambda e: e.activation(out, in_, func, bias=bias, scale=scale), reads, writes)

    def finish(self):
        self.p.emit()
        self.es.close()
        return self.nc


def run_spmd(nc, in_maps):
    if os.environ.get("KTRACE"):
        res = run_bass_kernel_spmd(nc, in_maps, core_ids=list(range(len(in_maps))), trace=True)
        print("EXEC_TIME_NS", res.exec_time_ns)
        return res.results
    res = run_bass_kernel_spmd(nc, in_maps, core_ids=list(range(len(in_maps))))
    return res.results


def const_mats():
    j = np.arange(128)[:, None]
    s = np.arange(128)[None, :]
    c = {}
    c["U"] = np.where(j >= s, -1.0, 0.0).astype(NPBF)
    c["Uc"] = np.where(j < s, -1.0, 0.0).astype(NPBF)
    c["Z"] = np.zeros((128, 128), NPBF)
    c["ONES"] = np.ones((128, 128), NPBF)
    c["MS"] = (s > j).astype(np.float32)
    c["MI"] = (s >= j).astype(np.float32)
    return c


def phase_sb(cx, io, C, NH, S, scale):
    nc, p = cx.nc, cx.p
    QT, KT, V = io["QT"], io["KT"], io["V"]
    NB = 2
    qt = [cx.sb([128, S], BF16) for _ in range(NB)]
    kt = [cx.sb([128, S], BF16) for _ in range(NB)]
    vv = [cx.sb([128, S // 128, 128], BF16) for _ in range(NB)]
    U, Uc, Z, MSb, ZR = C["U"], C["Uc"], C["Z"], C["MSb"], C["ZR"]
    NE = 3
    Eb = [cx.sb([128, 512], F32) for _ in range(NE)]
    Lb = [cx.sb([128, 512], BF16) for _ in range(NE)]
    Xb = [cx.sb([128, 512], F32) for _ in range(NE)]
    Ab = [cx.sb([128, 512], BF16) for _ in range(NE)]
    Ob = [cx.sb([128, 512], F32) for _ in range(2)]
    zb = [cx.ps() for _ in range(2)]
    runb = [cx.ps() for _ in range(2)]
    outb = [cx.ps() for _ in range(2)]
    dmy = cx.ps()
    ND = int(os.environ.get("SB_DUMMY", "3"))

    def warm(k):
        for _ in range(k):
            cx.mm(dmy[:, :], Z[:], ZR[:], True, True, reads=["Z", "ZR"])

    tiles = []
    nqg = S // 512
    for h in range(NH):
        for qg in range(nqg):
            t0 = qg * 512
            kb_hi = (t0 + 512) // 128 - 1
            for kb in range(kb_hi, -1, -1):
                off = max(0, kb * 128 - t0)
                tiles.append(dict(h=h, qg=qg, kb=kb, off=off, diag=(kb * 128 >= t0),
                                  first=(kb == kb_hi), last=(kb == 0), sid=h * nqg + qg))
    n = len(tiles)
    loaded = set()

    SL = S // 4
    KBS = SL // 128

    def load_slot(h, sl):
        if (h, sl) in loaded or h >= NH or sl >= 4:
            return
        loaded.add((h, sl))
        b = h % NB
        cs = slice(sl * SL, (sl + 1) * SL)
        cx.dma(qt[b][:, cs], QT(h, sl), reads=[("XO", h, sl)], writes=[("qt", b, sl)])
        cx.dma(kt[b][:, cs], KT(h, sl), reads=[("XO", h, sl)], writes=[("kt", b, sl)])
        cx.dma(vv[b][:, sl * KBS:(sl + 1) * KBS, :], V(h, sl), reads=[("XO", h, sl)], writes=[("vv", b, sl)])

    def prefetch(t):
        h, qg = t["h"], t["qg"]
        load_slot(h, qg * 512 // SL)
        nq = qg + 1
        if nq < nqg:
            load_slot(h, nq * 512 // SL)
        else:
            load_slot(h + 1, 0)

    def s1a(i):
        t = tiles[i]
        hb = t["h"] % NB
        if t["first"]:
            prefetch(t)
        c0, c1 = t["off"], 512
        q0 = t["qg"] * 512
        z = zb[i % 2]
        cx.mm(z[:, c0:c1], kt[hb][:, t["kb"] * 128:(t["kb"] + 1) * 128], qt[hb][:, q0 + c0:q0 + c1],
              True, True, reads=[("kt", hb, t["kb"] // KBS), ("qt", hb, q0 // SL)], writes=[("z", i % 2)])

    def s1b(i):
        t = tiles[i]
        c0, c1 = t["off"], 512
        cx.act(Eb[i % NE][:, c0:c1], zb[i % 2][:, c0:c1], AF.Exp, scale=scale,
               reads=[("z", i % 2)], writes=[("E", i % NE)])

    def s1c(i):
        t = tiles[i]
        c0, c1 = t["off"], 512
        cx.act(Lb[i % NE][:, c0:c1], Eb[i % NE][:, c0:c1], AF.Ln, bias=1.0,
               reads=[("E", i % NE)], writes=[("L", i % NE)])
        if t["diag"]:
            L = Lb[i % NE]
            p.add("dve", lambda e: e.tensor_tensor(L[:, c0:c0 + 128], L[:, c0:c0 + 128], MSb[:], ALU.mult),
                  reads=[("L", i % NE), "MSb"], writes=[("L", i % NE)])

    def s2(i):
        t = tiles[i]
        c0, c1 = t["off"], 512
        sp = t["sid"] % 2
        hb = t["h"] % NB
        rb, ob = runb[sp], outb[sp]
        L = Lb[i % NE]
        if t["first"]:
            cx.mm(rb[:, :], Z[:], ZR[:], True, False, reads=["Z", "ZR"], writes=[("run", sp)])
            cx.mm(ob[:, :], Z[:], ZR[:], True, False, reads=["Z", "ZR"], writes=[("out", sp)])
        cx.mm(rb[:, c0:c1], U[:], L[:, c0:c1], False, False, reads=["U", ("L", i % NE)], writes=[("run", sp)])
        return t, c0, c1, sp, hb, rb, ob, L

    def s2x(i):
        t = tiles[i]
        c0, c1 = t["off"], 512
        sp = t["sid"] % 2
        cx.act(Xb[i % NE][:, c0:c1], runb[sp][:, c0:c1], AF.Exp,
               reads=[("run", sp)], writes=[("X", i % NE)])

    def s2uc(i):
        t = tiles[i]
        c0, c1 = t["off"], 512
        sp = t["sid"] % 2
        rb = runb[sp]
        L, E, X, A = Lb[i % NE], Eb[i % NE], Xb[i % NE], Ab[i % NE]
        if not t["last"]:
            cx.mm(rb[:, c0:c1], Uc[:], L[:, c0:c1], False, False, reads=["Uc", ("L", i % NE)], writes=[("run", sp)])
        p.add("dve", lambda e: e.tensor_tensor(A[:, c0:c1], E[:, c0:c1], X[:, c0:c1], ALU.mult),
              reads=[("E", i % NE), ("X", i % NE)], writes=[("A", i % NE)])
        if t["diag"]:
            p.add("dve", lambda e: e.tensor_tensor(A[:, c0:c0 + 128], A[:, c0:c0 + 128], MSb[:], ALU.mult),
                  reads=[("A", i % NE), "MSb"], writes=[("A", i % NE)])

    def s2av(i):
        t = tiles[i]
        c0, c1 = t["off"], 512
        sp = t["sid"] % 2
        hb = t["h"] % NB
        ob = outb[sp]
        A = Ab[i % NE]
        cx.mm(ob[:, c0:c1], vv[hb][:, t["kb"], :], A[:, c0:c1], False, t["last"],
              reads=[("vv", hb, t["kb"] // KBS), ("A", i % NE)], writes=[("out", sp)])
        if t["last"]:
            O = Ob[sp]
            p.add("dve", lambda e: e.tensor_copy(O[:], ob[:]), reads=[("out", sp)], writes=[("O", sp)])
            io["out"](t["h"], t["qg"], O, ("O", sp))

    s1a(0)
    if n > 1:
        s1a(1)
    s1b(0)
    s1c(0)
    for i in range(n):
        s2(i)
        if i + 2 < n:
            s1a(i + 2)
        warm(ND)
        if i + 1 < n:
            s1b(i + 1)
        s2x(i)
        if i + 1 < n:
            s1c(i + 1)
        s2uc(i)
        if i >= 1:
            s2av(i - 1)
    s2av(n - 1)


def phase_mla(cx, io, C, NH, S, scale):
    nc, p = cx.nc, cx.p
    QnT, QrT, KnT, KrT, V = io["QnT"], io["QrT"], io["KnT"], io["KrT"], io["V"]
    NB = 2
    qn = [cx.sb([128, S], BF16) for _ in range(NB)]
    qr = [cx.sb([64, S], BF16) for _ in range(NB)]
    kn = [cx.sb([128, S], BF16) for _ in range(NB)]
    kr = cx.sb([64, S], BF16)
    vv = [cx.sb([128, S // 128, 128], BF16) for _ in range(NB)]
    ON, Z, MIb, ZR = C["ON"], C["Z"], C["MIb"], C["ZR"]
    NE = 3
    Pb = [cx.sb([128, 512], BF16) for _ in range(NE)]
    Rb = [cx.sb([128, 512], F32) for _ in range(2)]
    Ob = [cx.sb([128, 512], F32) for _ in range(2)]
    zb = [cx.ps() for _ in range(2)]
    denb = [cx.ps() for _ in range(2)]
    outb = [cx.ps() for _ in range(2)]
    dmy = cx.ps()
    ND = int(os.environ.get("MLA_DUMMY", "0"))

    def warm(k):
        for _ in range(k):
            cx.mm(dmy[:, :], Z[:], ZR[:], True, True, reads=["Z", "ZR"])
    SL = S // 4
    KBS = SL // 128

    tiles = []
    nqg = S // 512
    for h in range(NH):
        for qg in range(nqg):
            t0 = qg * 512
            kb_hi = (t0 + 512) // 128 - 1
            for kb in range(kb_hi, -1, -1):
                off = max(0, kb * 128 - t0)
                tiles.append(dict(h=h, qg=qg, kb=kb, off=off, diag=(kb * 128 >= t0),
                                  first=(kb == kb_hi), last=(kb == 0), sid=h * nqg + qg))
    n = len(tiles)
    loaded = set()

    def load_slot(h, sl):
        if (h, sl) in loaded or h >= NH or sl >= 4:
            return
        loaded.add((h, sl))
        b = h % NB
        cs = slice(sl * SL, (sl + 1) * SL)
        if h == 0:
            cx.dma(kr[:, cs], KrT(sl), reads=[("XO", 0, sl)], writes=[("kr", sl)])
        cx.dma(qn[b][:, cs], QnT(h, sl), reads=[("XO", h, sl)], writes=[("qn", b, sl)])
        cx.dma(qr[b][:, cs], QrT(h, sl), reads=[("XO", h, sl)], writes=[("qr", b, sl)])
        cx.dma(kn[b][:, cs], KnT(h, sl), reads=[("XO", h, sl)], writes=[("kn", b, sl)])
        cx.dma(vv[b][:, sl * KBS:(sl + 1) * KBS, :], V(h, sl), reads=[("XO", h, sl)], writes=[("vv", b, sl)])

    def prefetch(t):
        h, qg = t["h"], t["qg"]
        load_slot(h, qg * 512 // SL)
        nq = qg + 1
        if nq < nqg:
            load_slot(h, nq * 512 // SL)
        else:
            load_slot(h + 1, 0)

    def sA(i):
        t = tiles[i]
        hb = t["h"] % NB
        if t["first"]:
            prefetch(t)
        c0, c1 = t["off"], 512
        q0 = t["qg"] * 512
        k0 = t["kb"] * 128
        z = zb[i % 2]
        ks, qs = t["kb"] // KBS, q0 // SL
        cx.mm(z[:, c0:c1], kn[hb][:, k0:k0 + 128], qn[hb][:, q0 + c0:q0 + c1], True, False,
              reads=[("kn", hb, ks), ("qn", hb, qs)], writes=[("z", i % 2)])
        cx.mm(z[:, c0:c1], kr[:, k0:k0 + 128], qr[hb][:, q0 + c0:q0 + c1], False, True,
              reads=[("kr", ks), ("qr", hb, qs)], writes=[("z", i % 2)])

    def sB(i):
        t = tiles[i]
        c0, c1 = t["off"], 512
        P = Pb[i % NE]
        cx.act(P[:, c0:c1], zb[i % 2][:, c0:c1], AF.Exp, scale=scale,
               reads=[("z", i % 2)], writes=[("P", i % NE)])
        if t["diag"]:
            p.add("dve", lambda e: e.tensor_tensor(P[:, c0:c0 + 128], P[:, c0:c0 + 128], MIb[:], ALU.mult),
                  reads=[("P", i % NE), "MIb"], writes=[("P", i % NE)])

    def sC(i):
        t = tiles[i]
        c0, c1 = t["off"], 512
        sp = t["sid"] % 2
        hb = t["h"] % NB
        db, ob = denb[sp], outb[sp]
        P = Pb[i % NE]
        if t["first"]:
            cx.mm(db[:, :], Z[:], ZR[:], True, False, reads=["Z", "ZR"], writes=[("den", sp)])
            cx.mm(ob[:, :], Z[:], ZR[:], True, False, reads=["Z", "ZR"], writes=[("out", sp)])
        cx.mm(ob[:, c0:c1], vv[hb][:, t["kb"], :], P[:, c0:c1], False, t["last"],
              reads=[("vv", hb, t["kb"] // KBS), ("P", i % NE)], writes=[("out", sp)])
        cx.mm(db[:, c0:c1], ON[:], P[:, c0:c1], False, t["last"],
              reads=["ON", ("P", i % NE)], writes=[("den", sp)])
        if t["last"]:
            O, R = Ob[sp], Rb[sp]
            p.add("dve", lambda e: e.reciprocal(R[:], db[:]), reads=[("den", sp)], writes=[("R", sp)])
            p.add("dve", lambda e: e.tensor_tensor(O[:], ob[:], R[:], ALU.mult),
                  reads=[("out", sp), ("R", sp)], writes=[("O", sp)])
            io["out"](t["h"], t["qg"], O, ("O", sp))

    sA(0)
    if n > 1:
        sA(1)
    sB(0)
    for i in range(n):
        if i + 2 < n:
            sA(i + 2)
        warm(ND)
        if i + 1 < n:
            sB(i + 1)
        sC(i)


EPS = 1e-6
NWB = 3


def x3(X, r0, P, c0):
    return X[r0:r0 + P, :].rearrange("p (s c) -> p s c", s=4)[:, :, c0:c0 + 512]


class Tok:
    def __init__(self, cx, T, C):
        self.cx, self.p, self.T = cx, cx.p, T
        self.NTG = T // 512
        self.C = C
        self.ON = C["ON"]
        self.one = C["one"]
        self.mk = [cx.sb([128, 4, 512], BF16) for _ in range(3)]
        self.nmk = 0
        self.xops = []
        self.acc = [cx.ps() for _ in range(2)]
        self.ss = [cx.ps() for _ in range(4)]
        self.aux = [cx.ps() for _ in range(2)]
        deep = self.NTG <= 2
        self.hn = [(self.aux[1], ("aux", 1))]
        self.rp = [(self.aux[0], ("aux", 0))]
        if deep:
            self.hn.append((self.ss[2], ("ss", 2)))
            self.rp.append((self.ss[3], ("ss", 3)))
        self.nhn = self.nrp = 0
        self.NTMP, self.NSQ, self.NOB = (12, 4, 6) if deep else (4, 2, 3)
        self.delay = False
        self.rr_block = deep
        self.pend_ev = []
        self.wst = [cx.sb([128, 16, 128], F32) for _ in range(NWB)]
        self.wbf = [cx.sb([128, 16, 128], BF16) for _ in range(NWB)]
        self.tmp = [cx.sb([128, 512], F32) for _ in range(self.NTMP)]
        self.sq = [cx.sb([128, 512], BF16) for _ in range(self.NSQ)]
        self.ob = [cx.sb([128, 512], BF16) for _ in range(self.NOB)]
        self.nw = self.nt = self.nsq = self.nob = self.nacc = 0

    def xwrite(self, src_ap, skey, P, dst3):
        cx, p = self.cx, self.p
        sel = self.C["sel"]
        i = self.nmk % 3
        self.nmk += 1
        m = self.mk[i]
        for s_ in range(4):
            if s_ % 2 == 0:
                p.add("dve", lambda e, s_=s_: e.tensor_scalar(m[0:P, s_, :], src_ap, sel[0:P, s_:s_ + 1], None, ALU.mult),
                      reads=[skey, "sel"], writes=[("mk", i)])
            else:
                p.add("act", lambda e, s_=s_: e.mul(m[0:P, s_, :], src_ap, sel[0:P, s_:s_ + 1]),
                      reads=[skey, "sel"], writes=[("mk", i)])
        for s_ in range(4):
            self.xops.append(cx.dma(dst3(s_), m[0:P, s_, :], reads=[("mk", i)]))

    def t_tmp(self):
        i = self.nt % self.NTMP
        self.nt += 1
        return self.tmp[i], ("tmp", i)

    def t_sq(self):
        i = self.nsq % self.NSQ
        self.nsq += 1
        return self.sq[i], ("sq", i)

    def t_ob(self):
        i = self.nob % self.NOB
        self.nob += 1
        return self.ob[i], ("ob", i)

    def t_acc(self):
        i = self.nacc % len(self.acc)
        self.nacc += 1
        return self.acc[i], ("acc", i)

    def rstd_from(self, out_ap, ss_ap, C, rkeys, wkeys):
        cx = self.cx
        cx.act(out_ap, ss_ap, AF.Ln, scale=1.0 / C, bias=EPS, reads=rkeys, writes=wkeys)
        cx.act(out_ap, out_ap, AF.Exp, scale=-0.5, reads=wkeys, writes=wkeys)

    def load_x(self, src, nk, xb, xkey, rstd_b=None, rkey=None, C=None, stage=None):
        cx, p, T = self.cx, self.p, self.T
        self._wload(NWB)
        for k in range(nk):
            st = stage[k % 2]
            cx.dma(st[:], src[k * 128:(k + 1) * 128, :], writes=[("xst", k % 2)])
            p.add("act", lambda e, k=k, st=st: e.copy(xb[:, k, :], st[:]),
                  reads=[("xst", k % 2)], writes=[(xkey, k)])
            if rstd_b is not None:
                for tg in range(self.NTG):
                    sq, sk = self.t_sq()
                    cx.act(sq[:], st[:, tg * 512:(tg + 1) * 512], AF.Square, reads=[("xst", k % 2)], writes=[sk])
                    cx.mm(self.ss[tg][:], self.ON[:], sq[:], k == 0, k == nk - 1,
                          reads=["ON", sk], writes=[("ss", tg)])
        if rstd_b is not None:
            for tg in range(self.NTG):
                self.rstd_from(rstd_b[:, tg * 512:(tg + 1) * 512], self.ss[tg][:], C, [("ss", tg)], [(rkey, tg)])

    def rstd_tm(self, rstd_b, rkey, rtm, tkey):
        cx, p = self.cx, self.p
        ntb = self.T // 128
        a = self.aux[0]
        for tb in range(ntb):
            cx.mm(a[:, tb:tb + 1], rstd_b[0:1, tb * 128:(tb + 1) * 128], self.one[0:1, 0:1], True, True,
                  reads=[(rkey, tb // 4), "one"], writes=[("aux", 0)])
        p.add("dve", lambda e: e.tensor_copy(rtm[:, 0:ntb], a[:, 0:ntb]), reads=[("aux", 0)], writes=[tkey])

    def plan(self, specs):
        self.wplan = list(specs)
        self.wi = 0
        self.wl = 0

    def _wload(self, upto):
        cx = self.cx
        while self.wl < min(upto, len(self.wplan)):
            W, c0, nk, width = self.wplan[self.wl]
            i = self.wl % NWB
            cx.dma(self.wst[i][:, 0:nk, 0:width], W[0:nk * 128, c0:c0 + width].rearrange("(k p) f -> p k f", p=128),
                   writes=[("wst", i)])
            self.wl += 1

    def weights(self, W, f0, fo, nk, gain=None, width=128):
        cx, p = self.cx, self.p
        spec = self.wplan[self.wi]
        assert spec[0] is W and spec[1] == f0 + fo * 128 and spec[2] == nk and spec[3] == width, (spec[1:], f0, fo, nk, width)
        self._wload(self.wi + NWB)
        i = self.wi % NWB
        self.wi += 1
        wf, wb = self.wst[i], self.wbf[i]
        if gain is not None:
            gap, gkey = gain
            ceng = "dve" if self.NTG <= 2 else "pool"
            p.add(ceng, lambda e: e.tensor_tensor(wb[:, 0:nk, 0:width], wf[:, 0:nk, 0:width],
                                                   gap[:, 0:nk].unsqueeze(2).to_broadcast([128, nk, width]), ALU.mult),
                  reads=[("wst", i), gkey], writes=[("wbf", i)])
        else:
            p.add("pool", lambda e: e.tensor_copy(wb[:, 0:nk, 0:width], wf[:, 0:nk, 0:width]),
                  reads=[("wst", i)], writes=[("wbf", i)])
        return wb, ("wbf", i)

    @staticmethod
    def _run_rr(items):
        gens = []
        for f, a in items:
            r = f(*a)
            if r is not None and hasattr(r, "__next__"):
                gens.append(r)
        while gens:
            for g in list(gens):
                try:
                    next(g)
                except StopIteration:
                    gens.remove(g)

    def flush(self):
        ev, self.pend_ev = self.pend_ev, []
        self._run_rr(ev)

    def _evacs(self, items):
        if self.delay:
            prev, self.pend_ev = self.pend_ev, items
            self._run_rr(prev)
        else:
            self._run_rr(items)

    def proj_fm(self, wb, wkey, nk, xb, xkey, evac, width=128):
        cx = self.cx
        items = []
        for tg in range(self.NTG):
            acc, akey = self.t_acc()
            for k in range(nk):
                cx.mm(acc[0:width, :], wb[:, k, 0:width], xb[:, k, tg * 512:(tg + 1) * 512], k == 0, k == nk - 1,
                      reads=[wkey, (xkey, k)], writes=[akey])
            if self.rr_block:
                items.append((evac, (tg, acc, akey)))
            else:
                self._run_rr([(evac, (tg, acc, akey))])
        if self.rr_block:
            self._run_rr(items)

    def proj_tm(self, wb, wkey, nk, xb, xkey, evac, width=128):
        cx = self.cx
        ntb = self.T // 128
        items = []
        for g in range(ntb // 4):
            acc, akey = self.t_acc()
            for j in range(4):
                tb = g * 4 + j
                for k in range(nk):
                    cx.mm(acc[:, j * 128:j * 128 + width], xb[:, k, tb * 128:(tb + 1) * 128], wb[:, k, 0:width],
                          (j == 0 and k == 0), k == nk - 1, reads=[wkey, (xkey, k)], writes=[akey])
            if self.rr_block:
                items.append((evac, (g, acc, akey)))
            else:
                self._run_rr([(evac, (g, acc, akey))])
        if self.rr_block:
            self._run_rr(items)

    def head_norm_g(self, y, ykey, D, gcol, gkey, out_ap, okey):
        cx, p = self.cx, self.p
        sq, sk = self.t_sq()
        cx.act(sq[0:D, :], y[0:D, :], AF.Square, reads=[ykey], writes=[sk])
        a, ak = self.hn[self.nhn % len(self.hn)]
        self.nhn += 1
        cx.mm(a[0:D, :], self.ON[0:D, 0:D], sq[0:D, :], True, True, reads=["ON", sk], writes=[ak])
        yield
        r, rk = self.t_tmp()
        self.rstd_from(r[0:D, :], a[0:D, :], D, [ak], [rk])
        yield
        p.add("dve", lambda e: e.scalar_tensor_tensor(out_ap, y[0:D, :], gcol, r[0:D, :], ALU.mult, ALU.mult),
              reads=[ykey, rk, gkey], writes=[okey])

    def head_norm(self, *a):
        for _ in self.head_norm_g(*a):
            pass

    def silu_gate(self, acc, akey, rstd_ap, rkey, out_ap, okey):
        cx, p = self.cx, self.p
        g, gk = self.t_tmp()
        p.add("dve", lambda e: e.tensor_tensor(g[:], acc[:], rstd_ap, ALU.mult), reads=[akey, rkey], writes=[gk])
        cx.act(out_ap, g[:], AF.Silu, reads=[gk], writes=[okey])


def load_small(cx, dst, src, key):
    cx.dma(dst, src, writes=[key])


def mem_attention(tk, memT, gmem, gmem_key, Wkv, gq, gk, gk_key, qmn, qkey, sgm, sgkey, MmT_out, bufs):
    cx, p, T = tk.cx, tk.p, tk.T
    memb, rsm_b, rsm_tm, mkn, mvb, mst = bufs
    ML = 256
    for k in range(16):
        st = mst[k % 2]
        cx.dma(st[:], memT[k * 128:(k + 1) * 128, :], writes=[("mst", k % 2)])
        p.add("act", lambda e, k=k, st=st: e.copy(memb[:, k, :], st[:]), reads=[("mst", k % 2)], writes=[("memb", k)])
        sq, sk = tk.t_sq()
        cx.act(sq[:, 0:ML], st[:], AF.Square, reads=[("mst", k % 2)], writes=[sk])
        cx.mm(tk.ss[0][:, 0:ML], tk.ON[:], sq[:, 0:ML], k == 0, k == 15, reads=["ON", sk], writes=[("ss", 0)])
    tk.rstd_from(rsm_b[:], tk.ss[0][:, 0:ML], 2048, [("ss", 0)], ["rsm_b"])
    a = tk.aux[0]
    for mb in range(2):
        cx.mm(a[:, mb:mb + 1], rsm_b[0:1, mb * 128:(mb + 1) * 128], tk.one[0:1, 0:1], True, True,
              reads=["rsm_b", "one"], writes=[("aux", 0)])
    p.add("dve", lambda e: e.tensor_copy(rsm_tm[:], a[:, 0:2]), reads=[("aux", 0)], writes=["rsm_tm"])
    for h in range(4):
        wb, wkey = tk.weights(Wkv, 0, h, 16, gain=(gmem, gmem_key))
        acc, akey = tk.t_acc()
        for k in range(16):
            cx.mm(acc[:, 0:ML], wb[:, k, :], memb[:, k, :], k == 0, k == 15, reads=[wkey, ("memb", k)], writes=[akey])
        y, yk = tk.t_tmp()
        p.add("dve", lambda e, y=y, acc=acc: e.tensor_tensor(y[:, 0:ML], acc[:, 0:ML], rsm_b[:], ALU.mult),
              reads=[akey, "rsm_b"], writes=[yk])
        sq, sk = tk.t_sq()
        cx.act(sq[:, 0:ML], y[:, 0:ML], AF.Square, reads=[yk], writes=[sk])
        a1 = tk.aux[1]
        cx.mm(a1[:, 0:ML], tk.ON[:], sq[:, 0:ML], True, True, reads=["ON", sk], writes=[("aux", 1)])
        r, rk = tk.t_tmp()
        tk.rstd_from(r[:, 0:ML], a1[:, 0:ML], 128, [("aux", 1)], [rk])
        p.add("dve", lambda e, y=y, r=r, h=h: e.scalar_tensor_tensor(mkn[:, h, :], y[:, 0:ML], gk, r[:, 0:ML], ALU.mult, ALU.mult),
              reads=[yk, rk, gk_key], writes=[("mkn", h)])
    for h in range(4):
        wb, wkey = tk.weights(Wkv, 512, h, 16, gain=(gmem, gmem_key))
        acc, akey = tk.t_acc()
        for mb in range(2):
            for k in range(16):
                cx.mm(acc[:, mb * 128:(mb + 1) * 128], memb[:, k, mb * 128:(mb + 1) * 128], wb[:, k, :],
                      (mb == 0 and k == 0), k == 15, reads=[wkey, ("memb", k)], writes=[akey])
        p.add("dve", lambda e, acc=acc, h=h: e.tensor_tensor(
            mvb[:, :, h * 128:(h + 1) * 128], acc[:, 0:256].rearrange("p (m f) -> p m f", m=2),
            rsm_tm[:].unsqueeze(2).to_broadcast([128, 2, 128]), ALU.mult), reads=[akey, "rsm_tm"], writes=[("mvb", h)])
    sc = 128 ** -0.5
    for h in range(4):
        for tg in range(tk.NTG):
            cols = slice(tg * 512, (tg + 1) * 512)
            Ps = []
            for mb in range(2):
                acc, akey = tk.t_acc()
                cx.mm(acc[:], mkn[:, h, mb * 128:(mb + 1) * 128], qmn[:, h, cols], True, True,
                      reads=[("mkn", h), (qkey, h)], writes=[akey])
                P, pk = tk.t_ob()
                cx.act(P[:], acc[:], AF.Exp, scale=sc, reads=[akey], writes=[pk])
                Ps.append((P, pk))
            mo, den = tk.aux[0], tk.aux[1]
            for mb in range(2):
                P, pk = Ps[mb]
                cx.mm(mo[:], mvb[:, mb, h * 128:(h + 1) * 128], P[:], mb == 0, mb == 1, reads=[("mvb", h), pk], writes=[("aux", 0)])
            for mb in range(2):
                P, pk = Ps[mb]
                cx.mm(den[:], tk.ON[:], P[:], mb == 0, mb == 1, reads=["ON", pk], writes=[("aux", 1)])
            r, rk = tk.t_tmp()
            p.add("dve", lambda e, r=r, den=den: e.reciprocal(r[:], den[:]), reads=[("aux", 1)], writes=[rk])
            p.add("dve", lambda e, r=r, mo=mo: e.tensor_tensor(r[:], mo[:], r[:], ALU.mult), reads=[("aux", 0), rk], writes=[rk])
            o, ok = tk.t_ob()
            p.add("dve", lambda e, r=r, o=o, h=h, cols=cols: e.tensor_tensor(o[:], r[:], sgm[:, h, cols], ALU.mult),
                  reads=[rk, (sgkey, h)], writes=[ok])
            cx.dma(MmT_out[h * 128:(h + 1) * 128, cols], o[:], reads=[ok])


def phase_la1(cx, tk, io, C, T):
    p = cx.p
    D = 2048
    xT, W, memT, Wkv = io["xT"], io["a_w_in"], io["memT"], io["w_mem_kv0"]
    XI, GT, MmT = io["XI1"], io["GT"], io["MmT"]
    NR = 1152
    xb = cx.sb([128, 16, T], BF16)
    rstd_b = cx.sb([128, T], F32)
    rtm = cx.sb([128, 16], F32)
    stage = [cx.sb([128, T], F32) for _ in range(2)]
    gin, gmem, gq, gk = C["a_norm"], C["mem_norm0"], C["g_mem_q0"], C["g_mem_k0"]
    qmn = cx.sb([128, 4, T], BF16)
    sgm = cx.sb([128, 4, T], BF16)
    memb = cx.sb([128, 16, 256], BF16)
    rsm_b = cx.sb([128, 256], F32)
    rsm_tm = cx.sb([128, 2], F32)
    mkn = cx.sb([128, 4, 256], BF16)
    mvb = cx.sb([128, 2, 512], BF16)
    mst = [cx.sb([128, 256], F32) for _ in range(2)]
    vst = [cx.sb([128, 4, 128], BF16) for _ in range(2)]

    HO = [0, 3, 6, 9, 1, 4, 7, 10, 2, 5, 8, 11]
    order = []
    for r in range(3):
        hs = HO[4 * r:4 * r + 4]
        order += [hh for hh in hs] + [12 + hh for hh in hs] + [24 + hh for hh in hs]
    order += list(range(36, 56))
    tk.plan([(W, fo * 128, 16, 128) for fo in order] + [(Wkv, h * 128, 16, 128) for h in range(4)] + [(Wkv, 512 + h * 128, 16, 128) for h in range(4)])
    tk.load_x(xT, 16, xb, "xb", rstd_b, "rstd", D, stage)
    tk.rstd_tm(rstd_b, "rstd", rtm, "rtm")

    def rs(tg):
        return rstd_b[:, tg * 512:(tg + 1) * 512], ("rstd", tg)

    for n_, fo in enumerate(order):
        if n_ in (12, 24, 36) and io.get("cc"):
            io["cc"](n_ // 12 - 1, list(tk.xops))
        wb, wkey = tk.weights(W, 0, fo, 16, gain=(gin, "a_norm"))
        if fo < 24:
            def evac(tg, acc, akey, fo=fo):
                o, ok = tk.t_ob()
                ra, rk = rs(tg)
                p.add("dve", lambda e: e.tensor_tensor(o[:], acc[:], ra, ALU.mult), reads=[akey, rk], writes=[ok])
                hh = fo % 12
                r0 = (hh // 3) * 384 + (128 if fo >= 12 else 0)
                X_ = XI[hh % 3]
                tk.xwrite(o[:], ok, 128, lambda s_: X_[s_][r0:r0 + 128, tg * 512:(tg + 1) * 512])
            tk.proj_fm(wb, wkey, 16, xb, "xb", evac)
        elif fo < 36:
            def evac(g, acc, akey, fo=fo):
                i = g % 2
                v = vst[i]
                p.add("dve", lambda e: e.tensor_tensor(
                    v[:], acc[:].rearrange("p (j f) -> p j f", j=4),
                    rtm[:, g * 4:(g + 1) * 4].unsqueeze(2).to_broadcast([128, 4, 128]), ALU.mult),
                    reads=[akey, "rtm"], writes=[("vst", i)])
                hh = fo - 24
                r0 = (hh // 3) * 384 + 256
                X_ = XI[hh % 3]
                tk.xwrite(v[:].rearrange("p j f -> p (j f)"), ("vst", i), 128, lambda s_: X_[s_][r0:r0 + 128, g * 512:(g + 1) * 512])
            tk.proj_tm(wb, wkey, 16, xb, "xb", evac)
        elif fo < 48:
            def evac(tg, acc, akey, fo=fo):
                o, ok = tk.t_ob()
                ra, rk = rs(tg)
                tk.silu_gate(acc, akey, ra, rk, o[:], ok)
                cx.dma(GT[(fo - 36) * 128:(fo - 35) * 128, tg * 512:(tg + 1) * 512], o[:], reads=[ok])
            tk.proj_fm(wb, wkey, 16, xb, "xb", evac)
        elif fo < 52:
            def evac(tg, acc, akey, fo=fo):
                y, yk = tk.t_tmp()
                ra, rk = rs(tg)
                p.add("dve", lambda e: e.tensor_tensor(y[:], acc[:], ra, ALU.mult), reads=[akey, rk], writes=[yk])
                tk.head_norm(y, yk, 128, gq[:, 0:1], "g_mem_q0", qmn[:, fo - 48, tg * 512:(tg + 1) * 512], ("qmn", fo - 48))
            tk.proj_fm(wb, wkey, 16, xb, "xb", evac)
        else:
            def evac(tg, acc, akey, fo=fo):
                ra, rk = rs(tg)
                tk.silu_gate(acc, akey, ra, rk, sgm[:, fo - 52, tg * 512:(tg + 1) * 512], ("sgm", fo - 52))
            tk.proj_fm(wb, wkey, 16, xb, "xb", evac)

    tk.flush()
    mem_attention(tk, memT, gmem, "mem_norm0", Wkv, gq, gk[:, 0:1], "g_mem_k0", qmn, "qmn", sgm, "sgm", MmT,
                  (memb, rsm_b, rsm_tm, mkn, mvb, mst))


def phase_out(cx, tk, io, T):
    p = cx.p
    OT, GT, MmT, XR, W, Y = io["OT"], io["GT"], io["MmT"], io["XR"], io["W"], io["Y"]
    mixb = cx.sb([128, 16, T], BF16)
    HW_ = min(T, 1024)
    ost = [cx.sb([128, HW_], BF16) for _ in range(2)]
    gst = [cx.sb([128, HW_], BF16) for _ in range(2)]
    xrb = [cx.sb([128, 512], F32) for _ in range(8)]
    n = 0
    for k in range(12):
        for hh in range(T // HW_):
            i = n % 2
            n += 1
            cs = slice(hh * HW_, (hh + 1) * HW_)
            cx.dma(ost[i][:], OT[k % 3][(k // 3) * 128:(k // 3 + 1) * 128, cs], reads=[("XO", k % 3)], writes=[("ost", i)])
            cx.dma(gst[i][:], GT[k * 128:(k + 1) * 128, cs], writes=[("gst", i)])
            eng = "dve"
            p.add(eng, lambda e, i=i, k=k, cs=cs: e.tensor_tensor(mixb[:, k, cs], ost[i][:], gst[i][:], ALU.mult),
                  reads=[("ost", i), ("gst", i)], writes=[("mixb", k)])
    for k in range(12, 16):
        cx.dma(mixb[:, k, :], MmT[(k - 12) * 128:(k - 11) * 128, :], writes=[("mixb", k)])
    m = 0
    tk.plan([(W, fo * 128, 16, 128) for fo in range(16)])
    for fo in range(16):
        wb, wkey = tk.weights(W, 0, fo, 16)
        for tg_ in range(T // 512):
            i_ = (fo * (T // 512) + tg_) % 8
            cx.dma(xrb[i_][:], XR[fo * 128:(fo + 1) * 128, tg_ * 512:(tg_ + 1) * 512], writes=[("xrb", i_)])

        def evac(tg, acc, akey, fo=fo):
            i = (fo * (T // 512) + tg) % 8
            cs = slice(tg * 512, (tg + 1) * 512)
            y, yk = tk.t_tmp()
            p.add("dve", lambda e: e.tensor_tensor(y[:], acc[:], xrb[i][:], ALU.add), reads=[akey, ("xrb", i)], writes=[yk])
            cx.dma(Y[fo * 128:(fo + 1) * 128, cs], y[:], reads=[yk], is_out=io.get("final", False))
        tk.proj_fm(wb, wkey, 16, mixb, "mixb", evac)


def rope_tables(cx, tk, pos, invf, cosF, sinS, T):
    p = cx.p
    pi_ = cx.sb([64, 512], I32)
    ki = cx.sb([64, 512], I32)
    TWO_PI = float(2 * np.pi)
    for c in range(T // 512):
        cs = slice(c * 512, (c + 1) * 512)
        cx.dma(pi_[:], pos[cs].partition_broadcast(64), writes=["pi_"])
        ang, ak = tk.t_tmp()
        p.add("dve", lambda e, ang=ang: e.tensor_copy(ang[0:64, :], pi_[:]), reads=["pi_"], writes=[ak])
        p.add("dve", lambda e, ang=ang: e.tensor_scalar(ang[0:64, :], ang[0:64, :], invf[:, 0:1], None, ALU.mult),
              reads=[ak, "invf"], writes=[ak])
        for dst, dkey, ph in ((sinS, "sinS", 0.5), (cosF, "cosF", 0.75)):
            u, uk = tk.t_tmp()
            kf, kk = tk.t_tmp()
            p.add("dve", lambda e, u=u, ang=ang, ph=ph: e.tensor_scalar(u[0:64, :], ang[0:64, :], 1.0 / TWO_PI, ph, ALU.mult, ALU.add),
                  reads=[ak], writes=[uk])
            p.add("dve", lambda e, u=u: e.tensor_copy(ki[:], u[0:64, :]), reads=[uk], writes=["ki"])
            p.add("dve", lambda e, kf=kf: e.tensor_copy(kf[0:64, :], ki[:]), reads=["ki"], writes=[kk])
            p.add("dve", lambda e, u=u, kf=kf: e.tensor_tensor(u[0:64, :], u[0:64, :], kf[0:64, :], ALU.subtract),
                  reads=[uk, kk], writes=[uk])
            p.add("dve", lambda e, u=u, kf=kf: e.tensor_scalar(kf[0:64, :], u[0:64, :], 0.0, None, ALU.is_lt),
                  reads=[uk], writes=[kk])
            p.add("dve", lambda e, u=u, kf=kf: e.tensor_tensor(u[0:64, :], u[0:64, :], kf[0:64, :], ALU.add),
                  reads=[uk, kk], writes=[uk])
            p.add("dve", lambda e, u=u: e.tensor_scalar(u[0:64, :], u[0:64, :], TWO_PI, float(-np.pi), ALU.mult, ALU.add),
                  reads=[uk], writes=[uk])
            p.add("dve", lambda e, u=u: e.tensor_scalar(u[0:64, :], u[0:64, :], float(np.pi), float(-np.pi), ALU.min, ALU.max),
                  reads=[uk], writes=[uk])
            cx.act(dst[:, cs], u[0:64, :], AF.Sin, reads=[uk], writes=[(dkey, c)])
        p.add("dve", lambda e, cs=cs: e.tensor_scalar(sinS[0:32, cs], sinS[0:32, cs], -1.0, None, ALU.mult),
              reads=[("sinS", c)], writes=[("sinS", c)])


def phase_lb1(cx, tk, io, T, C, t_off):
    p = cx.p
    D = 2048
    XI = io["XI3"]
    RB = (512, 448, 448)
    TL_ = 2048

    def col(s_, tg):
        c0 = s_ * TL_ + t_off + tg * 512
        return slice(c0, c0 + 512)
    xb = cx.sb([128, 16, T], BF16)
    rstd1 = cx.sb([128, T], F32)
    rstd2 = cx.sb([128, T], F32)
    rtm2 = cx.sb([128, 16], F32)
    lat = cx.sb([128, 4, T], BF16)
    qmn = cx.sb([128, 4, T], BF16)
    sgm = cx.sb([128, 4, T], BF16)
    cosF = cx.sb([64, T], F32)
    sinS = cx.sb([64, T], F32)
    stage = [cx.sb([128, T], F32) for _ in range(2)]
    vst = [cx.sb([128, 4, 128], BF16) for _ in range(2)]
    memb = cx.sb([128, 16, 256], BF16)
    rsm_b = cx.sb([128, 256], F32)
    rsm_tm = cx.sb([128, 2], F32)
    mkn = cx.sb([128, 4, 256], BF16)
    mvb = cx.sb([128, 2, 512], BF16)
    mst = [cx.sb([128, 256], F32) for _ in range(2)]
    NTG = T // 512

    HO = [0, 3, 6, 9, 1, 4, 7, 10, 2, 5, 8, 11]
    pl = [(io["b_w_in"], fo * 128, 16, 128) for fo in range(4)]
    for h in HO:
        pl += [(io["b_w_uq"], h * 192, 4, 128), (io["b_w_uq"], h * 192 + 128, 4, 64)]
    pl += [(io["w_dkv"], j * 128, 16, 128) for j in range(4)] + [(io["w_dkv"], 512, 16, 64)]
    for h in HO:
        pl += [(io["w_ukv"], h * 256, 4, 128), (io["w_ukv"], h * 256 + 128, 4, 128)]
    pl += [(io["b_w_in"], fo * 128, 16, 128) for fo in range(4, 24)]
    pl += [(io["w_mem_kv"], h * 128, 16, 128) for h in range(4)] + [(io["w_mem_kv"], 512 + h * 128, 16, 128) for h in range(4)]
    tk.plan(pl)
    rope_tables(cx, tk, io["pos"], C["invf"], cosF, sinS, T)
    tk.load_x(io["X1T"], 16, xb, "xb", rstd1, "rstd1", D, stage)

    def rs(buf, key, tg):
        return buf[:, tg * 512:(tg + 1) * 512], (key, tg)

    def rope_apply(y, yk, tg, dsts):
        cs = slice(tg * 512, (tg + 1) * 512)
        sw, swk = tk.rp[tk.nrp % len(tk.rp)]
        tk.nrp += 1
        cx.mm(sw[0:64, :], C["PERM"][:, :], y[0:64, :], True, True, reads=["PERM", yk], writes=[swk])
        t1, k1 = tk.t_tmp()
        t2, k2 = tk.t_tmp()
        p.add("dve", lambda e: e.tensor_tensor(t1[0:64, :], y[0:64, :], cosF[:, cs], ALU.mult), reads=[yk, ("cosF", tg)], writes=[k1])
        yield
        p.add("dve", lambda e: e.tensor_tensor(t2[0:64, :], sw[0:64, :], sinS[:, cs], ALU.mult), reads=[swk, ("sinS", tg)], writes=[k2])
        o, ok = tk.t_ob()
        p.add("dve", lambda e: e.tensor_tensor(o[0:64, :], t1[0:64, :], t2[0:64, :], ALU.add), reads=[k1, k2], writes=[ok])
        yield
        for dst in dsts:
            tk.xwrite(o[0:64, :], ok, 64, dst)

    def lat_evac(rbuf, rkey, j):
        def evac(tg, acc, akey):
            y, yk = tk.t_tmp()
            ra, rk = rs(rbuf, rkey, tg)
            p.add("dve", lambda e: e.tensor_tensor(y[:], acc[:], ra, ALU.mult), reads=[akey, rk], writes=[yk])
            cx.act(lat[:, j, tg * 512:(tg + 1) * 512], y[:], AF.Copy, reads=[yk], writes=[("lat", j)])
            sq, sk = tk.t_sq()
            cx.act(sq[:], y[:], AF.Square, reads=[yk], writes=[sk])
            cx.mm(tk.ss[tg][:], tk.ON[:], sq[:], j == 0, j == 3, reads=["ON", sk], writes=[("ss", tg)])
        return evac

    def fin_lat_norm():
        tk.flush()
        for tg in range(NTG):
            tk.rstd_from(rstd2[:, tg * 512:(tg + 1) * 512], tk.ss[tg][:], 512, [("ss", tg)], [("rstd2", tg)])

    gB = (C["g_b"], "g_b")
    for j in range(4):
        wb, wkey = tk.weights(io["b_w_in"], 0, j, 16, gain=gB)
        tk.proj_fm(wb, wkey, 16, xb, "xb", lat_evac(rstd1, "rstd1", j))
    fin_lat_norm()
    gQ = (C["g_q_lat"], "g_q_lat")
    for h in HO:
        wb, wkey = tk.weights(io["b_w_uq"], h * 192, 0, 4, gain=gQ)

        def evac_qn(tg, acc, akey, h=h):
            y, yk = tk.t_tmp()
            ra, rk = rs(rstd2, "rstd2", tg)
            p.add("dve", lambda e: e.tensor_tensor(y[:], acc[:], ra, ALU.mult), reads=[akey, rk], writes=[yk])
            o, ok = tk.t_ob()
            yield
            yield from tk.head_norm_g(y, yk, 128, C["g_q_nope"][:, 0:1], "g_q_nope", o[:], ok)
            yield
            r0 = (h // 3) * RB[h % 3]
            tk.xwrite(o[:], ok, 128, lambda s_: XI[h % 3][s_][r0:r0 + 128, t_off + tg * 512:t_off + (tg + 1) * 512])
        tk.proj_fm(wb, wkey, 4, lat, "lat", evac_qn)
        wb, wkey = tk.weights(io["b_w_uq"], h * 192 + 128, 0, 4, gain=gQ, width=64)

        def evac_qr(tg, acc, akey, h=h):
            y, yk = tk.t_tmp()
            ra, rk = rs(rstd2, "rstd2", tg)
            p.add("dve", lambda e: e.tensor_tensor(y[0:64, :], acc[0:64, :], ra[0:64, :], ALU.mult), reads=[akey, rk], writes=[yk])
            yn, ynk = tk.t_tmp()
            yield
            yield from tk.head_norm_g(y, yk, 64, C["g_q_rope"][0:64, 0:1], "g_q_rope", yn[0:64, :], ynk)
            yield
            r0 = (h // 3) * RB[h % 3] + 384
            yield from rope_apply(yn, ynk, tg, [lambda s_: XI[h % 3][s_][r0:r0 + 64, t_off + tg * 512:t_off + (tg + 1) * 512]])
        tk.proj_fm(wb, wkey, 4, lat, "lat", evac_qr, width=64)

    for j in range(4):
        wb, wkey = tk.weights(io["w_dkv"], 0, j, 16, gain=(C["g_kv"], "g_kv"))
        tk.proj_fm(wb, wkey, 16, xb, "xb", lat_evac(rstd1, "rstd1", j))
    wb, wkey = tk.weights(io["w_dkv"], 512, 0, 16, gain=(C["g_kv"], "g_kv"), width=64)

    def evac_kr(tg, acc, akey):
        y, yk = tk.t_tmp()
        ra, rk = rs(rstd1, "rstd1", tg)
        p.add("dve", lambda e: e.tensor_tensor(y[0:64, :], acc[0:64, :], ra[0:64, :], ALU.mult), reads=[akey, rk], writes=[yk])
        yn, ynk = tk.t_tmp()
        yield
        yield from tk.head_norm_g(y, yk, 64, C["g_k_rope"][0:64, 0:1], "g_k_rope", yn[0:64, :], ynk)
        yield
        yield from rope_apply(yn, ynk, tg, [lambda s_, d=d: XI[0][s_][d * 512 + 448:d * 512 + 512, t_off + tg * 512:t_off + (tg + 1) * 512] for d in range(4)])
    tk.proj_fm(wb, wkey, 16, xb, "xb", evac_kr, width=64)
    fin_lat_norm()
    tk.rstd_tm(rstd2, "rstd2", rtm2, "rtm2")

    for n_, h in enumerate(HO):
        if n_ in (4, 8) and io.get("cc"):
            io["cc"](n_ // 4 - 1, list(tk.xops))
        wb, wkey = tk.weights(io["w_ukv"], h * 256, 0, 4, gain=(C["g_ckv"], "g_ckv"))

        def evac_kn(tg, acc, akey, h=h):
            y, yk = tk.t_tmp()
            ra, rk = rs(rstd2, "rstd2", tg)
            p.add("dve", lambda e: e.tensor_tensor(y[:], acc[:], ra, ALU.mult), reads=[akey, rk], writes=[yk])
            o, ok = tk.t_ob()
            yield
            yield from tk.head_norm_g(y, yk, 128, C["g_k_nope"][:, 0:1], "g_k_nope", o[:], ok)
            yield
            r0 = (h // 3) * RB[h % 3] + 128
            tk.xwrite(o[:], ok, 128, lambda s_: XI[h % 3][s_][r0:r0 + 128, t_off + tg * 512:t_off + (tg + 1) * 512])
        tk.proj_fm(wb, wkey, 4, lat, "lat", evac_kn)
        wb, wkey = tk.weights(io["w_ukv"], h * 256 + 128, 0, 4, gain=(C["g_ckv"], "g_ckv"))

        def evac_v(g, acc, akey, h=h):
            i = g % 2
            v = vst[i]
            p.add("dve", lambda e: e.tensor_tensor(
                v[:], acc[:].rearrange("p (j f) -> p j f", j=4),
                rtm2[:, g * 4:(g + 1) * 4].unsqueeze(2).to_broadcast([128, 4, 128]), ALU.mult),
                reads=[akey, "rtm2"], writes=[("vst", i)])
            r0 = (h // 3) * RB[h % 3] + 256
            kb0 = t_off // 128 + g * 4
            tk.xwrite(v[:].rearrange("p j f -> p (j f)"), ("vst", i), 128, lambda s_: XI[h % 3][s_][r0:r0 + 128, kb0 * 128:(kb0 + 4) * 128])
        tk.proj_tm(wb, wkey, 4, lat, "lat", evac_v)

    cc_snapshot = list(tk.xops)

    gB = (C["g_b"], "g_b")
    for fo in range(4, 24):
        if fo == 10 and io.get("cc_early"):
            io["cc_early"](cc_snapshot)
        wb, wkey = tk.weights(io["b_w_in"], 0, fo, 16, gain=gB)
        if fo < 16:
            def evac(tg, acc, akey, fo=fo):
                o, ok = tk.t_ob()
                ra, rk = rs(rstd1, "rstd1", tg)
                tk.silu_gate(acc, akey, ra, rk, o[:], ok)
                cx.dma(io["G2T"][(fo - 4) * 128:(fo - 3) * 128, tg * 512:(tg + 1) * 512], o[:], reads=[ok])
        elif fo < 20:
            def evac(tg, acc, akey, fo=fo):
                y, yk = tk.t_tmp()
                ra, rk = rs(rstd1, "rstd1", tg)
                p.add("dve", lambda e: e.tensor_tensor(y[:], acc[:], ra, ALU.mult), reads=[akey, rk], writes=[yk])
                tk.head_norm(y, yk, 128, C["g_q"][:, 0:1], "g_q", qmn[:, fo - 16, tg * 512:(tg + 1) * 512], ("qmn", fo - 16))
        else:
            def evac(tg, acc, akey, fo=fo):
                ra, rk = rs(rstd1, "rstd1", tg)
                tk.silu_gate(acc, akey, ra, rk, sgm[:, fo - 20, tg * 512:(tg + 1) * 512], ("sgm", fo - 20))
        tk.proj_fm(wb, wkey, 16, xb, "xb", evac)

    tk.flush()
    mem_attention(tk, io["memT"], C["g_mem"], "g_mem", io["w_mem_kv"], C["g_q"], C["g_k"][:, 0:1], "g_k", qmn, "qmn", sgm, "sgm",
                  io["Mm2T"], (memb, rsm_b, rsm_tm, mkn, mvb, mst))


def host_consts():
    c = const_mats()
    i = np.arange(64) % 32
    c["invf"] = (10000.0 ** (-(2.0 * i) / 64.0)).astype(np.float32).reshape(64, 1)
    k = np.arange(64)[:, None]
    m = np.arange(64)[None, :]
    c["PERM"] = (k == (m + 32) % 64).astype(np.float32)
    return c


TL, SEQ, BATCH, NCORE = 2048, 8192, 2, 8
NR1, NR3, NRO = 1152, 1408, 1536
GROUPS = [[0, 1, 2, 3], [4, 5, 6, 7]]
CC_QOS = os.environ.get("CC_QOS", "P2") or None

W_SHAPES = dict(a_w_in=[2048, 7168], a_w_out=[2048, 2048], w_dkv=[2048, 576], w_ukv=[512, 3072], b_w_in=[2048, 3072],
                b_w_uq=[512, 2304], b_w_out=[2048, 2048], w_mem_kv0=[2048, 1024], w_mem_kv1=[2048, 1024])
SMALL_F32 = dict(a_norm=[128, 16], mem_norm0=[128, 16], g_mem=[128, 16], g_mem_q0=[128, 1], g_mem_k0=[128, 1], g_q=[128, 1],
                 g_k=[128, 1], g_kv=[128, 16], g_b=[128, 16], g_ckv=[128, 4], g_q_lat=[128, 4], g_k_nope=[128, 1],
                 g_k_rope=[128, 1], g_q_nope=[128, 1], g_q_rope=[128, 1], invf=[64, 1], PERM=[64, 64], sel=[128, 4],
                 MS=[128, 128], MI=[128, 128])
SMALL_BF = dict(ONES=[128, 128], U=[128, 128], Uc=[128, 128], Z=[128, 128])


def attn_out_writer(cx, C, XIs, XOs, nqg):
    p = cx.p
    mk = [cx.sb([128, 4, 512], BF16) for _ in range(3)]
    st = {"n": 0, "ops": {}}

    def out(h, qg, O, okey):
        j = qg // 4
        cs = slice((qg % 4) * 512, (qg % 4 + 1) * 512)
        i = st["n"] % 3
        st["n"] += 1
        m = mk[i]
        for s_ in range(4):
            p.add("dve", lambda e, s_=s_: e.tensor_scalar(m[:, s_, :], O[:], C["sel"][:, s_:s_ + 1], None, ALU.mult),
                  reads=[okey, "sel"], writes=[("omk", i)])
        dst = XIs[h][j * 512:(j + 1) * 512, cs].rearrange("(s p) c -> p s c", p=128)
        st["ops"].setdefault(h, []).append(cx.dma(dst, m[:], reads=[("omk", i)]))
        if qg == nqg - 1:
            xi, xo = XIs[h], XOs[h]
            p.add_cc(lambda e: e.collective_compute("ReduceScatter", ALU.add, replica_groups=GROUPS, dma_qos=CC_QOS, ins=[xi], outs=[xo]),
                     writes=[("XO", h)], deps=st["ops"][h])
    return out


def build_fused():
    cx = Ctx()
    p, nc = cx.p, cx.nc
    ext = {}
    ext["xT"] = cx.dram("xT", [2048, TL], F32, "ExternalInput")
    ext["memT"] = cx.dram("memT", [2048, 256], F32, "ExternalInput")
    ext["pos"] = cx.dram("pos", [TL], I32, "ExternalInput")
    for nm, shp in W_SHAPES.items():
        ext[nm] = cx.dram(nm, shp, F32, "ExternalInput")
    Y = cx.dram("Y", [2048, TL], F32, "ExternalOutput")
    C = {}
    for nm, shp in SMALL_F32.items():
        d = cx.dram(nm, shp, F32, "ExternalInput")
        t = cx.gsb(shp, F32)
        cx.dma(t[:], d, writes=[nm])
        C[nm] = t
    for nm, shp in SMALL_BF.items():
        d = cx.dram(nm, shp, BF16, "ExternalInput")
        t = cx.gsb(shp, BF16)
        cx.dma(t[:], d, writes=[nm])
        C[nm] = t
    C["ON"] = C["ONES"]
    p.last_w["ON"] = p.last_w["ONES"]
    C["MSb"] = cx.gsb([128, 128], BF16)
    C["MIb"] = cx.gsb([128, 128], BF16)
    C["ZR"] = cx.gsb([128, 512], BF16)
    C["one"] = cx.gsb([1, 1], F32)
    p.add("dve", lambda e: e.tensor_copy(C["MSb"][:], C["MS"][:]), reads=["MS"], writes=["MSb"])
    p.add("dve", lambda e: e.tensor_copy(C["MIb"][:], C["MI"][:]), reads=["MI"], writes=["MIb"])
    p.add("pool", lambda e: e.memset(C["ZR"][:], 0.0), writes=["ZR"])
    p.add("pool", lambda e: e.memset(C["one"][:], 1.0), writes=["one"])

    def idram(name, shape, dt):
        return nc.dram_tensor(name, list(shape), dt).ap()

    RB = (512, 448, 448)
    XI1 = [[idram("XI1_%d_%d" % (r, s_), [4 * 384, TL], BF16) for s_ in range(4)] for r in range(3)]
    XO1 = [[idram("XO1_%d_%d" % (r, s_), [384, TL], BF16) for s_ in range(4)] for r in range(3)]
    XI2 = [idram("XI2_%d" % r, [4 * 512, TL], BF16) for r in range(3)]
    XO2 = [idram("XO2_%d" % r, [512, TL], BF16) for r in range(3)]
    XI3 = [[idram("XI3_%d_%d" % (r, s_), [4 * RB[r], TL], BF16) for s_ in range(4)] for r in range(3)]
    XO3 = [[idram("XO3_%d_%d" % (r, s_), [RB[r], TL], BF16) for s_ in range(4)] for r in range(3)]
    XI4 = [idram("XI4_%d" % r, [4 * 512, TL], BF16) for r in range(3)]
    XO4 = [idram("XO4_%d" % r, [512, TL], BF16) for r in range(3)]
    GT, MmT = idram("GT", [1536, TL], BF16), idram("MmT", [512, TL], BF16)
    G2T, Mm2T = idram("G2T", [1536, TL], BF16), idram("Mm2T", [512, TL], BF16)
    X1T = idram("X1T", [2048, TL], F32)

    def xchg3(xis, xos, rounds=(0, 1, 2), chain=True, deps=()):
        prev = []
        for r in rounds:
            for s_ in range(4):
                xi, xo = xis[r][s_], xos[r][s_]
                op = p.add_cc(lambda e, xi=xi, xo=xo: e.collective_compute("ReduceScatter", ALU.add, replica_groups=GROUPS, dma_qos=CC_QOS,
                                                                            ins=[xi], outs=[xo]),
                              writes=[("XO", r, s_)], deps=list(deps) + (prev if chain else []))
                prev = [op]

    def mkcc(xis, xos):
        def cc(r, deps):
            xi, xo = xis[r], xos[r]
            p.add_cc(lambda e: e.collective_compute("ReduceScatter", ALU.add, replica_groups=GROUPS, dma_qos=CC_QOS, ins=[xi], outs=[xo]),
                     writes=[("XO", r)], deps=deps)
        return cc

    cx.begin_phase()
    tk = Tok(cx, TL, C)
    phase_la1(cx, tk, dict(xT=ext["xT"], a_w_in=ext["a_w_in"], memT=ext["memT"], w_mem_kv0=ext["w_mem_kv0"],
                           XI1=XI1, GT=GT, MmT=MmT), C, TL)
    cx.end_phase()
    xchg3(XI1, XO1)
    cx.begin_phase()
    io = dict(QT=lambda h, sl: XO1[h][sl][0:128, :], KT=lambda h, sl: XO1[h][sl][128:256, :],
              V=lambda h, sl: XO1[h][sl][256:384, :].rearrange("p (k d) -> p k d", d=128),
              out=attn_out_writer(cx, C, XI2, XO2, SEQ // 512))
    phase_sb(cx, io, C, 3, SEQ, 128 ** -0.5)
    cx.end_phase()
    cx.begin_phase()
    tk = Tok(cx, TL, C)
    phase_out(cx, tk, dict(OT=XO2, GT=GT, MmT=MmT, XR=ext["xT"], W=ext["a_w_out"], Y=X1T), TL)
    cx.end_phase()
    TH = 1024
    for hh in range(TL // TH):
        cs = slice(hh * TH, (hh + 1) * TH)
        cx.begin_phase()
        tk = Tok(cx, TH, C)
        phase_lb1(cx, tk, dict(X1T=X1T[:, cs], pos=ext["pos"][cs], w_dkv=ext["w_dkv"], w_ukv=ext["w_ukv"],
                               b_w_in=ext["b_w_in"], b_w_uq=ext["b_w_uq"], memT=ext["memT"], w_mem_kv=ext["w_mem_kv1"],
                               G2T=G2T[:, cs], Mm2T=Mm2T[:, cs], XI3=XI3,
                               cc_early=((lambda deps: xchg3(XI3, XO3, rounds=(0,), chain=False, deps=deps))
                                         if hh == TL // TH - 1 else None)), TH, C, hh * TH)
        cx.end_phase(wait_cc=False)
    xchg3(XI3, XO3, rounds=(1, 2))
    cx.begin_phase()
    io = dict(QnT=lambda h, sl: XO3[h][sl][0:128, :], KnT=lambda h, sl: XO3[h][sl][128:256, :],
              V=lambda h, sl: XO3[h][sl][256:384, :].rearrange("p (k d) -> p k d", d=128),
              QrT=lambda h, sl: XO3[h][sl][384:448, :], KrT=lambda sl: XO3[0][sl][448:512, :],
              out=attn_out_writer(cx, C, XI4, XO4, SEQ // 512))
    phase_mla(cx, io, C, 3, SEQ, 192 ** -0.5)
    cx.end_phase()
    cx.begin_phase()
    tk = Tok(cx, TL, C)
    phase_out(cx, tk, dict(OT=XO4, GT=G2T, MmT=Mm2T, XR=X1T, W=ext["b_w_out"], Y=Y, final=True), TL)
    cx.end_phase()
    return cx.finish()


def _g16(g):
    return np.ascontiguousarray(np.asarray(g, np.float32).reshape(-1, 128).T)


def _col(g):
    g = np.asarray(g, np.float32)
    o = np.zeros((128, 1), np.float32)
    o[:len(g), 0] = g
    return o


def make_inputs(x, mem, positions, a_norm, a_w_in, a_w_out, kv_norm, w_dkv, g_ckv, w_ukv,
                g_k_nope, g_k_rope, b_norm, b_w_in, b_g_q_lat, b_w_uq, b_g_q_nope, b_g_q_rope,
                b_w_out, mem_norm, w_mem_kv, g_mem_q, g_mem_k):
    f32 = lambda a: np.ascontiguousarray(np.asarray(a, np.float32))
    x, mem = f32(x), f32(mem)
    positions = np.asarray(positions, np.int32)
    c = host_consts()
    shared = dict(a_w_in=f32(a_w_in[0]), a_w_out=f32(a_w_out[0]), w_dkv=f32(w_dkv), w_ukv=f32(w_ukv), b_w_in=f32(b_w_in[0]),
                  b_w_uq=f32(b_w_uq[0]), b_w_out=f32(b_w_out[0]), w_mem_kv0=f32(w_mem_kv[0]), w_mem_kv1=f32(w_mem_kv[1]),
                  a_norm=_g16(a_norm[0]), mem_norm0=_g16(mem_norm[0]), g_mem=_g16(mem_norm[1]), g_mem_q0=_col(g_mem_q[0]),
                  g_mem_k0=_col(g_mem_k[0]), g_q=_col(g_mem_q[1]), g_k=_col(g_mem_k[1]), g_kv=_g16(kv_norm), g_b=_g16(b_norm[0]),
                  g_ckv=_g16(g_ckv), g_q_lat=_g16(b_g_q_lat[0]), g_k_nope=_col(g_k_nope), g_k_rope=_col(g_k_rope),
                  g_q_nope=_col(b_g_q_nope[0]), g_q_rope=_col(b_g_q_rope[0]), invf=c["invf"], PERM=c["PERM"],
                  MS=c["MS"], MI=c["MI"], ONES=c["ONES"], U=c["U"], Uc=c["Uc"], Z=c["Z"])
    ims = []
    for b in range(BATCH):
        memT = np.ascontiguousarray(mem[b].T)
        for j in range(4):
            sel = np.zeros((128, 4), np.float32)
            sel[:, j] = 1.0
            d = dict(shared)
            d.update(xT=np.ascontiguousarray(x[b, j * TL:(j + 1) * TL].T), memT=memT,
                     pos=np.ascontiguousarray(positions[b, j * TL:(j + 1) * TL]), sel=sel)
            ims.append(d)
    return ims


_NC_CACHE = {}


def kernel(**inputs):
    ims = make_inputs(**inputs)
    if "nc" not in _NC_CACHE:
        _NC_CACHE["nc"] = build_fused()
    res = run_spmd(_NC_CACHE["nc"], ims)
    out = np.empty((BATCH, SEQ, 2048), np.float32)
    for i in range(NCORE):
        b, j = divmod(i, 4)
        out[b, j * TL:(j + 1) * TL] = res[i]["Y"].T
    return out
```

```python
import contextlib
import numpy as np
import ml_dtypes
import concourse.bass as bass
import concourse.mybir as mybir
from concourse.bass_utils import run_bass_kernel_spmd

F32 = mybir.dt.float32
BF16 = mybir.dt.bfloat16
I32 = mybir.dt.int32
AF = mybir.ActivationFunctionType
ALU = mybir.AluOpType
NPBF = ml_dtypes.bfloat16

ENGS = ("pe", "act", "dve", "pool", "sp")
NDSEM = 16
import os
SAFE_SAME_ENGINE = os.environ.get("UNSAFE_SE") is None


class Op:
    __slots__ = ("eng", "fn", "dma", "deps", "needed", "sig", "idx", "cc")

    def __init__(self, eng, fn, dma):
        self.eng, self.fn, self.dma = eng, fn, dma
        self.deps = []
        self.needed = False
        self.sig = None
        self.cc = False


class Prog:
    def __init__(self, nc):
        self.nc = nc
        self.ops = {e: [] for e in ENGS}
        self.last_w = {}
        self.readers = {}
        self.dmas = []
        self.out_dmas = []
        self.ccs = []
        self.bar_idx = 0

    def barrier(self, wait_cc=True):
        deps = []
        for e in ENGS:
            for op in reversed(self.ops[e]):
                if not op.dma and op.fn is not None:
                    deps.append(op)
                    break
        deps.extend(self.dmas[self.bar_idx:])
        self.bar_idx = len(self.dmas)
        if wait_cc:
            deps.extend(self.ccs)
        for d in deps:
            d.needed = True
        for e in ENGS:
            op = Op(e, None, False)
            op.deps = [d for d in deps if d.dma or d.cc or d.eng != e]
            self.ops[e].append(op)

    def add_cc(self, fn, reads=(), writes=(), deps=()):
        op = self.add("pool", fn, reads, writes, dma=True)
        self.dmas.pop()
        op.cc = True
        for d in deps:
            d.needed = True
            op.deps.append(d)
        if op.deps and op.deps[-1].dma and len(self.dmas) >= NDSEM and op.deps[-1] is self.dmas[len(self.dmas) - NDSEM]:
            pass
        self.ccs.append(op)
        return op

    def _need(self, d, op):
        if d is op:
            return False
        if d.dma or op.dma:
            return True
        if d.eng == op.eng:
            if d.eng == "pe":
                return False
            return SAFE_SAME_ENGINE
        return True

    def add(self, eng, fn, reads=(), writes=(), dma=False, out=False):
        op = Op(eng, fn, dma)
        deps = []
        for k in reads:
            w = self.last_w.get(k)
            if w is not None:
                deps.append(w)
        for k in writes:
            w = self.last_w.get(k)
            if w is not None:
                deps.append(w)
            deps.extend(self.readers.get(k, ()))
        seen = set()
        for d in deps:
            if id(d) in seen or not self._need(d, op):
                continue
            seen.add(id(d))
            d.needed = True
            op.deps.append(d)
        for k in writes:
            self.last_w[k] = op
            self.readers[k] = []
        for k in reads:
            lst = self.readers.setdefault(k, [])
            if not dma:
                lst[:] = [r for r in lst if r.dma or r.eng != eng]
            lst.append(op)
        if dma:
            n = len(self.dmas)
            if n >= NDSEM:
                op.deps.append(self.dmas[n - NDSEM])
            self.dmas.append(op)
            op.needed = True
            if out:
                self.out_dmas.append(op)
        self.ops[eng].append(op)
        return op

    def emit(self):
        nc = self.nc
        fin = Op("sp", None, False)
        fin.deps = list(self.out_dmas)
        self.ops["sp"].append(fin)
        with contextlib.ExitStack() as es:
            esem = {e: es.enter_context(nc.semaphore("s_" + e)) for e in ENGS}
            dsem = [es.enter_context(nc.semaphore("d_%d" % i)) for i in range(NDSEM)]
            for n, op in enumerate(self.dmas):
                op.sig = (dsem[n % NDSEM], 16 * (n // NDSEM + 1))
            for n, op in enumerate(self.ccs):
                op.sig = (es.enter_context(nc.semaphore("cc_%d" % n)), 1)
            for e in ENGS:
                c = 0
                for op in self.ops[e]:
                    if op.dma or op.cc:
                        continue
                    if op.needed:
                        c += 1
                        op.sig = (esem[e], c)
            block = es.enter_context(nc.Block())

            def run(engname):
                def body(eng):
                    waited = {}
                    for op in self.ops[engname]:
                        need = {}
                        for d in op.deps:
                            s, v = d.sig
                            if waited.get(s.num, 0) < v and need.get(s.num, (None, 0))[1] < v:
                                need[s.num] = (s, v)
                        for s, v in need.values():
                            eng.wait_ge(s, v)
                            waited[s.num] = v
                        if op.fn is None:
                            continue
                        ins = op.fn(eng)
                        if op.cc:
                            ins.then_inc(op.sig[0])
                        elif op.needed and ins is not None:
                            ins.then_inc(op.sig[0], 16 if op.dma else 1)
                return body

            block.tensor(run("pe"))
            block.scalar(run("act"))
            block.vector(run("dve"))
            block.gpsimd(run("pool"))
            block.sync(run("sp"))


class Ctx:
    def __init__(self, name="k"):
        self.nc = bass.Bass("TRN2", target_bir_lowering=False)
        self.es = contextlib.ExitStack()
        self.p = Prog(self.nc)
        self.n = 0
        self.ph = None
        self.banks = [self.es.enter_context(self.nc.psum_tensor("bank%d" % i, [128, 512], F32)) for i in range(8)]
        self.nps = 0

    def begin_phase(self):
        self.ph = contextlib.ExitStack()
        self.nps = 0

    def end_phase(self, wait_cc=True):
        self.p.barrier(wait_cc)
        self.ph.close()
        self.ph = None

    def gsb(self, shape, dt):
        self.n += 1
        return self.es.enter_context(self.nc.sbuf_tensor("gsb%d" % self.n, list(shape), dt))

    def dram(self, name, shape, dt, kind):
        return self.nc.dram_tensor(name, list(shape), dt, kind=kind).ap()

    def sb(self, shape, dt, name=None):
        self.n += 1
        st = self.ph if self.ph is not None else self.es
        return st.enter_context(self.nc.sbuf_tensor(name or "sb%d" % self.n, list(shape), dt))

    def ps(self, name=None):
        b = self.banks[self.nps]
        self.nps += 1
        return b

    def dma(self, out, in_, reads=(), writes=(), is_out=False):
        return self.p.add("sp", lambda e: e.dma_start(out=out, in_=in_), reads, writes, dma=True, out=is_out)

    def mm(self, out, lhsT, rhs, start, stop, reads=(), writes=()):
        return self.p.add("pe", lambda e: e.matmul(out, lhsT, rhs, start=start, stop=stop, skip_group_check=True),
                          reads, writes)

    def act(self, out, in_, func, reads=(), writes=(), scale=1.0, bias=0.0):
        return self.p.add("act", lambda e: e.activation(out, in_, func, bias=bias, scale=scale), reads, writes)

    def finish(self):
        self.p.emit()
        self.es.close()
        return self.nc


def run_spmd(nc, in_maps):
    if os.environ.get("KTRACE"):
        res = run_bass_kernel_spmd(nc, in_maps, core_ids=list(range(len(in_maps))), trace=True)
        print("EXEC_TIME_NS", res.exec_time_ns)
        return res.results
    res = run_bass_kernel_spmd(nc, in_maps, core_ids=list(range(len(in_maps))))
    return res.results


def const_mats():
    j = np.arange(128)[:, None]
    s = np.arange(128)[None, :]
    c = {}
    c["U"] = np.where(j >= s, -1.0, 0.0).astype(NPBF)
    c["Uc"] = np.where(j < s, -1.0, 0.0).astype(NPBF)
    c["Z"] = np.zeros((128, 128), NPBF)
    c["ONES"] = np.ones((128, 128), NPBF)
    c["MS"] = (s > j).astype(np.float32)
    c["MI"] = (s >= j).astype(np.float32)
    return c


def phase_sb(cx, io, C, NH, S, scale):
    nc, p = cx.nc, cx.p
    QT, KT, V = io["QT"], io["KT"], io["V"]
    NB = 2
    qt = [cx.sb([128, S], BF16) for _ in range(NB)]
    kt = [cx.sb([128, S], BF16) for _ in range(NB)]
    vv = [cx.sb([128, S // 128, 128], BF16) for _ in range(NB)]
    U, Uc, Z, MSb, ZR = C["U"], C["Uc"], C["Z"], C["MSb"], C["ZR"]
    NE = 3
    Eb = [cx.sb([128, 512], F32) for _ in range(NE)]
    Lb = [cx.sb([128, 512], BF16) for _ in range(NE)]
    Xb = [cx.sb([128, 512], F32) for _ in range(NE)]
    Ab = [cx.sb([128, 512], BF16) for _ in range(NE)]
    Ob = [cx.sb([128, 512], F32) for _ in range(2)]
    zb = [cx.ps() for _ in range(2)]
    runb = [cx.ps() for _ in range(2)]
    outb = [cx.ps() for _ in range(2)]
    dmy = cx.ps()
    ND = int(os.environ.get("SB_DUMMY", "3"))

    def warm(k):
        for _ in range(k):
            cx.mm(dmy[:, :], Z[:], ZR[:], True, True, reads=["Z", "ZR"])

    tiles = []
    nqg = S // 512
    for h in range(NH):
        for qg in range(nqg):
            t0 = qg * 512
            kb_hi = (t0 + 512) // 128 - 1
            for kb in range(kb_hi, -1, -1):
                off = max(0, kb * 128 - t0)
                tiles.append(dict(h=h, qg=qg, kb=kb, off=off, diag=(kb * 128 >= t0),
                                  first=(kb == kb_hi), last=(kb == 0), sid=h * nqg + qg))
    n = len(tiles)
    loaded = set()

    SL = S // 4
    KBS = SL // 128

    def load_slot(h, sl):
        if (h, sl) in loaded or h >= NH or sl >= 4:
            return
        loaded.add((h, sl))
        b = h % NB
        cs = slice(sl * SL, (sl + 1) * SL)
        cx.dma(qt[b][:, cs], QT(h, sl), reads=[("XO", h, sl)], writes=[("qt", b, sl)])
        cx.dma(kt[b][:, cs], KT(h, sl), reads=[("XO", h, sl)], writes=[("kt", b, sl)])
        cx.dma(vv[b][:, sl * KBS:(sl + 1) * KBS, :], V(h, sl), reads=[("XO", h, sl)], writes=[("vv", b, sl)])

    def prefetch(t):
        h, qg = t["h"], t["qg"]
        load_slot(h, qg * 512 // SL)
        nq = qg + 1
        if nq < nqg:
            load_slot(h, nq * 512 // SL)
        else:
            load_slot(h + 1, 0)

    def s1a(i):
        t = tiles[i]
        hb = t["h"] % NB
        if t["first"]:
            prefetch(t)
        c0, c1 = t["off"], 512
        q0 = t["qg"] * 512
        z = zb[i % 2]
        cx.mm(z[:, c0:c1], kt[hb][:, t["kb"] * 128:(t["kb"] + 1) * 128], qt[hb][:, q0 + c0:q0 + c1],
              True, True, reads=[("kt", hb, t["kb"] // KBS), ("qt", hb, q0 // SL)], writes=[("z", i % 2)])

    def s1b(i):
        t = tiles[i]
        c0, c1 = t["off"], 512
        cx.act(Eb[i % NE][:, c0:c1], zb[i % 2][:, c0:c1], AF.Exp, scale=scale,
               reads=[("z", i % 2)], writes=[("E", i % NE)])

    def s1c(i):
        t = tiles[i]
        c0, c1 = t["off"], 512
        cx.act(Lb[i % NE][:, c0:c1], Eb[i % NE][:, c0:c1], AF.Ln, bias=1.0,
               reads=[("E", i % NE)], writes=[("L", i % NE)])
        if t["diag"]:
            L = Lb[i % NE]
            p.add("dve", lambda e: e.tensor_tensor(L[:, c0:c0 + 128], L[:, c0:c0 + 128], MSb[:], ALU.mult),
                  reads=[("L", i % NE), "MSb"], writes=[("L", i % NE)])

    def s2(i):
        t = tiles[i]
        c0, c1 = t["off"], 512
        sp = t["sid"] % 2
        hb = t["h"] % NB
        rb, ob = runb[sp], outb[sp]
        L = Lb[i % NE]
        if t["first"]:
            cx.mm(rb[:, :], Z[:], ZR[:], True, False, reads=["Z", "ZR"], writes=[("run", sp)])
            cx.mm(ob[:, :], Z[:], ZR[:], True, False, reads=["Z", "ZR"], writes=[("out", sp)])
        cx.mm(rb[:, c0:c1], U[:], L[:, c0:c1], False, False, reads=["U", ("L", i % NE)], writes=[("run", sp)])
        return t, c0, c1, sp, hb, rb, ob, L

    def s2x(i):
        t = tiles[i]
        c0, c1 = t["off"], 512
        sp = t["sid"] % 2
        cx.act(Xb[i % NE][:, c0:c1], runb[sp][:, c0:c1], AF.Exp,
               reads=[("run", sp)], writes=[("X", i % NE)])

    def s2uc(i):
        t = tiles[i]
        c0, c1 = t["off"], 512
        sp = t["sid"] % 2
        rb = runb[sp]
        L, E, X, A = Lb[i % NE], Eb[i % NE], Xb[i % NE], Ab[i % NE]
        if not t["last"]:
            cx.mm(rb[:, c0:c1], Uc[:], L[:, c0:c1], False, False, reads=["Uc", ("L", i % NE)], writes=[("run", sp)])
        p.add("dve", lambda e: e.tensor_tensor(A[:, c0:c1], E[:, c0:c1], X[:, c0:c1], ALU.mult),
              reads=[("E", i % NE), ("X", i % NE)], writes=[("A", i % NE)])
        if t["diag"]:
            p.add("dve", lambda e: e.tensor_tensor(A[:, c0:c0 + 128], A[:, c0:c0 + 128], MSb[:], ALU.mult),
                  reads=[("A", i % NE), "MSb"], writes=[("A", i % NE)])

    def s2av(i):
        t = tiles[i]
        c0, c1 = t["off"], 512
        sp = t["sid"] % 2
        hb = t["h"] % NB
        ob = outb[sp]
        A = Ab[i % NE]
        cx.mm(ob[:, c0:c1], vv[hb][:, t["kb"], :], A[:, c0:c1], False, t["last"],
              reads=[("vv", hb, t["kb"] // KBS), ("A", i % NE)], writes=[("out", sp)])
        if t["last"]:
            O = Ob[sp]
            p.add("dve", lambda e: e.tensor_copy(O[:], ob[:]), reads=[("out", sp)], writes=[("O", sp)])
            io["out"](t["h"], t["qg"], O, ("O", sp))

    s1a(0)
    if n > 1:
        s1a(1)
    s1b(0)
    s1c(0)
    for i in range(n):
        s2(i)
        if i + 2 < n:
            s1a(i + 2)
        warm(ND)
        if i + 1 < n:
            s1b(i + 1)
        s2x(i)
        if i + 1 < n:
            s1c(i + 1)
        s2uc(i)
        if i >= 1:
            s2av(i - 1)
    s2av(n - 1)


def phase_mla(cx, io, C, NH, S, scale):
    nc, p = cx.nc, cx.p
    QnT, QrT, KnT, KrT, V = io["QnT"], io["QrT"], io["KnT"], io["KrT"], io["V"]
    NB = 2
    qn = [cx.sb([128, S], BF16) for _ in range(NB)]
    qr = [cx.sb([64, S], BF16) for _ in range(NB)]
    kn = [cx.sb([128, S], BF16) for _ in range(NB)]
    kr = cx.sb([64, S], BF16)
    vv = [cx.sb([128, S // 128, 128], BF16) for _ in range(NB)]
    ON, Z, MIb, ZR = C["ON"], C["Z"], C["MIb"], C["ZR"]
    NE = 3
    Pb = [cx.sb([128, 512], BF16) for _ in range(NE)]
    Rb = [cx.sb([128, 512], F32) for _ in range(2)]
    Ob = [cx.sb([128, 512], F32) for _ in range(2)]
    zb = [cx.ps() for _ in range(2)]
    denb = [cx.ps() for _ in range(2)]
    outb = [cx.ps() for _ in range(2)]
    dmy = cx.ps()
    ND = int(os.environ.get("MLA_DUMMY", "0"))

    def warm(k):
        for _ in range(k):
            cx.mm(dmy[:, :], Z[:], ZR[:], True, True, reads=["Z", "ZR"])
    SL = S // 4
    KBS = SL // 128

    tiles = []
    nqg = S // 512
    for h in range(NH):
        for qg in range(nqg):
            t0 = qg * 512
            kb_hi = (t0 + 512) // 128 - 1
            for kb in range(kb_hi, -1, -1):
                off = max(0, kb * 128 - t0)
                tiles.append(dict(h=h, qg=qg, kb=kb, off=off, diag=(kb * 128 >= t0),
                                  first=(kb == kb_hi), last=(kb == 0), sid=h * nqg + qg))
    n = len(tiles)
    loaded = set()

    def load_slot(h, sl):
        if (h, sl) in loaded or h >= NH or sl >= 4:
            return
        loaded.add((h, sl))
        b = h % NB
        cs = slice(sl * SL, (sl + 1) * SL)
        if h == 0:
            cx.dma(kr[:, cs], KrT(sl), reads=[("XO", 0, sl)], writes=[("kr", sl)])
        cx.dma(qn[b][:, cs], QnT(h, sl), reads=[("XO", h, sl)], writes=[("qn", b, sl)])
        cx.dma(qr[b][:, cs], QrT(h, sl), reads=[("XO", h, sl)], writes=[("qr", b, sl)])
        cx.dma(kn[b][:, cs], KnT(h, sl), reads=[("XO", h, sl)], writes=[("kn", b, sl)])
        cx.dma(vv[b][:, sl * KBS:(sl + 1) * KBS, :], V(h, sl), reads=[("XO", h, sl)], writes=[("vv", b, sl)])

    def prefetch(t):
        h, qg = t["h"], t["qg"]
        load_slot(h, qg * 512 // SL)
        nq = qg + 1
        if nq < nqg:
            load_slot(h, nq * 512 // SL)
        else:
            load_slot(h + 1, 0)

    def sA(i):
        t = tiles[i]
        hb = t["h"] % NB
        if t["first"]:
            prefetch(t)
        c0, c1 = t["off"], 512
        q0 = t["qg"] * 512
        k0 = t["kb"] * 128
        z = zb[i % 2]
        ks, qs = t["kb"] // KBS, q0 // SL
        cx.mm(z[:, c0:c1], kn[hb][:, k0:k0 + 128], qn[hb][:, q0 + c0:q0 + c1], True, False,
              reads=[("kn", hb, ks), ("qn", hb, qs)], writes=[("z", i % 2)])
        cx.mm(z[:, c0:c1], kr[:, k0:k0 + 128], qr[hb][:, q0 + c0:q0 + c1], False, True,
              reads=[("kr", ks), ("qr", hb, qs)], writes=[("z", i % 2)])

    def sB(i):
        t = tiles[i]
        c0, c1 = t["off"], 512
        P = Pb[i % NE]
        cx.act(P[:, c0:c1], zb[i % 2][:, c0:c1], AF.Exp, scale=scale,
               reads=[("z", i % 2)], writes=[("P", i % NE)])
        if t["diag"]:
            p.add("dve", lambda e: e.tensor_tensor(P[:, c0:c0 + 128], P[:, c0:c0 + 128], MIb[:], ALU.mult),
                  reads=[("P", i % NE), "MIb"], writes=[("P", i % NE)])

    def sC(i):
        t = tiles[i]
        c0, c1 = t["off"], 512
        sp = t["sid"] % 2
        hb = t["h"] % NB
        db, ob = denb[sp], outb[sp]
        P = Pb[i % NE]
        if t["first"]:
            cx.mm(db[:, :], Z[:], ZR[:], True, False, reads=["Z", "ZR"], writes=[("den", sp)])
            cx.mm(ob[:, :], Z[:], ZR[:], True, False, reads=["Z", "ZR"], writes=[("out", sp)])
        cx.mm(ob[:, c0:c1], vv[hb][:, t["kb"], :], P[:, c0:c1], False, t["last"],
              reads=[("vv", hb, t["kb"] // KBS), ("P", i % NE)], writes=[("out", sp)])
        cx.mm(db[:, c0:c1], ON[:], P[:, c0:c1], False, t["last"],
              reads=["ON", ("P", i % NE)], writes=[("den", sp)])
        if t["last"]:
            O, R = Ob[sp], Rb[sp]
            p.add("dve", lambda e: e.reciprocal(R[:], db[:]), reads=[("den", sp)], writes=[("R", sp)])
            p.add("dve", lambda e: e.tensor_tensor(O[:], ob[:], R[:], ALU.mult),
                  reads=[("out", sp), ("R", sp)], writes=[("O", sp)])
            io["out"](t["h"], t["qg"], O, ("O", sp))

    sA(0)
    if n > 1:
        sA(1)
    sB(0)
    for i in range(n):
        if i + 2 < n:
            sA(i + 2)
        warm(ND)
        if i + 1 < n:
            sB(i + 1)
        sC(i)


EPS = 1e-6
NWB = 3


def x3(X, r0, P, c0):
    return X[r0:r0 + P, :].rearrange("p (s c) -> p s c", s=4)[:, :, c0:c0 + 512]


class Tok:
    def __init__(self, cx, T, C):
        self.cx, self.p, self.T = cx, cx.p, T
        self.NTG = T // 512
        self.C = C
        self.ON = C["ON"]
        self.one = C["one"]
        self.mk = [cx.sb([128, 4, 512], BF16) for _ in range(3)]
        self.nmk = 0
        self.xops = []
        self.acc = [cx.ps() for _ in range(2)]
        self.ss = [cx.ps() for _ in range(4)]
        self.aux = [cx.ps() for _ in range(2)]
        deep = self.NTG <= 2
        self.hn = [(self.aux[1], ("aux", 1))]
        self.rp = [(self.aux[0], ("aux", 0))]
        if deep:
            self.hn.append((self.ss[2], ("ss", 2)))
            self.rp.append((self.ss[3], ("ss", 3)))
        self.nhn = self.nrp = 0
        self.NTMP, self.NSQ, self.NOB = (12, 4, 6) if deep else (4, 2, 3)
        self.delay = False
        self.rr_block = deep
        self.pend_ev = []
        self.wst = [cx.sb([128, 16, 128], F32) for _ in range(NWB)]
        self.wbf = [cx.sb([128, 16, 128], BF16) for _ in range(NWB)]
        self.tmp = [cx.sb([128, 512], F32) for _ in range(self.NTMP)]
        self.sq = [cx.sb([128, 512], BF16) for _ in range(self.NSQ)]
        self.ob = [cx.sb([128, 512], BF16) for _ in range(self.NOB)]
        self.nw = self.nt = self.nsq = self.nob = self.nacc = 0

    def xwrite(self, src_ap, skey, P, dst3):
        cx, p = self.cx, self.p
        sel = self.C["sel"]
        i = self.nmk % 3
        self.nmk += 1
        m = self.mk[i]
        for s_ in range(4):
            if s_ % 2 == 0:
                p.add("dve", lambda e, s_=s_: e.tensor_scalar(m[0:P, s_, :], src_ap, sel[0:P, s_:s_ + 1], None, ALU.mult),
                      reads=[skey, "sel"], writes=[("mk", i)])
            else:
                p.add("act", lambda e, s_=s_: e.mul(m[0:P, s_, :], src_ap, sel[0:P, s_:s_ + 1]),
                      reads=[skey, "sel"], writes=[("mk", i)])
        for s_ in range(4):
            self.xops.append(cx.dma(dst3(s_), m[0:P, s_, :], reads=[("mk", i)]))

    def t_tmp(self):
        i = self.nt % self.NTMP
        self.nt += 1
        return self.tmp[i], ("tmp", i)

    def t_sq(self):
        i = self.nsq % self.NSQ
        self.nsq += 1
        return self.sq[i], ("sq", i)

    def t_ob(self):
        i = self.nob % self.NOB
        self.nob += 1
        return self.ob[i], ("ob", i)

    def t_acc(self):
        i = self.nacc % len(self.acc)
        self.nacc += 1
        return self.acc[i], ("acc", i)

    def rstd_from(self, out_ap, ss_ap, C, rkeys, wkeys):
        cx = self.cx
        cx.act(out_ap, ss_ap, AF.Ln, scale=1.0 / C, bias=EPS, reads=rkeys, writes=wkeys)
        cx.act(out_ap, out_ap, AF.Exp, scale=-0.5, reads=wkeys, writes=wkeys)

    def load_x(self, src, nk, xb, xkey, rstd_b=None, rkey=None, C=None, stage=None):
        cx, p, T = self.cx, self.p, self.T
        self._wload(NWB)
        for k in range(nk):
            st = stage[k % 2]
            cx.dma(st[:], src[k * 128:(k + 1) * 128, :], writes=[("xst", k % 2)])
            p.add("act", lambda e, k=k, st=st: e.copy(xb[:, k, :], st[:]),
                  reads=[("xst", k % 2)], writes=[(xkey, k)])
            if rstd_b is not None:
                for tg in range(self.NTG):
                    sq, sk = self.t_sq()
                    cx.act(sq[:], st[:, tg * 512:(tg + 1) * 512], AF.Square, reads=[("xst", k % 2)], writes=[sk])
                    cx.mm(self.ss[tg][:], self.ON[:], sq[:], k == 0, k == nk - 1,
                          reads=["ON", sk], writes=[("ss", tg)])
        if rstd_b is not None:
            for tg in range(self.NTG):
                self.rstd_from(rstd_b[:, tg * 512:(tg + 1) * 512], self.ss[tg][:], C, [("ss", tg)], [(rkey, tg)])

    def rstd_tm(self, rstd_b, rkey, rtm, tkey):
        cx, p = self.cx, self.p
        ntb = self.T // 128
        a = self.aux[0]
        for tb in range(ntb):
            cx.mm(a[:, tb:tb + 1], rstd_b[0:1, tb * 128:(tb + 1) * 128], self.one[0:1, 0:1], True, True,
                  reads=[(rkey, tb // 4), "one"], writes=[("aux", 0)])
        p.add("dve", lambda e: e.tensor_copy(rtm[:, 0:ntb], a[:, 0:ntb]), reads=[("aux", 0)], writes=[tkey])

    def plan(self, specs):
        self.wplan = list(specs)
        self.wi = 0
        self.wl = 0

    def _wload(self, upto):
        cx = self.cx
        while self.wl < min(upto, len(self.wplan)):
            W, c0, nk, width = self.wplan[self.wl]
            i = self.wl % NWB
            cx.dma(self.wst[i][:, 0:nk, 0:width], W[0:nk * 128, c0:c0 + width].rearrange("(k p) f -> p k f", p=128),
                   writes=[("wst", i)])
            self.wl += 1

    def weights(self, W, f0, fo, nk, gain=None, width=128):
        cx, p = self.cx, self.p
        spec = self.wplan[self.wi]
        assert spec[0] is W and spec[1] == f0 + fo * 128 and spec[2] == nk and spec[3] == width, (spec[1:], f0, fo, nk, width)
        self._wload(self.wi + NWB)
        i = self.wi % NWB
        self.wi += 1
        wf, wb = self.wst[i], self.wbf[i]
        if gain is not None:
            gap, gkey = gain
            p.add("pool", lambda e: e.tensor_tensor(wb[:, 0:nk, 0:width], wf[:, 0:nk, 0:width],
                                                     gap[:, 0:nk].unsqueeze(2).to_broadcast([128, nk, width]), ALU.mult),
                  reads=[("wst", i), gkey], writes=[("wbf", i)])
        else:
            p.add("pool", lambda e: e.tensor_copy(wb[:, 0:nk, 0:width], wf[:, 0:nk, 0:width]),
                  reads=[("wst", i)], writes=[("wbf", i)])
        return wb, ("wbf", i)

    @staticmethod
    def _run_rr(items):
        gens = []
        for f, a in items:
            r = f(*a)
            if r is not None and hasattr(r, "__next__"):
                gens.append(r)
        while gens:
            for g in list(gens):
                try:
                    next(g)
                except StopIteration:
                    gens.remove(g)

    def flush(self):
        ev, self.pend_ev = self.pend_ev, []
        self._run_rr(ev)

    def _evacs(self, items):
        if self.delay:
            prev, self.pend_ev = self.pend_ev, items
            self._run_rr(prev)
        else:
            self._run_rr(items)

    def proj_fm(self, wb, wkey, nk, xb, xkey, evac, width=128):
        cx = self.cx
        items = []
        for tg in range(self.NTG):
            acc, akey = self.t_acc()
            for k in range(nk):
                cx.mm(acc[0:width, :], wb[:, k, 0:width], xb[:, k, tg * 512:(tg + 1) * 512], k == 0, k == nk - 1,
                      reads=[wkey, (xkey, k)], writes=[akey])
            if self.rr_block:
                items.append((evac, (tg, acc, akey)))
            else:
                self._run_rr([(evac, (tg, acc, akey))])
        if self.rr_block:
            self._run_rr(items)

    def proj_tm(self, wb, wkey, nk, xb, xkey, evac, width=128):
        cx = self.cx
        ntb = self.T // 128
        items = []
        for g in range(ntb // 4):
            acc, akey = self.t_acc()
            for j in range(4):
                tb = g * 4 + j
                for k in range(nk):
                    cx.mm(acc[:, j * 128:j * 128 + width], xb[:, k, tb * 128:(tb + 1) * 128], wb[:, k, 0:width],
                          (j == 0 and k == 0), k == nk - 1, reads=[wkey, (xkey, k)], writes=[akey])
            if self.rr_block:
                items.append((evac, (g, acc, akey)))
            else:
                self._run_rr([(evac, (g, acc, akey))])
        if self.rr_block:
            self._run_rr(items)

    def head_norm_g(self, y, ykey, D, gcol, gkey, out_ap, okey):
        cx, p = self.cx, self.p
        sq, sk = self.t_sq()
        cx.act(sq[0:D, :], y[0:D, :], AF.Square, reads=[ykey], writes=[sk])
        a, ak = self.hn[self.nhn % len(self.hn)]
        self.nhn += 1
        cx.mm(a[0:D, :], self.ON[0:D, 0:D], sq[0:D, :], True, True, reads=["ON", sk], writes=[ak])
        yield
        r, rk = self.t_tmp()
        self.rstd_from(r[0:D, :], a[0:D, :], D, [ak], [rk])
        yield
        p.add("dve", lambda e: e.scalar_tensor_tensor(out_ap, y[0:D, :], gcol, r[0:D, :], ALU.mult, ALU.mult),
              reads=[ykey, rk, gkey], writes=[okey])

    def head_norm(self, *a):
        for _ in self.head_norm_g(*a):
            pass

    def silu_gate(self, acc, akey, rstd_ap, rkey, out_ap, okey):
        cx, p = self.cx, self.p
        g, gk = self.t_tmp()
        p.add("dve", lambda e: e.tensor_tensor(g[:], acc[:], rstd_ap, ALU.mult), reads=[akey, rkey], writes=[gk])
        cx.act(out_ap, g[:], AF.Silu, reads=[gk], writes=[okey])


def load_small(cx, dst, src, key):
    cx.dma(dst, src, writes=[key])


def mem_attention(tk, memT, gmem, gmem_key, Wkv, gq, gk, gk_key, qmn, qkey, sgm, sgkey, MmT_out, bufs):
    cx, p, T = tk.cx, tk.p, tk.T
    memb, rsm_b, rsm_tm, mkn, mvb, mst = bufs
    ML = 256
    for k in range(16):
        st = mst[k % 2]
        cx.dma(st[:], memT[k * 128:(k + 1) * 128, :], writes=[("mst", k % 2)])
        p.add("act", lambda e, k=k, st=st: e.copy(memb[:, k, :], st[:]), reads=[("mst", k % 2)], writes=[("memb", k)])
        sq, sk = tk.t_sq()
        cx.act(sq[:, 0:ML], st[:], AF.Square, reads=[("mst", k % 2)], writes=[sk])
        cx.mm(tk.ss[0][:, 0:ML], tk.ON[:], sq[:, 0:ML], k == 0, k == 15, reads=["ON", sk], writes=[("ss", 0)])
    tk.rstd_from(rsm_b[:], tk.ss[0][:, 0:ML], 2048, [("ss", 0)], ["rsm_b"])
    a = tk.aux[0]
    for mb in range(2):
        cx.mm(a[:, mb:mb + 1], rsm_b[0:1, mb * 128:(mb + 1) * 128], tk.one[0:1, 0:1], True, True,
              reads=["rsm_b", "one"], writes=[("aux", 0)])
    p.add("dve", lambda e: e.tensor_copy(rsm_tm[:], a[:, 0:2]), reads=[("aux", 0)], writes=["rsm_tm"])
    for h in range(4):
        wb, wkey = tk.weights(Wkv, 0, h, 16, gain=(gmem, gmem_key))
        acc, akey = tk.t_acc()
        for k in range(16):
            cx.mm(acc[:, 0:ML], wb[:, k, :], memb[:, k, :], k == 0, k == 15, reads=[wkey, ("memb", k)], writes=[akey])
        y, yk = tk.t_tmp()
        p.add("dve", lambda e, y=y, acc=acc: e.tensor_tensor(y[:, 0:ML], acc[:, 0:ML], rsm_b[:], ALU.mult),
              reads=[akey, "rsm_b"], writes=[yk])
        sq, sk = tk.t_sq()
        cx.act(sq[:, 0:ML], y[:, 0:ML], AF.Square, reads=[yk], writes=[sk])
        a1 = tk.aux[1]
        cx.mm(a1[:, 0:ML], tk.ON[:], sq[:, 0:ML], True, True, reads=["ON", sk], writes=[("aux", 1)])
        r, rk = tk.t_tmp()
        tk.rstd_from(r[:, 0:ML], a1[:, 0:ML], 128, [("aux", 1)], [rk])
        p.add("dve", lambda e, y=y, r=r, h=h: e.scalar_tensor_tensor(mkn[:, h, :], y[:, 0:ML], gk, r[:, 0:ML], ALU.mult, ALU.mult),
              reads=[yk, rk, gk_key], writes=[("mkn", h)])
    for h in range(4):
        wb, wkey = tk.weights(Wkv, 512, h, 16, gain=(gmem, gmem_key))
        acc, akey = tk.t_acc()
        for mb in range(2):
            for k in range(16):
                cx.mm(acc[:, mb * 128:(mb + 1) * 128], memb[:, k, mb * 128:(mb + 1) * 128], wb[:, k, :],
                      (mb == 0 and k == 0), k == 15, reads=[wkey, ("memb", k)], writes=[akey])
        p.add("dve", lambda e, acc=acc, h=h: e.tensor_tensor(
            mvb[:, :, h * 128:(h + 1) * 128], acc[:, 0:256].rearrange("p (m f) -> p m f", m=2),
            rsm_tm[:].unsqueeze(2).to_broadcast([128, 2, 128]), ALU.mult), reads=[akey, "rsm_tm"], writes=[("mvb", h)])
    sc = 128 ** -0.5
    for h in range(4):
        for tg in range(tk.NTG):
            cols = slice(tg * 512, (tg + 1) * 512)
            Ps = []
            for mb in range(2):
                acc, akey = tk.t_acc()
                cx.mm(acc[:], mkn[:, h, mb * 128:(mb + 1) * 128], qmn[:, h, cols], True, True,
                      reads=[("mkn", h), (qkey, h)], writes=[akey])
                P, pk = tk.t_ob()
                cx.act(P[:], acc[:], AF.Exp, scale=sc, reads=[akey], writes=[pk])
                Ps.append((P, pk))
            mo, den = tk.aux[0], tk.aux[1]
            for mb in range(2):
                P, pk = Ps[mb]
                cx.mm(mo[:], mvb[:, mb, h * 128:(h + 1) * 128], P[:], mb == 0, mb == 1, reads=[("mvb", h), pk], writes=[("aux", 0)])
            for mb in range(2):
                P, pk = Ps[mb]
                cx.mm(den[:], tk.ON[:], P[:], mb == 0, mb == 1, reads=["ON", pk], writes=[("aux", 1)])
            r, rk = tk.t_tmp()
            p.add("dve", lambda e, r=r, den=den: e.reciprocal(r[:], den[:]), reads=[("aux", 1)], writes=[rk])
            p.add("dve", lambda e, r=r, mo=mo: e.tensor_tensor(r[:], mo[:], r[:], ALU.mult), reads=[("aux", 0), rk], writes=[rk])
            o, ok = tk.t_ob()
            p.add("dve", lambda e, r=r, o=o, h=h, cols=cols: e.tensor_tensor(o[:], r[:], sgm[:, h, cols], ALU.mult),
                  reads=[rk, (sgkey, h)], writes=[ok])
            cx.dma(MmT_out[h * 128:(h + 1) * 128, cols], o[:], reads=[ok])


def phase_la1(cx, tk, io, C, T):
    p = cx.p
    D = 2048
    xT, W, memT, Wkv = io["xT"], io["a_w_in"], io["memT"], io["w_mem_kv0"]
    XI, GT, MmT = io["XI1"], io["GT"], io["MmT"]
    NR = 1152
    xb = cx.sb([128, 16, T], BF16)
    rstd_b = cx.sb([128, T], F32)
    rtm = cx.sb([128, 16], F32)
    stage = [cx.sb([128, T], F32) for _ in range(2)]
    gin, gmem, gq, gk = C["a_norm"], C["mem_norm0"], C["g_mem_q0"], C["g_mem_k0"]
    qmn = cx.sb([128, 4, T], BF16)
    sgm = cx.sb([128, 4, T], BF16)
    memb = cx.sb([128, 16, 256], BF16)
    rsm_b = cx.sb([128, 256], F32)
    rsm_tm = cx.sb([128, 2], F32)
    mkn = cx.sb([128, 4, 256], BF16)
    mvb = cx.sb([128, 2, 512], BF16)
    mst = [cx.sb([128, 256], F32) for _ in range(2)]
    vst = [cx.sb([128, 4, 128], BF16) for _ in range(2)]

    HO = [0, 3, 6, 9, 1, 4, 7, 10, 2, 5, 8, 11]
    order = []
    for r in range(3):
        hs = HO[4 * r:4 * r + 4]
        order += [hh for hh in hs] + [12 + hh for hh in hs] + [24 + hh for hh in hs]
    order += list(range(36, 56))
    tk.plan([(W, fo * 128, 16, 128) for fo in order] + [(Wkv, h * 128, 16, 128) for h in range(4)] + [(Wkv, 512 + h * 128, 16, 128) for h in range(4)])
    tk.load_x(xT, 16, xb, "xb", rstd_b, "rstd", D, stage)
    tk.rstd_tm(rstd_b, "rstd", rtm, "rtm")

    def rs(tg):
        return rstd_b[:, tg * 512:(tg + 1) * 512], ("rstd", tg)

    for n_, fo in enumerate(order):
        if n_ in (12, 24, 36) and io.get("cc"):
            io["cc"](n_ // 12 - 1, list(tk.xops))
        wb, wkey = tk.weights(W, 0, fo, 16, gain=(gin, "a_norm"))
        if fo < 24:
            def evac(tg, acc, akey, fo=fo):
                o, ok = tk.t_ob()
                ra, rk = rs(tg)
                p.add("dve", lambda e: e.tensor_tensor(o[:], acc[:], ra, ALU.mult), reads=[akey, rk], writes=[ok])
                hh = fo % 12
                r0 = (hh // 3) * 384 + (128 if fo >= 12 else 0)
                X_ = XI[hh % 3]
                tk.xwrite(o[:], ok, 128, lambda s_: X_[s_][r0:r0 + 128, tg * 512:(tg + 1) * 512])
            tk.proj_fm(wb, wkey, 16, xb, "xb", evac)
        elif fo < 36:
            def evac(g, acc, akey, fo=fo):
                i = g % 2
                v = vst[i]
                p.add("dve", lambda e: e.tensor_tensor(
                    v[:], acc[:].rearrange("p (j f) -> p j f", j=4),
                    rtm[:, g * 4:(g + 1) * 4].unsqueeze(2).to_broadcast([128, 4, 128]), ALU.mult),
                    reads=[akey, "rtm"], writes=[("vst", i)])
                hh = fo - 24
                r0 = (hh // 3) * 384 + 256
                X_ = XI[hh % 3]
                tk.xwrite(v[:].rearrange("p j f -> p (j f)"), ("vst", i), 128, lambda s_: X_[s_][r0:r0 + 128, g * 512:(g + 1) * 512])
            tk.proj_tm(wb, wkey, 16, xb, "xb", evac)
        elif fo < 48:
            def evac(tg, acc, akey, fo=fo):
                o, ok = tk.t_ob()
                ra, rk = rs(tg)
                tk.silu_gate(acc, akey, ra, rk, o[:], ok)
                cx.dma(GT[(fo - 36) * 128:(fo - 35) * 128, tg * 512:(tg + 1) * 512], o[:], reads=[ok])
            tk.proj_fm(wb, wkey, 16, xb, "xb", evac)
        elif fo < 52:
            def evac(tg, acc, akey, fo=fo):
                y, yk = tk.t_tmp()
                ra, rk = rs(tg)
                p.add("dve", lambda e: e.tensor_tensor(y[:], acc[:], ra, ALU.mult), reads=[akey, rk], writes=[yk])
                tk.head_norm(y, yk, 128, gq[:, 0:1], "g_mem_q0", qmn[:, fo - 48, tg * 512:(tg + 1) * 512], ("qmn", fo - 48))
            tk.proj_fm(wb, wkey, 16, xb, "xb", evac)
        else:
            def evac(tg, acc, akey, fo=fo):
                ra, rk = rs(tg)
                tk.silu_gate(acc, akey, ra, rk, sgm[:, fo - 52, tg * 512:(tg + 1) * 512], ("sgm", fo - 52))
            tk.proj_fm(wb, wkey, 16, xb, "xb", evac)

    tk.flush()
    mem_attention(tk, memT, gmem, "mem_norm0", Wkv, gq, gk[:, 0:1], "g_mem_k0", qmn, "qmn", sgm, "sgm", MmT,
                  (memb, rsm_b, rsm_tm, mkn, mvb, mst))


def phase_out(cx, tk, io, T):
    p = cx.p
    OT, GT, MmT, XR, W, Y = io["OT"], io["GT"], io["MmT"], io["XR"], io["W"], io["Y"]
    mixb = cx.sb([128, 16, T], BF16)
    HW_ = min(T, 1024)
    ost = [cx.sb([128, HW_], BF16) for _ in range(2)]
    gst = [cx.sb([128, HW_], BF16) for _ in range(2)]
    xrb = [cx.sb([128, 512], F32) for _ in range(8)]
    n = 0
    for k in range(12):
        for hh in range(T // HW_):
            i = n % 2
            n += 1
            cs = slice(hh * HW_, (hh + 1) * HW_)
            cx.dma(ost[i][:], OT[k % 3][(k // 3) * 128:(k // 3 + 1) * 128, cs], reads=[("XO", k % 3)], writes=[("ost", i)])
            cx.dma(gst[i][:], GT[k * 128:(k + 1) * 128, cs], writes=[("gst", i)])
            eng = "dve"
            p.add(eng, lambda e, i=i, k=k, cs=cs: e.tensor_tensor(mixb[:, k, cs], ost[i][:], gst[i][:], ALU.mult),
                  reads=[("ost", i), ("gst", i)], writes=[("mixb", k)])
    for k in range(12, 16):
        cx.dma(mixb[:, k, :], MmT[(k - 12) * 128:(k - 11) * 128, :], writes=[("mixb", k)])
    m = 0
    tk.plan([(W, fo * 128, 16, 128) for fo in range(16)])
    for fo in range(16):
        wb, wkey = tk.weights(W, 0, fo, 16)
        for tg_ in range(T // 512):
            i_ = (fo * (T // 512) + tg_) % 8
            cx.dma(xrb[i_][:], XR[fo * 128:(fo + 1) * 128, tg_ * 512:(tg_ + 1) * 512], writes=[("xrb", i_)])

        def evac(tg, acc, akey, fo=fo):
            i = (fo * (T // 512) + tg) % 8
            cs = slice(tg * 512, (tg + 1) * 512)
            y, yk = tk.t_tmp()
            p.add("dve", lambda e: e.tensor_tensor(y[:], acc[:], xrb[i][:], ALU.add), reads=[akey, ("xrb", i)], writes=[yk])
            cx.dma(Y[fo * 128:(fo + 1) * 128, cs], y[:], reads=[yk], is_out=io.get("final", False))
        tk.proj_fm(wb, wkey, 16, mixb, "mixb", evac)


def rope_tables(cx, tk, pos, invf, cosF, sinS, T):
    p = cx.p
    pi_ = cx.sb([64, 512], I32)
    ki = cx.sb([64, 512], I32)
    TWO_PI = float(2 * np.pi)
    for c in range(T // 512):
        cs = slice(c * 512, (c + 1) * 512)
        cx.dma(pi_[:], pos[cs].partition_broadcast(64), writes=["pi_"])
        ang, ak = tk.t_tmp()
        p.add("dve", lambda e, ang=ang: e.tensor_copy(ang[0:64, :], pi_[:]), reads=["pi_"], writes=[ak])
        p.add("dve", lambda e, ang=ang: e.tensor_scalar(ang[0:64, :], ang[0:64, :], invf[:, 0:1], None, ALU.mult),
              reads=[ak, "invf"], writes=[ak])
        for dst, dkey, ph in ((sinS, "sinS", 0.5), (cosF, "cosF", 0.75)):
            u, uk = tk.t_tmp()
            kf, kk = tk.t_tmp()
            p.add("dve", lambda e, u=u, ang=ang, ph=ph: e.tensor_scalar(u[0:64, :], ang[0:64, :], 1.0 / TWO_PI, ph, ALU.mult, ALU.add),
                  reads=[ak], writes=[uk])
            p.add("dve", lambda e, u=u: e.tensor_copy(ki[:], u[0:64, :]), reads=[uk], writes=["ki"])
            p.add("dve", lambda e, kf=kf: e.tensor_copy(kf[0:64, :], ki[:]), reads=["ki"], writes=[kk])
            p.add("dve", lambda e, u=u, kf=kf: e.tensor_tensor(u[0:64, :], u[0:64, :], kf[0:64, :], ALU.subtract),
                  reads=[uk, kk], writes=[uk])
            p.add("dve", lambda e, u=u, kf=kf: e.tensor_scalar(kf[0:64, :], u[0:64, :], 0.0, None, ALU.is_lt),
                  reads=[uk], writes=[kk])
            p.add("dve", lambda e, u=u, kf=kf: e.tensor_tensor(u[0:64, :], u[0:64, :], kf[0:64, :], ALU.add),
                  reads=[uk, kk], writes=[uk])
            p.add("dve", lambda e, u=u: e.tensor_scalar(u[0:64, :], u[0:64, :], TWO_PI, float(-np.pi), ALU.mult, ALU.add),
                  reads=[uk], writes=[uk])
            p.add("dve", lambda e, u=u: e.tensor_scalar(u[0:64, :], u[0:64, :], float(np.pi), float(-np.pi), ALU.min, ALU.max),
                  reads=[uk], writes=[uk])
            cx.act(dst[:, cs], u[0:64, :], AF.Sin, reads=[uk], writes=[(dkey, c)])
        p.add("dve", lambda e, cs=cs: e.tensor_scalar(sinS[0:32, cs], sinS[0:32, cs], -1.0, None, ALU.mult),
              reads=[("sinS", c)], writes=[("sinS", c)])


def phase_lb1(cx, tk, io, T, C, t_off):
    p = cx.p
    D = 2048
    XI = io["XI3"]
    RB = (512, 448, 448)
    TL_ = 2048

    def col(s_, tg):
        c0 = s_ * TL_ + t_off + tg * 512
        return slice(c0, c0 + 512)
    xb = cx.sb([128, 16, T], BF16)
    rstd1 = cx.sb([128, T], F32)
    rstd2 = cx.sb([128, T], F32)
    rtm2 = cx.sb([128, 16], F32)
    lat = cx.sb([128, 4, T], BF16)
    qmn = cx.sb([128, 4, T], BF16)
    sgm = cx.sb([128, 4, T], BF16)
    cosF = cx.sb([64, T], F32)
    sinS = cx.sb([64, T], F32)
    stage = [cx.sb([128, T], F32) for _ in range(2)]
    vst = [cx.sb([128, 4, 128], BF16) for _ in range(2)]
    memb = cx.sb([128, 16, 256], BF16)
    rsm_b = cx.sb([128, 256], F32)
    rsm_tm = cx.sb([128, 2], F32)
    mkn = cx.sb([128, 4, 256], BF16)
    mvb = cx.sb([128, 2, 512], BF16)
    mst = [cx.sb([128, 256], F32) for _ in range(2)]
    NTG = T // 512

    HO = [0, 3, 6, 9, 1, 4, 7, 10, 2, 5, 8, 11]
    pl = [(io["b_w_in"], fo * 128, 16, 128) for fo in range(4)]
    for h in HO:
        pl += [(io["b_w_uq"], h * 192, 4, 128), (io["b_w_uq"], h * 192 + 128, 4, 64)]
    pl += [(io["w_dkv"], j * 128, 16, 128) for j in range(4)] + [(io["w_dkv"], 512, 16, 64)]
    for h in HO:
        pl += [(io["w_ukv"], h * 256, 4, 128), (io["w_ukv"], h * 256 + 128, 4, 128)]
    pl += [(io["b_w_in"], fo * 128, 16, 128) for fo in range(4, 24)]
    pl += [(io["w_mem_kv"], h * 128, 16, 128) for h in range(4)] + [(io["w_mem_kv"], 512 + h * 128, 16, 128) for h in range(4)]
    tk.plan(pl)
    rope_tables(cx, tk, io["pos"], C["invf"], cosF, sinS, T)
    tk.load_x(io["X1T"], 16, xb, "xb", rstd1, "rstd1", D, stage)

    def rs(buf, key, tg):
        return buf[:, tg * 512:(tg + 1) * 512], (key, tg)

    def rope_apply(y, yk, tg, dsts):
        cs = slice(tg * 512, (tg + 1) * 512)
        sw, swk = tk.rp[tk.nrp % len(tk.rp)]
        tk.nrp += 1
        cx.mm(sw[0:64, :], C["PERM"][:, :], y[0:64, :], True, True, reads=["PERM", yk], writes=[swk])
        t1, k1 = tk.t_tmp()
        t2, k2 = tk.t_tmp()
        p.add("dve", lambda e: e.tensor_tensor(t1[0:64, :], y[0:64, :], cosF[:, cs], ALU.mult), reads=[yk, ("cosF", tg)], writes=[k1])
        yield
        p.add("dve", lambda e: e.tensor_tensor(t2[0:64, :], sw[0:64, :], sinS[:, cs], ALU.mult), reads=[swk, ("sinS", tg)], writes=[k2])
        o, ok = tk.t_ob()
        p.add("dve", lambda e: e.tensor_tensor(o[0:64, :], t1[0:64, :], t2[0:64, :], ALU.add), reads=[k1, k2], writes=[ok])
        yield
        for dst in dsts:
            tk.xwrite(o[0:64, :], ok, 64, dst)

    def lat_evac(rbuf, rkey, j):
        def evac(tg, acc, akey):
            y, yk = tk.t_tmp()
            ra, rk = rs(rbuf, rkey, tg)
            p.add("dve", lambda e: e.tensor_tensor(y[:], acc[:], ra, ALU.mult), reads=[akey, rk], writes=[yk])
            cx.act(lat[:, j, tg * 512:(tg + 1) * 512], y[:], AF.Copy, reads=[yk], writes=[("lat", j)])
            sq, sk = tk.t_sq()
            cx.act(sq[:], y[:], AF.Square, reads=[yk], writes=[sk])
            cx.mm(tk.ss[tg][:], tk.ON[:], sq[:], j == 0, j == 3, reads=["ON", sk], writes=[("ss", tg)])
        return evac

    def fin_lat_norm():
        tk.flush()
        for tg in range(NTG):
            tk.rstd_from(rstd2[:, tg * 512:(tg + 1) * 512], tk.ss[tg][:], 512, [("ss", tg)], [("rstd2", tg)])

    gB = (C["g_b"], "g_b")
    for j in range(4):
        wb, wkey = tk.weights(io["b_w_in"], 0, j, 16, gain=gB)
        tk.proj_fm(wb, wkey, 16, xb, "xb", lat_evac(rstd1, "rstd1", j))
    fin_lat_norm()
    gQ = (C["g_q_lat"], "g_q_lat")
    for h in HO:
        wb, wkey = tk.weights(io["b_w_uq"], h * 192, 0, 4, gain=gQ)

        def evac_qn(tg, acc, akey, h=h):
            y, yk = tk.t_tmp()
            ra, rk = rs(rstd2, "rstd2", tg)
            p.add("dve", lambda e: e.tensor_tensor(y[:], acc[:], ra, ALU.mult), reads=[akey, rk], writes=[yk])
            o, ok = tk.t_ob()
            yield
            yield from tk.head_norm_g(y, yk, 128, C["g_q_nope"][:, 0:1], "g_q_nope", o[:], ok)
            yield
            r0 = (h // 3) * RB[h % 3]
            tk.xwrite(o[:], ok, 128, lambda s_: XI[h % 3][s_][r0:r0 + 128, t_off + tg * 512:t_off + (tg + 1) * 512])
        tk.proj_fm(wb, wkey, 4, lat, "lat", evac_qn)
        wb, wkey = tk.weights(io["b_w_uq"], h * 192 + 128, 0, 4, gain=gQ, width=64)

        def evac_qr(tg, acc, akey, h=h):
            y, yk = tk.t_tmp()
            ra, rk = rs(rstd2, "rstd2", tg)
            p.add("dve", lambda e: e.tensor_tensor(y[0:64, :], acc[0:64, :], ra[0:64, :], ALU.mult), reads=[akey, rk], writes=[yk])
            yn, ynk = tk.t_tmp()
            yield
            yield from tk.head_norm_g(y, yk, 64, C["g_q_rope"][0:64, 0:1], "g_q_rope", yn[0:64, :], ynk)
            yield
            r0 = (h // 3) * RB[h % 3] + 384
            yield from rope_apply(yn, ynk, tg, [lambda s_: XI[h % 3][s_][r0:r0 + 64, t_off + tg * 512:t_off + (tg + 1) * 512]])
        tk.proj_fm(wb, wkey, 4, lat, "lat", evac_qr, width=64)

    for j in range(4):
        wb, wkey = tk.weights(io["w_dkv"], 0, j, 16, gain=(C["g_kv"], "g_kv"))
        tk.proj_fm(wb, wkey, 16, xb, "xb", lat_evac(rstd1, "rstd1", j))
    wb, wkey = tk.weights(io["w_dkv"], 512, 0, 16, gain=(C["g_kv"], "g_kv"), width=64)

    def evac_kr(tg, acc, akey):
        y, yk = tk.t_tmp()
        ra, rk = rs(rstd1, "rstd1", tg)
        p.add("dve", lambda e: e.tensor_tensor(y[0:64, :], acc[0:64, :], ra[0:64, :], ALU.mult), reads=[akey, rk], writes=[yk])
        yn, ynk = tk.t_tmp()
        yield
        yield from tk.head_norm_g(y, yk, 64, C["g_k_rope"][0:64, 0:1], "g_k_rope", yn[0:64, :], ynk)
        yield
        yield from rope_apply(yn, ynk, tg, [lambda s_, d=d: XI[0][s_][d * 512 + 448:d * 512 + 512, t_off + tg * 512:t_off + (tg + 1) * 512] for d in range(4)])
    tk.proj_fm(wb, wkey, 16, xb, "xb", evac_kr, width=64)
    fin_lat_norm()
    tk.rstd_tm(rstd2, "rstd2", rtm2, "rtm2")

    for n_, h in enumerate(HO):
        if n_ in (4, 8) and io.get("cc"):
            io["cc"](n_ // 4 - 1, list(tk.xops))
        wb, wkey = tk.weights(io["w_ukv"], h * 256, 0, 4, gain=(C["g_ckv"], "g_ckv"))

        def evac_kn(tg, acc, akey, h=h):
            y, yk = tk.t_tmp()
            ra, rk = rs(rstd2, "rstd2", tg)
            p.add("dve", lambda e: e.tensor_tensor(y[:], acc[:], ra, ALU.mult), reads=[akey, rk], writes=[yk])
            o, ok = tk.t_ob()
            yield
            yield from tk.head_norm_g(y, yk, 128, C["g_k_nope"][:, 0:1], "g_k_nope", o[:], ok)
            yield
            r0 = (h // 3) * RB[h % 3] + 128
            tk.xwrite(o[:], ok, 128, lambda s_: XI[h % 3][s_][r0:r0 + 128, t_off + tg * 512:t_off + (tg + 1) * 512])
        tk.proj_fm(wb, wkey, 4, lat, "lat", evac_kn)
        wb, wkey = tk.weights(io["w_ukv"], h * 256 + 128, 0, 4, gain=(C["g_ckv"], "g_ckv"))

        def evac_v(g, acc, akey, h=h):
            i = g % 2
            v = vst[i]
            p.add("dve", lambda e: e.tensor_tensor(
                v[:], acc[:].rearrange("p (j f) -> p j f", j=4),
                rtm2[:, g * 4:(g + 1) * 4].unsqueeze(2).to_broadcast([128, 4, 128]), ALU.mult),
                reads=[akey, "rtm2"], writes=[("vst", i)])
            r0 = (h // 3) * RB[h % 3] + 256
            kb0 = t_off // 128 + g * 4
            tk.xwrite(v[:].rearrange("p j f -> p (j f)"), ("vst", i), 128, lambda s_: XI[h % 3][s_][r0:r0 + 128, kb0 * 128:(kb0 + 4) * 128])
        tk.proj_tm(wb, wkey, 4, lat, "lat", evac_v)

    if io.get("cc"):
        io["cc"](2, list(tk.xops))

    gB = (C["g_b"], "g_b")
    for fo in range(4, 24):
        wb, wkey = tk.weights(io["b_w_in"], 0, fo, 16, gain=gB)
        if fo < 16:
            def evac(tg, acc, akey, fo=fo):
                o, ok = tk.t_ob()
                ra, rk = rs(rstd1, "rstd1", tg)
                tk.silu_gate(acc, akey, ra, rk, o[:], ok)
                cx.dma(io["G2T"][(fo - 4) * 128:(fo - 3) * 128, tg * 512:(tg + 1) * 512], o[:], reads=[ok])
        elif fo < 20:
            def evac(tg, acc, akey, fo=fo):
                y, yk = tk.t_tmp()
                ra, rk = rs(rstd1, "rstd1", tg)
                p.add("dve", lambda e: e.tensor_tensor(y[:], acc[:], ra, ALU.mult), reads=[akey, rk], writes=[yk])
                tk.head_norm(y, yk, 128, C["g_q"][:, 0:1], "g_q", qmn[:, fo - 16, tg * 512:(tg + 1) * 512], ("qmn", fo - 16))
        else:
            def evac(tg, acc, akey, fo=fo):
                ra, rk = rs(rstd1, "rstd1", tg)
                tk.silu_gate(acc, akey, ra, rk, sgm[:, fo - 20, tg * 512:(tg + 1) * 512], ("sgm", fo - 20))
        tk.proj_fm(wb, wkey, 16, xb, "xb", evac)

    tk.flush()
    mem_attention(tk, io["memT"], C["g_mem"], "g_mem", io["w_mem_kv"], C["g_q"], C["g_k"][:, 0:1], "g_k", qmn, "qmn", sgm, "sgm",
                  io["Mm2T"], (memb, rsm_b, rsm_tm, mkn, mvb, mst))


def host_consts():
    c = const_mats()
    i = np.arange(64) % 32
    c["invf"] = (10000.0 ** (-(2.0 * i) / 64.0)).astype(np.float32).reshape(64, 1)
    k = np.arange(64)[:, None]
    m = np.arange(64)[None, :]
    c["PERM"] = (k == (m + 32) % 64).astype(np.float32)
    return c


TL, SEQ, BATCH, NCORE = 2048, 8192, 2, 8
NR1, NR3, NRO = 1152, 1408, 1536
GROUPS = [[0, 1, 2, 3], [4, 5, 6, 7]]

W_SHAPES = dict(a_w_in=[2048, 7168], a_w_out=[2048, 2048], w_dkv=[2048, 576], w_ukv=[512, 3072], b_w_in=[2048, 3072],
                b_w_uq=[512, 2304], b_w_out=[2048, 2048], w_mem_kv0=[2048, 1024], w_mem_kv1=[2048, 1024])
SMALL_F32 = dict(a_norm=[128, 16], mem_norm0=[128, 16], g_mem=[128, 16], g_mem_q0=[128, 1], g_mem_k0=[128, 1], g_q=[128, 1],
                 g_k=[128, 1], g_kv=[128, 16], g_b=[128, 16], g_ckv=[128, 4], g_q_lat=[128, 4], g_k_nope=[128, 1],
                 g_k_rope=[128, 1], g_q_nope=[128, 1], g_q_rope=[128, 1], invf=[64, 1], PERM=[64, 64], sel=[128, 4],
                 MS=[128, 128], MI=[128, 128])
SMALL_BF = dict(ONES=[128, 128], U=[128, 128], Uc=[128, 128], Z=[128, 128])


def attn_out_writer(cx, C, XIs, XOs, nqg):
    p = cx.p
    mk = [cx.sb([128, 4, 512], BF16) for _ in range(3)]
    st = {"n": 0, "ops": {}}

    def out(h, qg, O, okey):
        j = qg // 4
        cs = slice((qg % 4) * 512, (qg % 4 + 1) * 512)
        i = st["n"] % 3
        st["n"] += 1
        m = mk[i]
        for s_ in range(4):
            p.add("dve", lambda e, s_=s_: e.tensor_scalar(m[:, s_, :], O[:], C["sel"][:, s_:s_ + 1], None, ALU.mult),
                  reads=[okey, "sel"], writes=[("omk", i)])
        dst = XIs[h][j * 512:(j + 1) * 512, cs].rearrange("(s p) c -> p s c", p=128)
        st["ops"].setdefault(h, []).append(cx.dma(dst, m[:], reads=[("omk", i)]))
        if qg == nqg - 1:
            xi, xo = XIs[h], XOs[h]
            p.add_cc(lambda e: e.collective_compute("ReduceScatter", ALU.add, replica_groups=GROUPS, ins=[xi], outs=[xo]),
                     writes=[("XO", h)], deps=st["ops"][h])
    return out


def build_fused():
    cx = Ctx()
    p, nc = cx.p, cx.nc
    ext = {}
    ext["xT"] = cx.dram("xT", [2048, TL], F32, "ExternalInput")
    ext["memT"] = cx.dram("memT", [2048, 256], F32, "ExternalInput")
    ext["pos"] = cx.dram("pos", [TL], I32, "ExternalInput")
    for nm, shp in W_SHAPES.items():
        ext[nm] = cx.dram(nm, shp, F32, "ExternalInput")
    Y = cx.dram("Y", [2048, TL], F32, "ExternalOutput")
    C = {}
    for nm, shp in SMALL_F32.items():
        d = cx.dram(nm, shp, F32, "ExternalInput")
        t = cx.gsb(shp, F32)
        cx.dma(t[:], d, writes=[nm])
        C[nm] = t
    for nm, shp in SMALL_BF.items():
        d = cx.dram(nm, shp, BF16, "ExternalInput")
        t = cx.gsb(shp, BF16)
        cx.dma(t[:], d, writes=[nm])
        C[nm] = t
    C["ON"] = C["ONES"]
    p.last_w["ON"] = p.last_w["ONES"]
    C["MSb"] = cx.gsb([128, 128], BF16)
    C["MIb"] = cx.gsb([128, 128], BF16)
    C["ZR"] = cx.gsb([128, 512], BF16)
    C["one"] = cx.gsb([1, 1], F32)
    p.add("dve", lambda e: e.tensor_copy(C["MSb"][:], C["MS"][:]), reads=["MS"], writes=["MSb"])
    p.add("dve", lambda e: e.tensor_copy(C["MIb"][:], C["MI"][:]), reads=["MI"], writes=["MIb"])
    p.add("pool", lambda e: e.memset(C["ZR"][:], 0.0), writes=["ZR"])
    p.add("pool", lambda e: e.memset(C["one"][:], 1.0), writes=["one"])

    def idram(name, shape, dt):
        return nc.dram_tensor(name, list(shape), dt).ap()

    RB = (512, 448, 448)
    XI1 = [[idram("XI1_%d_%d" % (r, s_), [4 * 384, TL], BF16) for s_ in range(4)] for r in range(3)]
    XO1 = [[idram("XO1_%d_%d" % (r, s_), [384, TL], BF16) for s_ in range(4)] for r in range(3)]
    XI2 = [idram("XI2_%d" % r, [4 * 512, TL], BF16) for r in range(3)]
    XO2 = [idram("XO2_%d" % r, [512, TL], BF16) for r in range(3)]
    XI3 = [[idram("XI3_%d_%d" % (r, s_), [4 * RB[r], TL], BF16) for s_ in range(4)] for r in range(3)]
    XO3 = [[idram("XO3_%d_%d" % (r, s_), [RB[r], TL], BF16) for s_ in range(4)] for r in range(3)]
    XI4 = [idram("XI4_%d" % r, [4 * 512, TL], BF16) for r in range(3)]
    XO4 = [idram("XO4_%d" % r, [512, TL], BF16) for r in range(3)]
    GT, MmT = idram("GT", [1536, TL], BF16), idram("MmT", [512, TL], BF16)
    G2T, Mm2T = idram("G2T", [1536, TL], BF16), idram("Mm2T", [512, TL], BF16)
    X1T = idram("X1T", [2048, TL], F32)

    def xchg3(xis, xos):
        prev = []
        for r in range(3):
            for s_ in range(4):
                xi, xo = xis[r][s_], xos[r][s_]
                op = p.add_cc(lambda e, xi=xi, xo=xo: e.collective_compute("ReduceScatter", ALU.add, replica_groups=GROUPS,
                                                                            ins=[xi], outs=[xo]),
                              writes=[("XO", r, s_)], deps=prev)
                prev = [op]

    def mkcc(xis, xos):
        def cc(r, deps):
            xi, xo = xis[r], xos[r]
            p.add_cc(lambda e: e.collective_compute("ReduceScatter", ALU.add, replica_groups=GROUPS, ins=[xi], outs=[xo]),
                     writes=[("XO", r)], deps=deps)
        return cc

    cx.begin_phase()
    tk = Tok(cx, TL, C)
    phase_la1(cx, tk, dict(xT=ext["xT"], a_w_in=ext["a_w_in"], memT=ext["memT"], w_mem_kv0=ext["w_mem_kv0"],
                           XI1=XI1, GT=GT, MmT=MmT), C, TL)
    cx.end_phase()
    xchg3(XI1, XO1)
    cx.begin_phase()
    io = dict(QT=lambda h, sl: XO1[h][sl][0:128, :], KT=lambda h, sl: XO1[h][sl][128:256, :],
              V=lambda h, sl: XO1[h][sl][256:384, :].rearrange("p (k d) -> p k d", d=128),
              out=attn_out_writer(cx, C, XI2, XO2, SEQ // 512))
    phase_sb(cx, io, C, 3, SEQ, 128 ** -0.5)
    cx.end_phase()
    cx.begin_phase()
    tk = Tok(cx, TL, C)
    phase_out(cx, tk, dict(OT=XO2, GT=GT, MmT=MmT, XR=ext["xT"], W=ext["a_w_out"], Y=X1T), TL)
    cx.end_phase()
    TH = 1024
    for hh in range(TL // TH):
        cs = slice(hh * TH, (hh + 1) * TH)
        cx.begin_phase()
        tk = Tok(cx, TH, C)
        phase_lb1(cx, tk, dict(X1T=X1T[:, cs], pos=ext["pos"][cs], w_dkv=ext["w_dkv"], w_ukv=ext["w_ukv"],
                               b_w_in=ext["b_w_in"], b_w_uq=ext["b_w_uq"], memT=ext["memT"], w_mem_kv=ext["w_mem_kv1"],
                               G2T=G2T[:, cs], Mm2T=Mm2T[:, cs], XI3=XI3), TH, C, hh * TH)
        cx.end_phase()
    xchg3(XI3, XO3)
    cx.begin_phase()
    io = dict(QnT=lambda h, sl: XO3[h][sl][0:128, :], KnT=lambda h, sl: XO3[h][sl][128:256, :],
              V=lambda h, sl: XO3[h][sl][256:384, :].rearrange("p (k d) -> p k d", d=128),
              QrT=lambda h, sl: XO3[h][sl][384:448, :], KrT=lambda sl: XO3[0][sl][448:512, :],
              out=attn_out_writer(cx, C, XI4, XO4, SEQ // 512))
    phase_mla(cx, io, C, 3, SEQ, 192 ** -0.5)
    cx.end_phase()
    cx.begin_phase()
    tk = Tok(cx, TL, C)
    phase_out(cx, tk, dict(OT=XO4, GT=G2T, MmT=Mm2T, XR=X1T, W=ext["b_w_out"], Y=Y, final=True), TL)
    cx.end_phase()
    return cx.finish()


def _g16(g):
    return np.ascontiguousarray(np.asarray(g, np.float32).reshape(-1, 128).T)


def _col(g):
    g = np.asarray(g, np.float32)
    o = np.zeros((128, 1), np.float32)
    o[:len(g), 0] = g
    return o


def make_inputs(x, mem, positions, a_norm, a_w_in, a_w_out, kv_norm, w_dkv, g_ckv, w_ukv,
                g_k_nope, g_k_rope, b_norm, b_w_in, b_g_q_lat, b_w_uq, b_g_q_nope, b_g_q_rope,
                b_w_out, mem_norm, w_mem_kv, g_mem_q, g_mem_k):
    f32 = lambda a: np.ascontiguousarray(np.asarray(a, np.float32))
    x, mem = f32(x), f32(mem)
    positions = np.asarray(positions, np.int32)
    c = host_consts()
    shared = dict(a_w_in=f32(a_w_in[0]), a_w_out=f32(a_w_out[0]), w_dkv=f32(w_dkv), w_ukv=f32(w_ukv), b_w_in=f32(b_w_in[0]),
                  b_w_uq=f32(b_w_uq[0]), b_w_out=f32(b_w_out[0]), w_mem_kv0=f32(w_mem_kv[0]), w_mem_kv1=f32(w_mem_kv[1]),
                  a_norm=_g16(a_norm[0]), mem_norm0=_g16(mem_norm[0]), g_mem=_g16(mem_norm[1]), g_mem_q0=_col(g_mem_q[0]),
                  g_mem_k0=_col(g_mem_k[0]), g_q=_col(g_mem_q[1]), g_k=_col(g_mem_k[1]), g_kv=_g16(kv_norm), g_b=_g16(b_norm[0]),
                  g_ckv=_g16(g_ckv), g_q_lat=_g16(b_g_q_lat[0]), g_k_nope=_col(g_k_nope), g_k_rope=_col(g_k_rope),
                  g_q_nope=_col(b_g_q_nope[0]), g_q_rope=_col(b_g_q_rope[0]), invf=c["invf"], PERM=c["PERM"],
                  MS=c["MS"], MI=c["MI"], ONES=c["ONES"], U=c["U"], Uc=c["Uc"], Z=c["Z"])
    ims = []
    for b in range(BATCH):
        memT = np.ascontiguousarray(mem[b].T)
        for j in range(4):
            sel = np.zeros((128, 4), np.float32)
            sel[:, j] = 1.0
            d = dict(shared)
            d.update(xT=np.ascontiguousarray(x[b, j * TL:(j + 1) * TL].T), memT=memT,
                     pos=np.ascontiguousarray(positions[b, j * TL:(j + 1) * TL]), sel=sel)
            ims.append(d)
    return ims


_NC_CACHE = {}


def kernel(**inputs):
    ims = make_inputs(**inputs)
    if "nc" not in _NC_CACHE:
        _NC_CACHE["nc"] = build_fused()
    res = run_spmd(_NC_CACHE["nc"], ims)
    out = np.empty((BATCH, SEQ, 2048), np.float32)
    for i in range(NCORE):
        b, j = divmod(i, 4)
        out[b, j * TL:(j + 1) * TL] = res[i]["Y"].T
    return out
```

```python
import contextlib
import numpy as np
import ml_dtypes
import concourse.bass as bass
import concourse.mybir as mybir
from concourse.bass_utils import run_bass_kernel_spmd

F32 = mybir.dt.float32
BF16 = mybir.dt.bfloat16
I32 = mybir.dt.int32
AF = mybir.ActivationFunctionType
ALU = mybir.AluOpType
NPBF = ml_dtypes.bfloat16

ENGS = ("pe", "act", "dve", "pool", "sp")
NDSEM = 8
import os
SAFE_SAME_ENGINE = os.environ.get("UNSAFE_SE") is None


class Op:
    __slots__ = ("eng", "fn", "dma", "deps", "needed", "sig", "idx", "cc")

    def __init__(self, eng, fn, dma):
        self.eng, self.fn, self.dma = eng, fn, dma
        self.deps = []
        self.needed = False
        self.sig = None
        self.cc = False


class Prog:
    def __init__(self, nc):
        self.nc = nc
        self.ops = {e: [] for e in ENGS}
        self.last_w = {}
        self.readers = {}
        self.dmas = []
        self.out_dmas = []
        self.ccs = []
        self.bar_idx = 0

    def barrier(self, wait_cc=True):
        deps = []
        for e in ENGS:
            for op in reversed(self.ops[e]):
                if not op.dma and op.fn is not None:
                    deps.append(op)
                    break
        deps.extend(self.dmas[self.bar_idx:])
        self.bar_idx = len(self.dmas)
        if wait_cc:
            deps.extend(self.ccs)
        for d in deps:
            d.needed = True
        for e in ENGS:
            op = Op(e, None, False)
            op.deps = [d for d in deps if d.dma or d.cc or d.eng != e]
            self.ops[e].append(op)

    def add_cc(self, fn, reads=(), writes=(), deps=()):
        op = self.add("pool", fn, reads, writes, dma=True)
        self.dmas.pop()
        op.cc = True
        for d in deps:
            d.needed = True
            op.deps.append(d)
        if op.deps and op.deps[-1].dma and len(self.dmas) >= NDSEM and op.deps[-1] is self.dmas[len(self.dmas) - NDSEM]:
            pass
        self.ccs.append(op)
        return op

    def _need(self, d, op):
        if d is op:
            return False
        if d.dma or op.dma:
            return True
        if d.eng == op.eng:
            if d.eng == "pe":
                return False
            return SAFE_SAME_ENGINE
        return True

    def add(self, eng, fn, reads=(), writes=(), dma=False, out=False):
        op = Op(eng, fn, dma)
        deps = []
        for k in reads:
            w = self.last_w.get(k)
            if w is not None:
                deps.append(w)
        for k in writes:
            w = self.last_w.get(k)
            if w is not None:
                deps.append(w)
            deps.extend(self.readers.get(k, ()))
        seen = set()
        for d in deps:
            if id(d) in seen or not self._need(d, op):
                continue
            seen.add(id(d))
            d.needed = True
            op.deps.append(d)
        for k in writes:
            self.last_w[k] = op
            self.readers[k] = []
        for k in reads:
            lst = self.readers.setdefault(k, [])
            if not dma:
                lst[:] = [r for r in lst if r.dma or r.eng != eng]
            lst.append(op)
        if dma:
            n = len(self.dmas)
            if n >= NDSEM:
                op.deps.append(self.dmas[n - NDSEM])
            self.dmas.append(op)
            op.needed = True
            if out:
                self.out_dmas.append(op)
        self.ops[eng].append(op)
        return op

    def emit(self):
        nc = self.nc
        fin = Op("sp", None, False)
        fin.deps = list(self.out_dmas)
        self.ops["sp"].append(fin)
        with contextlib.ExitStack() as es:
            esem = {e: es.enter_context(nc.semaphore("s_" + e)) for e in ENGS}
            dsem = [es.enter_context(nc.semaphore("d_%d" % i)) for i in range(NDSEM)]
            for n, op in enumerate(self.dmas):
                op.sig = (dsem[n % NDSEM], 16 * (n // NDSEM + 1))
            for n, op in enumerate(self.ccs):
                op.sig = (es.enter_context(nc.semaphore("cc_%d" % n)), 1)
            for e in ENGS:
                c = 0
                for op in self.ops[e]:
                    if op.dma or op.cc:
                        continue
                    if op.needed:
                        c += 1
                        op.sig = (esem[e], c)
            block = es.enter_context(nc.Block())

            def run(engname):
                def body(eng):
                    waited = {}
                    for op in self.ops[engname]:
                        need = {}
                        for d in op.deps:
                            s, v = d.sig
                            if waited.get(s.num, 0) < v and need.get(s.num, (None, 0))[1] < v:
                                need[s.num] = (s, v)
                        for s, v in need.values():
                            eng.wait_ge(s, v)
                            waited[s.num] = v
                        if op.fn is None:
                            continue
                        ins = op.fn(eng)
                        if op.cc:
                            ins.then_inc(op.sig[0])
                        elif op.needed and ins is not None:
                            ins.then_inc(op.sig[0], 16 if op.dma else 1)
                return body

            block.tensor(run("pe"))
            block.scalar(run("act"))
            block.vector(run("dve"))
            block.gpsimd(run("pool"))
            block.sync(run("sp"))


class Ctx:
    def __init__(self, name="k"):
        self.nc = bass.Bass("TRN2", target_bir_lowering=False)
        self.es = contextlib.ExitStack()
        self.p = Prog(self.nc)
        self.n = 0
        self.ph = None
        self.banks = [self.es.enter_context(self.nc.psum_tensor("bank%d" % i, [128, 512], F32)) for i in range(8)]
        self.nps = 0

    def begin_phase(self):
        self.ph = contextlib.ExitStack()
        self.nps = 0

    def end_phase(self, wait_cc=True):
        self.p.barrier(wait_cc)
        self.ph.close()
        self.ph = None

    def gsb(self, shape, dt):
        self.n += 1
        return self.es.enter_context(self.nc.sbuf_tensor("gsb%d" % self.n, list(shape), dt))

    def dram(self, name, shape, dt, kind):
        return self.nc.dram_tensor(name, list(shape), dt, kind=kind).ap()

    def sb(self, shape, dt, name=None):
        self.n += 1
        st = self.ph if self.ph is not None else self.es
        return st.enter_context(self.nc.sbuf_tensor(name or "sb%d" % self.n, list(shape), dt))

    def ps(self, name=None):
        b = self.banks[self.nps]
        self.nps += 1
        return b

    def dma(self, out, in_, reads=(), writes=(), is_out=False):
        return self.p.add("sp", lambda e: e.dma_start(out=out, in_=in_), reads, writes, dma=True, out=is_out)

    def mm(self, out, lhsT, rhs, start, stop, reads=(), writes=()):
        return self.p.add("pe", lambda e: e.matmul(out, lhsT, rhs, start=start, stop=stop, skip_group_check=True),
                          reads, writes)

    def act(self, out, in_, func, reads=(), writes=(), scale=1.0, bias=0.0):
        return self.p.add("act", lambda e: e.activation(out, in_, func, bias=bias, scale=scale), reads, writes)

    def finish(self):
        self.p.emit()
        self.es.close()
        return self.nc


def run_spmd(nc, in_maps):
    if os.environ.get("KTRACE"):
        res = run_bass_kernel_spmd(nc, in_maps, core_ids=list(range(len(in_maps))), trace=True)
        print("EXEC_TIME_NS", res.exec_time_ns)
        return res.results
    res = run_bass_kernel_spmd(nc, in_maps, core_ids=list(range(len(in_maps))))
    return res.results


def const_mats():
    j = np.arange(128)[:, None]
    s = np.arange(128)[None, :]
    c = {}
    c["U"] = np.where(j >= s, -1.0, 0.0).astype(NPBF)
    c["Uc"] = np.where(j < s, -1.0, 0.0).astype(NPBF)
    c["Z"] = np.zeros((128, 128), NPBF)
    c["ONES"] = np.ones((128, 128), NPBF)
    c["MS"] = (s > j).astype(np.float32)
    c["MI"] = (s >= j).astype(np.float32)
    return c


def phase_sb(cx, io, C, NH, S, scale):
    nc, p = cx.nc, cx.p
    QT, KT, V = io["QT"], io["KT"], io["V"]
    NB = 2
    qt = [cx.sb([128, S], BF16) for _ in range(NB)]
    kt = [cx.sb([128, S], BF16) for _ in range(NB)]
    vv = [cx.sb([128, S // 128, 128], BF16) for _ in range(NB)]
    U, Uc, Z, MSb, ZR = C["U"], C["Uc"], C["Z"], C["MSb"], C["ZR"]
    NE = 3
    Eb = [cx.sb([128, 512], F32) for _ in range(NE)]
    Lb = [cx.sb([128, 512], BF16) for _ in range(NE)]
    Xb = [cx.sb([128, 512], F32) for _ in range(NE)]
    Ab = [cx.sb([128, 512], BF16) for _ in range(NE)]
    Ob = [cx.sb([128, 512], F32) for _ in range(2)]
    zb = [cx.ps() for _ in range(2)]
    runb = [cx.ps() for _ in range(2)]
    outb = [cx.ps() for _ in range(2)]
    dmy = cx.ps()
    ND = int(os.environ.get("SB_DUMMY", "3"))

    def warm(k):
        for _ in range(k):
            cx.mm(dmy[:, :], Z[:], ZR[:], True, True, reads=["Z", "ZR"])

    tiles = []
    nqg = S // 512
    for h in range(NH):
        for qg in range(nqg):
            t0 = qg * 512
            kb_hi = (t0 + 512) // 128 - 1
            for kb in range(kb_hi, -1, -1):
                off = max(0, kb * 128 - t0)
                tiles.append(dict(h=h, qg=qg, kb=kb, off=off, diag=(kb * 128 >= t0),
                                  first=(kb == kb_hi), last=(kb == 0), sid=h * nqg + qg))
    n = len(tiles)
    loaded = set()

    SL = S // 4
    KBS = SL // 128

    def load_slot(h, sl):
        if (h, sl) in loaded or h >= NH or sl >= 4:
            return
        loaded.add((h, sl))
        b = h % NB
        cs = slice(sl * SL, (sl + 1) * SL)
        cx.dma(qt[b][:, cs], QT(h, sl), reads=[("XO", h, sl)], writes=[("qt", b, sl)])
        cx.dma(kt[b][:, cs], KT(h, sl), reads=[("XO", h, sl)], writes=[("kt", b, sl)])
        cx.dma(vv[b][:, sl * KBS:(sl + 1) * KBS, :], V(h, sl), reads=[("XO", h, sl)], writes=[("vv", b, sl)])

    def prefetch(t):
        h, qg = t["h"], t["qg"]
        load_slot(h, qg * 512 // SL)
        for nq in (qg + 1, qg + 2):
            if nq < nqg:
                load_slot(h, nq * 512 // SL)
            else:
                load_slot(h + 1, 0)

    def s1a(i):
        t = tiles[i]
        hb = t["h"] % NB
        if t["first"]:
            prefetch(t)
        c0, c1 = t["off"], 512
        q0 = t["qg"] * 512
        z = zb[i % 2]
        cx.mm(z[:, c0:c1], kt[hb][:, t["kb"] * 128:(t["kb"] + 1) * 128], qt[hb][:, q0 + c0:q0 + c1],
              True, True, reads=[("kt", hb, t["kb"] // KBS), ("qt", hb, q0 // SL)], writes=[("z", i % 2)])

    def s1b(i):
        t = tiles[i]
        c0, c1 = t["off"], 512
        cx.act(Eb[i % NE][:, c0:c1], zb[i % 2][:, c0:c1], AF.Exp, scale=scale,
               reads=[("z", i % 2)], writes=[("E", i % NE)])

    def s1c(i):
        t = tiles[i]
        c0, c1 = t["off"], 512
        cx.act(Lb[i % NE][:, c0:c1], Eb[i % NE][:, c0:c1], AF.Ln, bias=1.0,
               reads=[("E", i % NE)], writes=[("L", i % NE)])
        if t["diag"]:
            L = Lb[i % NE]
            p.add("dve", lambda e: e.tensor_tensor(L[:, c0:c0 + 128], L[:, c0:c0 + 128], MSb[:], ALU.mult),
                  reads=[("L", i % NE), "MSb"], writes=[("L", i % NE)])

    def s2(i):
        t = tiles[i]
        c0, c1 = t["off"], 512
        sp = t["sid"] % 2
        hb = t["h"] % NB
        rb, ob = runb[sp], outb[sp]
        L = Lb[i % NE]
        if t["first"]:
            cx.mm(rb[:, :], Z[:], ZR[:], True, False, reads=["Z", "ZR"], writes=[("run", sp)])
            cx.mm(ob[:, :], Z[:], ZR[:], True, False, reads=["Z", "ZR"], writes=[("out", sp)])
        cx.mm(rb[:, c0:c1], U[:], L[:, c0:c1], False, False, reads=["U", ("L", i % NE)], writes=[("run", sp)])
        return t, c0, c1, sp, hb, rb, ob, L

    def s2x(i):
        t = tiles[i]
        c0, c1 = t["off"], 512
        sp = t["sid"] % 2
        cx.act(Xb[i % NE][:, c0:c1], runb[sp][:, c0:c1], AF.Exp,
               reads=[("run", sp)], writes=[("X", i % NE)])

    def s2uc(i):
        t = tiles[i]
        c0, c1 = t["off"], 512
        sp = t["sid"] % 2
        rb = runb[sp]
        L, E, X, A = Lb[i % NE], Eb[i % NE], Xb[i % NE], Ab[i % NE]
        if not t["last"]:
            cx.mm(rb[:, c0:c1], Uc[:], L[:, c0:c1], False, False, reads=["Uc", ("L", i % NE)], writes=[("run", sp)])
        p.add("dve", lambda e: e.tensor_tensor(A[:, c0:c1], E[:, c0:c1], X[:, c0:c1], ALU.mult),
              reads=[("E", i % NE), ("X", i % NE)], writes=[("A", i % NE)])
        if t["diag"]:
            p.add("dve", lambda e: e.tensor_tensor(A[:, c0:c0 + 128], A[:, c0:c0 + 128], MSb[:], ALU.mult),
                  reads=[("A", i % NE), "MSb"], writes=[("A", i % NE)])

    def s2av(i):
        t = tiles[i]
        c0, c1 = t["off"], 512
        sp = t["sid"] % 2
        hb = t["h"] % NB
        ob = outb[sp]
        A = Ab[i % NE]
        cx.mm(ob[:, c0:c1], vv[hb][:, t["kb"], :], A[:, c0:c1], False, t["last"],
              reads=[("vv", hb, t["kb"] // KBS), ("A", i % NE)], writes=[("out", sp)])
        if t["last"]:
            O = Ob[sp]
            p.add("dve", lambda e: e.tensor_copy(O[:], ob[:]), reads=[("out", sp)], writes=[("O", sp)])
            io["out"](t["h"], t["qg"], O, ("O", sp))

    s1a(0)
    if n > 1:
        s1a(1)
    s1b(0)
    s1c(0)
    for i in range(n):
        s2(i)
        if i + 2 < n:
            s1a(i + 2)
        warm(ND)
        if i + 1 < n:
            s1b(i + 1)
        s2x(i)
        if i + 1 < n:
            s1c(i + 1)
        s2uc(i)
        if i >= 1:
            s2av(i - 1)
    s2av(n - 1)


def phase_mla(cx, io, C, NH, S, scale):
    nc, p = cx.nc, cx.p
    QnT, QrT, KnT, KrT, V = io["QnT"], io["QrT"], io["KnT"], io["KrT"], io["V"]
    NB = 2
    qn = [cx.sb([128, S], BF16) for _ in range(NB)]
    qr = [cx.sb([64, S], BF16) for _ in range(NB)]
    kn = [cx.sb([128, S], BF16) for _ in range(NB)]
    kr = cx.sb([64, S], BF16)
    vv = [cx.sb([128, S // 128, 128], BF16) for _ in range(NB)]
    ON, Z, MIb, ZR = C["ON"], C["Z"], C["MIb"], C["ZR"]
    NE = 3
    Pb = [cx.sb([128, 512], BF16) for _ in range(NE)]
    Rb = [cx.sb([128, 512], F32) for _ in range(2)]
    Ob = [cx.sb([128, 512], F32) for _ in range(2)]
    zb = [cx.ps() for _ in range(2)]
    denb = [cx.ps() for _ in range(2)]
    outb = [cx.ps() for _ in range(2)]
    dmy = cx.ps()
    ND = int(os.environ.get("MLA_DUMMY", "0"))

    def warm(k):
        for _ in range(k):
            cx.mm(dmy[:, :], Z[:], ZR[:], True, True, reads=["Z", "ZR"])
    SL = S // 4
    KBS = SL // 128

    tiles = []
    nqg = S // 512
    for h in range(NH):
        for qg in range(nqg):
            t0 = qg * 512
            kb_hi = (t0 + 512) // 128 - 1
            for kb in range(kb_hi, -1, -1):
                off = max(0, kb * 128 - t0)
                tiles.append(dict(h=h, qg=qg, kb=kb, off=off, diag=(kb * 128 >= t0),
                                  first=(kb == kb_hi), last=(kb == 0), sid=h * nqg + qg))
    n = len(tiles)
    loaded = set()

    def load_slot(h, sl):
        if (h, sl) in loaded or h >= NH or sl >= 4:
            return
        loaded.add((h, sl))
        b = h % NB
        cs = slice(sl * SL, (sl + 1) * SL)
        if h == 0:
            cx.dma(kr[:, cs], KrT(sl), reads=[("XO", 0, sl)], writes=[("kr", sl)])
        cx.dma(qn[b][:, cs], QnT(h, sl), reads=[("XO", h, sl)], writes=[("qn", b, sl)])
        cx.dma(qr[b][:, cs], QrT(h, sl), reads=[("XO", h, sl)], writes=[("qr", b, sl)])
        cx.dma(kn[b][:, cs], KnT(h, sl), reads=[("XO", h, sl)], writes=[("kn", b, sl)])
        cx.dma(vv[b][:, sl * KBS:(sl + 1) * KBS, :], V(h, sl), reads=[("XO", h, sl)], writes=[("vv", b, sl)])

    def prefetch(t):
        h, qg = t["h"], t["qg"]
        load_slot(h, qg * 512 // SL)
        for nq in (qg + 1, qg + 2):
            if nq < nqg:
                load_slot(h, nq * 512 // SL)
            else:
                load_slot(h + 1, 0)

    def sA(i):
        t = tiles[i]
        hb = t["h"] % NB
        if t["first"]:
            prefetch(t)
        c0, c1 = t["off"], 512
        q0 = t["qg"] * 512
        k0 = t["kb"] * 128
        z = zb[i % 2]
        ks, qs = t["kb"] // KBS, q0 // SL
        cx.mm(z[:, c0:c1], kn[hb][:, k0:k0 + 128], qn[hb][:, q0 + c0:q0 + c1], True, False,
              reads=[("kn", hb, ks), ("qn", hb, qs)], writes=[("z", i % 2)])
        cx.mm(z[:, c0:c1], kr[:, k0:k0 + 128], qr[hb][:, q0 + c0:q0 + c1], False, True,
              reads=[("kr", ks), ("qr", hb, qs)], writes=[("z", i % 2)])

    def sB(i):
        t = tiles[i]
        c0, c1 = t["off"], 512
        P = Pb[i % NE]
        cx.act(P[:, c0:c1], zb[i % 2][:, c0:c1], AF.Exp, scale=scale,
               reads=[("z", i % 2)], writes=[("P", i % NE)])
        if t["diag"]:
            p.add("dve", lambda e: e.tensor_tensor(P[:, c0:c0 + 128], P[:, c0:c0 + 128], MIb[:], ALU.mult),
                  reads=[("P", i % NE), "MIb"], writes=[("P", i % NE)])

    def sC(i):
        t = tiles[i]
        c0, c1 = t["off"], 512
        sp = t["sid"] % 2
        hb = t["h"] % NB
        db, ob = denb[sp], outb[sp]
        P = Pb[i % NE]
        if t["first"]:
            cx.mm(db[:, :], Z[:], ZR[:], True, False, reads=["Z", "ZR"], writes=[("den", sp)])
            cx.mm(ob[:, :], Z[:], ZR[:], True, False, reads=["Z", "ZR"], writes=[("out", sp)])
        cx.mm(ob[:, c0:c1], vv[hb][:, t["kb"], :], P[:, c0:c1], False, t["last"],
              reads=[("vv", hb, t["kb"] // KBS), ("P", i % NE)], writes=[("out", sp)])
        cx.mm(db[:, c0:c1], ON[:], P[:, c0:c1], False, t["last"],
              reads=["ON", ("P", i % NE)], writes=[("den", sp)])
        if t["last"]:
            O, R = Ob[sp], Rb[sp]
            p.add("dve", lambda e: e.reciprocal(R[:], db[:]), reads=[("den", sp)], writes=[("R", sp)])
            p.add("dve", lambda e: e.tensor_tensor(O[:], ob[:], R[:], ALU.mult),
                  reads=[("out", sp), ("R", sp)], writes=[("O", sp)])
            io["out"](t["h"], t["qg"], O, ("O", sp))

    sA(0)
    if n > 1:
        sA(1)
    sB(0)
    for i in range(n):
        if i + 2 < n:
            sA(i + 2)
        warm(ND)
        if i + 1 < n:
            sB(i + 1)
        sC(i)


EPS = 1e-6
NWB = 3


def x3(X, r0, P, c0):
    return X[r0:r0 + P, :].rearrange("p (s c) -> p s c", s=4)[:, :, c0:c0 + 512]


class Tok:
    def __init__(self, cx, T, C):
        self.cx, self.p, self.T = cx, cx.p, T
        self.NTG = T // 512
        self.C = C
        self.ON = C["ON"]
        self.one = C["one"]
        self.mk = [cx.sb([128, 4, 512], BF16) for _ in range(3)]
        self.nmk = 0
        self.xops = []
        self.acc = [cx.ps() for _ in range(2)]
        self.ss = [cx.ps() for _ in range(4)]
        self.aux = [cx.ps() for _ in range(2)]
        deep = self.NTG <= 2
        self.hn = [(self.aux[1], ("aux", 1))]
        self.rp = [(self.aux[0], ("aux", 0))]
        if deep:
            self.hn.append((self.ss[2], ("ss", 2)))
            self.rp.append((self.ss[3], ("ss", 3)))
        self.nhn = self.nrp = 0
        self.NTMP, self.NSQ, self.NOB = (12, 4, 6) if deep else (4, 2, 3)
        self.delay = False
        self.rr_block = deep
        self.pend_ev = []
        self.wst = [cx.sb([128, 16, 128], F32) for _ in range(NWB)]
        self.wbf = [cx.sb([128, 16, 128], BF16) for _ in range(NWB)]
        self.tmp = [cx.sb([128, 512], F32) for _ in range(self.NTMP)]
        self.sq = [cx.sb([128, 512], BF16) for _ in range(self.NSQ)]
        self.ob = [cx.sb([128, 512], BF16) for _ in range(self.NOB)]
        self.nw = self.nt = self.nsq = self.nob = self.nacc = 0

    def xwrite(self, src_ap, skey, P, dst3):
        cx, p = self.cx, self.p
        sel = self.C["sel"]
        i = self.nmk % 3
        self.nmk += 1
        m = self.mk[i]
        for s_ in range(4):
            if s_ % 2 == 0:
                p.add("dve", lambda e, s_=s_: e.tensor_scalar(m[0:P, s_, :], src_ap, sel[0:P, s_:s_ + 1], None, ALU.mult),
                      reads=[skey, "sel"], writes=[("mk", i)])
            else:
                p.add("act", lambda e, s_=s_: e.mul(m[0:P, s_, :], src_ap, sel[0:P, s_:s_ + 1]),
                      reads=[skey, "sel"], writes=[("mk", i)])
        for s_ in range(4):
            self.xops.append(cx.dma(dst3(s_), m[0:P, s_, :], reads=[("mk", i)]))

    def t_tmp(self):
        i = self.nt % self.NTMP
        self.nt += 1
        return self.tmp[i], ("tmp", i)

    def t_sq(self):
        i = self.nsq % self.NSQ
        self.nsq += 1
        return self.sq[i], ("sq", i)

    def t_ob(self):
        i = self.nob % self.NOB
        self.nob += 1
        return self.ob[i], ("ob", i)

    def t_acc(self):
        i = self.nacc % len(self.acc)
        self.nacc += 1
        return self.acc[i], ("acc", i)

    def rstd_from(self, out_ap, ss_ap, C, rkeys, wkeys):
        cx = self.cx
        cx.act(out_ap, ss_ap, AF.Ln, scale=1.0 / C, bias=EPS, reads=rkeys, writes=wkeys)
        cx.act(out_ap, out_ap, AF.Exp, scale=-0.5, reads=wkeys, writes=wkeys)

    def load_x(self, src, nk, xb, xkey, rstd_b=None, rkey=None, C=None, stage=None):
        cx, p, T = self.cx, self.p, self.T
        self._wload(NWB)
        for k in range(nk):
            st = stage[k % 2]
            cx.dma(st[:], src[k * 128:(k + 1) * 128, :], writes=[("xst", k % 2)])
            p.add("act", lambda e, k=k, st=st: e.copy(xb[:, k, :], st[:]),
                  reads=[("xst", k % 2)], writes=[(xkey, k)])
            if rstd_b is not None:
                for tg in range(self.NTG):
                    sq, sk = self.t_sq()
                    cx.act(sq[:], st[:, tg * 512:(tg + 1) * 512], AF.Square, reads=[("xst", k % 2)], writes=[sk])
                    cx.mm(self.ss[tg][:], self.ON[:], sq[:], k == 0, k == nk - 1,
                          reads=["ON", sk], writes=[("ss", tg)])
        if rstd_b is not None:
            for tg in range(self.NTG):
                self.rstd_from(rstd_b[:, tg * 512:(tg + 1) * 512], self.ss[tg][:], C, [("ss", tg)], [(rkey, tg)])

    def rstd_tm(self, rstd_b, rkey, rtm, tkey):
        cx, p = self.cx, self.p
        ntb = self.T // 128
        a = self.aux[0]
        for tb in range(ntb):
            cx.mm(a[:, tb:tb + 1], rstd_b[0:1, tb * 128:(tb + 1) * 128], self.one[0:1, 0:1], True, True,
                  reads=[(rkey, tb // 4), "one"], writes=[("aux", 0)])
        p.add("dve", lambda e: e.tensor_copy(rtm[:, 0:ntb], a[:, 0:ntb]), reads=[("aux", 0)], writes=[tkey])

    def plan(self, specs):
        self.wplan = list(specs)
        self.wi = 0
        self.wl = 0

    def _wload(self, upto):
        cx = self.cx
        while self.wl < min(upto, len(self.wplan)):
            W, c0, nk, width = self.wplan[self.wl]
            i = self.wl % NWB
            cx.dma(self.wst[i][:, 0:nk, 0:width], W[0:nk * 128, c0:c0 + width].rearrange("(k p) f -> p k f", p=128),
                   writes=[("wst", i)])
            self.wl += 1

    def weights(self, W, f0, fo, nk, gain=None, width=128):
        cx, p = self.cx, self.p
        spec = self.wplan[self.wi]
        assert spec[0] is W and spec[1] == f0 + fo * 128 and spec[2] == nk and spec[3] == width, (spec[1:], f0, fo, nk, width)
        self._wload(self.wi + NWB)
        i = self.wi % NWB
        self.wi += 1
        wf, wb = self.wst[i], self.wbf[i]
        if gain is not None:
            gap, gkey = gain
            p.add("pool", lambda e: e.tensor_tensor(wb[:, 0:nk, 0:width], wf[:, 0:nk, 0:width],
                                                     gap[:, 0:nk].unsqueeze(2).to_broadcast([128, nk, width]), ALU.mult),
                  reads=[("wst", i), gkey], writes=[("wbf", i)])
        else:
            p.add("pool", lambda e: e.tensor_copy(wb[:, 0:nk, 0:width], wf[:, 0:nk, 0:width]),
                  reads=[("wst", i)], writes=[("wbf", i)])
        return wb, ("wbf", i)

    @staticmethod
    def _run_rr(items):
        gens = []
        for f, a in items:
            r = f(*a)
            if r is not None and hasattr(r, "__next__"):
                gens.append(r)
        while gens:
            for g in list(gens):
                try:
                    next(g)
                except StopIteration:
                    gens.remove(g)

    def flush(self):
        ev, self.pend_ev = self.pend_ev, []
        self._run_rr(ev)

    def _evacs(self, items):
        if self.delay:
            prev, self.pend_ev = self.pend_ev, items
            self._run_rr(prev)
        else:
            self._run_rr(items)

    def proj_fm(self, wb, wkey, nk, xb, xkey, evac, width=128):
        cx = self.cx
        items = []
        for tg in range(self.NTG):
            acc, akey = self.t_acc()
            for k in range(nk):
                cx.mm(acc[0:width, :], wb[:, k, 0:width], xb[:, k, tg * 512:(tg + 1) * 512], k == 0, k == nk - 1,
                      reads=[wkey, (xkey, k)], writes=[akey])
            if self.rr_block:
                items.append((evac, (tg, acc, akey)))
            else:
                self._run_rr([(evac, (tg, acc, akey))])
        if self.rr_block:
            self._run_rr(items)

    def proj_tm(self, wb, wkey, nk, xb, xkey, evac, width=128):
        cx = self.cx
        ntb = self.T // 128
        items = []
        for g in range(ntb // 4):
            acc, akey = self.t_acc()
            for j in range(4):
                tb = g * 4 + j
                for k in range(nk):
                    cx.mm(acc[:, j * 128:j * 128 + width], xb[:, k, tb * 128:(tb + 1) * 128], wb[:, k, 0:width],
                          (j == 0 and k == 0), k == nk - 1, reads=[wkey, (xkey, k)], writes=[akey])
            if self.rr_block:
                items.append((evac, (g, acc, akey)))
            else:
                self._run_rr([(evac, (g, acc, akey))])
        if self.rr_block:
            self._run_rr(items)

    def head_norm_g(self, y, ykey, D, gcol, gkey, out_ap, okey):
        cx, p = self.cx, self.p
        sq, sk = self.t_sq()
        cx.act(sq[0:D, :], y[0:D, :], AF.Square, reads=[ykey], writes=[sk])
        a, ak = self.hn[self.nhn % len(self.hn)]
        self.nhn += 1
        cx.mm(a[0:D, :], self.ON[0:D, 0:D], sq[0:D, :], True, True, reads=["ON", sk], writes=[ak])
        yield
        r, rk = self.t_tmp()
        self.rstd_from(r[0:D, :], a[0:D, :], D, [ak], [rk])
        yield
        p.add("dve", lambda e: e.scalar_tensor_tensor(out_ap, y[0:D, :], gcol, r[0:D, :], ALU.mult, ALU.mult),
              reads=[ykey, rk, gkey], writes=[okey])

    def head_norm(self, *a):
        for _ in self.head_norm_g(*a):
            pass

    def silu_gate(self, acc, akey, rstd_ap, rkey, out_ap, okey):
        cx, p = self.cx, self.p
        g, gk = self.t_tmp()
        p.add("dve", lambda e: e.tensor_tensor(g[:], acc[:], rstd_ap, ALU.mult), reads=[akey, rkey], writes=[gk])
        cx.act(out_ap, g[:], AF.Silu, reads=[gk], writes=[okey])


def load_small(cx, dst, src, key):
    cx.dma(dst, src, writes=[key])


def mem_attention(tk, memT, gmem, gmem_key, Wkv, gq, gk, gk_key, qmn, qkey, sgm, sgkey, MmT_out, bufs):
    cx, p, T = tk.cx, tk.p, tk.T
    memb, rsm_b, rsm_tm, mkn, mvb, mst = bufs
    ML = 256
    for k in range(16):
        st = mst[k % 2]
        cx.dma(st[:], memT[k * 128:(k + 1) * 128, :], writes=[("mst", k % 2)])
        p.add("act", lambda e, k=k, st=st: e.copy(memb[:, k, :], st[:]), reads=[("mst", k % 2)], writes=[("memb", k)])
        sq, sk = tk.t_sq()
        cx.act(sq[:, 0:ML], st[:], AF.Square, reads=[("mst", k % 2)], writes=[sk])
        cx.mm(tk.ss[0][:, 0:ML], tk.ON[:], sq[:, 0:ML], k == 0, k == 15, reads=["ON", sk], writes=[("ss", 0)])
    tk.rstd_from(rsm_b[:], tk.ss[0][:, 0:ML], 2048, [("ss", 0)], ["rsm_b"])
    a = tk.aux[0]
    for mb in range(2):
        cx.mm(a[:, mb:mb + 1], rsm_b[0:1, mb * 128:(mb + 1) * 128], tk.one[0:1, 0:1], True, True,
              reads=["rsm_b", "one"], writes=[("aux", 0)])
    p.add("dve", lambda e: e.tensor_copy(rsm_tm[:], a[:, 0:2]), reads=[("aux", 0)], writes=["rsm_tm"])
    for h in range(4):
        wb, wkey = tk.weights(Wkv, 0, h, 16, gain=(gmem, gmem_key))
        acc, akey = tk.t_acc()
        for k in range(16):
            cx.mm(acc[:, 0:ML], wb[:, k, :], memb[:, k, :], k == 0, k == 15, reads=[wkey, ("memb", k)], writes=[akey])
        y, yk = tk.t_tmp()
        p.add("dve", lambda e, y=y, acc=acc: e.tensor_tensor(y[:, 0:ML], acc[:, 0:ML], rsm_b[:], ALU.mult),
              reads=[akey, "rsm_b"], writes=[yk])
        sq, sk = tk.t_sq()
        cx.act(sq[:, 0:ML], y[:, 0:ML], AF.Square, reads=[yk], writes=[sk])
        a1 = tk.aux[1]
        cx.mm(a1[:, 0:ML], tk.ON[:], sq[:, 0:ML], True, True, reads=["ON", sk], writes=[("aux", 1)])
        r, rk = tk.t_tmp()
        tk.rstd_from(r[:, 0:ML], a1[:, 0:ML], 128, [("aux", 1)], [rk])
        p.add("dve", lambda e, y=y, r=r, h=h: e.scalar_tensor_tensor(mkn[:, h, :], y[:, 0:ML], gk, r[:, 0:ML], ALU.mult, ALU.mult),
              reads=[yk, rk, gk_key], writes=[("mkn", h)])
    for h in range(4):
        wb, wkey = tk.weights(Wkv, 512, h, 16, gain=(gmem, gmem_key))
        acc, akey = tk.t_acc()
        for mb in range(2):
            for k in range(16):
                cx.mm(acc[:, mb * 128:(mb + 1) * 128], memb[:, k, mb * 128:(mb + 1) * 128], wb[:, k, :],
                      (mb == 0 and k == 0), k == 15, reads=[wkey, ("memb", k)], writes=[akey])
        p.add("dve", lambda e, acc=acc, h=h: e.tensor_tensor(
            mvb[:, :, h * 128:(h + 1) * 128], acc[:, 0:256].rearrange("p (m f) -> p m f", m=2),
            rsm_tm[:].unsqueeze(2).to_broadcast([128, 2, 128]), ALU.mult), reads=[akey, "rsm_tm"], writes=[("mvb", h)])
    sc = 128 ** -0.5
    for h in range(4):
        for tg in range(tk.NTG):
            cols = slice(tg * 512, (tg + 1) * 512)
            Ps = []
            for mb in range(2):
                acc, akey = tk.t_acc()
                cx.mm(acc[:], mkn[:, h, mb * 128:(mb + 1) * 128], qmn[:, h, cols], True, True,
                      reads=[("mkn", h), (qkey, h)], writes=[akey])
                P, pk = tk.t_ob()
                cx.act(P[:], acc[:], AF.Exp, scale=sc, reads=[akey], writes=[pk])
                Ps.append((P, pk))
            mo, den = tk.aux[0], tk.aux[1]
            for mb in range(2):
                P, pk = Ps[mb]
                cx.mm(mo[:], mvb[:, mb, h * 128:(h + 1) * 128], P[:], mb == 0, mb == 1, reads=[("mvb", h), pk], writes=[("aux", 0)])
            for mb in range(2):
                P, pk = Ps[mb]
                cx.mm(den[:], tk.ON[:], P[:], mb == 0, mb == 1, reads=["ON", pk], writes=[("aux", 1)])
            r, rk = tk.t_tmp()
            p.add("dve", lambda e, r=r, den=den: e.reciprocal(r[:], den[:]), reads=[("aux", 1)], writes=[rk])
            p.add("dve", lambda e, r=r, mo=mo: e.tensor_tensor(r[:], mo[:], r[:], ALU.mult), reads=[("aux", 0), rk], writes=[rk])
            o, ok = tk.t_ob()
            p.add("dve", lambda e, r=r, o=o, h=h, cols=cols: e.tensor_tensor(o[:], r[:], sgm[:, h, cols], ALU.mult),
                  reads=[rk, (sgkey, h)], writes=[ok])
            cx.dma(MmT_out[h * 128:(h + 1) * 128, cols], o[:], reads=[ok])


def phase_la1(cx, tk, io, C, T):
    p = cx.p
    D = 2048
    xT, W, memT, Wkv = io["xT"], io["a_w_in"], io["memT"], io["w_mem_kv0"]
    XI, GT, MmT = io["XI1"], io["GT"], io["MmT"]
    NR = 1152
    xb = cx.sb([128, 16, T], BF16)
    rstd_b = cx.sb([128, T], F32)
    rtm = cx.sb([128, 16], F32)
    stage = [cx.sb([128, T], F32) for _ in range(2)]
    gin, gmem, gq, gk = C["a_norm"], C["mem_norm0"], C["g_mem_q0"], C["g_mem_k0"]
    qmn = cx.sb([128, 4, T], BF16)
    sgm = cx.sb([128, 4, T], BF16)
    memb = cx.sb([128, 16, 256], BF16)
    rsm_b = cx.sb([128, 256], F32)
    rsm_tm = cx.sb([128, 2], F32)
    mkn = cx.sb([128, 4, 256], BF16)
    mvb = cx.sb([128, 2, 512], BF16)
    mst = [cx.sb([128, 256], F32) for _ in range(2)]
    vst = [cx.sb([128, 4, 128], BF16) for _ in range(2)]

    HO = [0, 3, 6, 9, 1, 4, 7, 10, 2, 5, 8, 11]
    order = []
    for r in range(3):
        hs = HO[4 * r:4 * r + 4]
        order += [hh for hh in hs] + [12 + hh for hh in hs] + [24 + hh for hh in hs]
    order += list(range(36, 56))
    tk.plan([(W, fo * 128, 16, 128) for fo in order] + [(Wkv, h * 128, 16, 128) for h in range(4)] + [(Wkv, 512 + h * 128, 16, 128) for h in range(4)])
    tk.load_x(xT, 16, xb, "xb", rstd_b, "rstd", D, stage)
    tk.rstd_tm(rstd_b, "rstd", rtm, "rtm")

    def rs(tg):
        return rstd_b[:, tg * 512:(tg + 1) * 512], ("rstd", tg)

    for n_, fo in enumerate(order):
        if n_ in (12, 24, 36) and io.get("cc"):
            io["cc"](n_ // 12 - 1, list(tk.xops))
        wb, wkey = tk.weights(W, 0, fo, 16, gain=(gin, "a_norm"))
        if fo < 24:
            def evac(tg, acc, akey, fo=fo):
                o, ok = tk.t_ob()
                ra, rk = rs(tg)
                p.add("dve", lambda e: e.tensor_tensor(o[:], acc[:], ra, ALU.mult), reads=[akey, rk], writes=[ok])
                hh = fo % 12
                r0 = (hh // 3) * 384 + (128 if fo >= 12 else 0)
                X_ = XI[hh % 3]
                tk.xwrite(o[:], ok, 128, lambda s_: X_[s_][r0:r0 + 128, tg * 512:(tg + 1) * 512])
            tk.proj_fm(wb, wkey, 16, xb, "xb", evac)
        elif fo < 36:
            def evac(g, acc, akey, fo=fo):
                i = g % 2
                v = vst[i]
                p.add("dve", lambda e: e.tensor_tensor(
                    v[:], acc[:].rearrange("p (j f) -> p j f", j=4),
                    rtm[:, g * 4:(g + 1) * 4].unsqueeze(2).to_broadcast([128, 4, 128]), ALU.mult),
                    reads=[akey, "rtm"], writes=[("vst", i)])
                hh = fo - 24
                r0 = (hh // 3) * 384 + 256
                X_ = XI[hh % 3]
                tk.xwrite(v[:].rearrange("p j f -> p (j f)"), ("vst", i), 128, lambda s_: X_[s_][r0:r0 + 128, g * 512:(g + 1) * 512])
            tk.proj_tm(wb, wkey, 16, xb, "xb", evac)
        elif fo < 48:
            def evac(tg, acc, akey, fo=fo):
                o, ok = tk.t_ob()
                ra, rk = rs(tg)
                tk.silu_gate(acc, akey, ra, rk, o[:], ok)
                cx.dma(GT[(fo - 36) * 128:(fo - 35) * 128, tg * 512:(tg + 1) * 512], o[:], reads=[ok])
            tk.proj_fm(wb, wkey, 16, xb, "xb", evac)
        elif fo < 52:
            def evac(tg, acc, akey, fo=fo):
                y, yk = tk.t_tmp()
                ra, rk = rs(tg)
                p.add("dve", lambda e: e.tensor_tensor(y[:], acc[:], ra, ALU.mult), reads=[akey, rk], writes=[yk])
                tk.head_norm(y, yk, 128, gq[:, 0:1], "g_mem_q0", qmn[:, fo - 48, tg * 512:(tg + 1) * 512], ("qmn", fo - 48))
            tk.proj_fm(wb, wkey, 16, xb, "xb", evac)
        else:
            def evac(tg, acc, akey, fo=fo):
                ra, rk = rs(tg)
                tk.silu_gate(acc, akey, ra, rk, sgm[:, fo - 52, tg * 512:(tg + 1) * 512], ("sgm", fo - 52))
            tk.proj_fm(wb, wkey, 16, xb, "xb", evac)

    tk.flush()
    mem_attention(tk, memT, gmem, "mem_norm0", Wkv, gq, gk[:, 0:1], "g_mem_k0", qmn, "qmn", sgm, "sgm", MmT,
                  (memb, rsm_b, rsm_tm, mkn, mvb, mst))


def phase_out(cx, tk, io, T):
    p = cx.p
    OT, GT, MmT, XR, W, Y = io["OT"], io["GT"], io["MmT"], io["XR"], io["W"], io["Y"]
    mixb = cx.sb([128, 16, T], BF16)
    HW_ = min(T, 1024)
    ost = [cx.sb([128, HW_], BF16) for _ in range(2)]
    gst = [cx.sb([128, HW_], BF16) for _ in range(2)]
    xrb = [cx.sb([128, 512], F32) for _ in range(8)]
    n = 0
    for k in range(12):
        for hh in range(T // HW_):
            i = n % 2
            n += 1
            cs = slice(hh * HW_, (hh + 1) * HW_)
            cx.dma(ost[i][:], OT[k % 3][(k // 3) * 128:(k // 3 + 1) * 128, cs], reads=[("XO", k % 3)], writes=[("ost", i)])
            cx.dma(gst[i][:], GT[k * 128:(k + 1) * 128, cs], writes=[("gst", i)])
            eng = "dve"
            p.add(eng, lambda e, i=i, k=k, cs=cs: e.tensor_tensor(mixb[:, k, cs], ost[i][:], gst[i][:], ALU.mult),
                  reads=[("ost", i), ("gst", i)], writes=[("mixb", k)])
    for k in range(12, 16):
        cx.dma(mixb[:, k, :], MmT[(k - 12) * 128:(k - 11) * 128, :], writes=[("mixb", k)])
    m = 0
    tk.plan([(W, fo * 128, 16, 128) for fo in range(16)])
    for fo in range(16):
        wb, wkey = tk.weights(W, 0, fo, 16)
        for tg_ in range(T // 512):
            i_ = (fo * (T // 512) + tg_) % 8
            cx.dma(xrb[i_][:], XR[fo * 128:(fo + 1) * 128, tg_ * 512:(tg_ + 1) * 512], writes=[("xrb", i_)])

        def evac(tg, acc, akey, fo=fo):
            i = (fo * (T // 512) + tg) % 8
            cs = slice(tg * 512, (tg + 1) * 512)
            y, yk = tk.t_tmp()
            p.add("dve", lambda e: e.tensor_tensor(y[:], acc[:], xrb[i][:], ALU.add), reads=[akey, ("xrb", i)], writes=[yk])
            cx.dma(Y[fo * 128:(fo + 1) * 128, cs], y[:], reads=[yk], is_out=io.get("final", False))
        tk.proj_fm(wb, wkey, 16, mixb, "mixb", evac)


def rope_tables(cx, tk, pos, invf, cosF, sinS, T):
    p = cx.p
    pi_ = cx.sb([64, 512], I32)
    ki = cx.sb([64, 512], I32)
    TWO_PI = float(2 * np.pi)
    for c in range(T // 512):
        cs = slice(c * 512, (c + 1) * 512)
        cx.dma(pi_[:], pos[cs].partition_broadcast(64), writes=["pi_"])
        ang, ak = tk.t_tmp()
        p.add("dve", lambda e, ang=ang: e.tensor_copy(ang[0:64, :], pi_[:]), reads=["pi_"], writes=[ak])
        p.add("dve", lambda e, ang=ang: e.tensor_scalar(ang[0:64, :], ang[0:64, :], invf[:, 0:1], None, ALU.mult),
              reads=[ak, "invf"], writes=[ak])
        for dst, dkey, ph in ((sinS, "sinS", 0.5), (cosF, "cosF", 0.75)):
            u, uk = tk.t_tmp()
            kf, kk = tk.t_tmp()
            p.add("dve", lambda e, u=u, ang=ang, ph=ph: e.tensor_scalar(u[0:64, :], ang[0:64, :], 1.0 / TWO_PI, ph, ALU.mult, ALU.add),
                  reads=[ak], writes=[uk])
            p.add("dve", lambda e, u=u: e.tensor_copy(ki[:], u[0:64, :]), reads=[uk], writes=["ki"])
            p.add("dve", lambda e, kf=kf: e.tensor_copy(kf[0:64, :], ki[:]), reads=["ki"], writes=[kk])
            p.add("dve", lambda e, u=u, kf=kf: e.tensor_tensor(u[0:64, :], u[0:64, :], kf[0:64, :], ALU.subtract),
                  reads=[uk, kk], writes=[uk])
            p.add("dve", lambda e, u=u, kf=kf: e.tensor_scalar(kf[0:64, :], u[0:64, :], 0.0, None, ALU.is_lt),
                  reads=[uk], writes=[kk])
            p.add("dve", lambda e, u=u, kf=kf: e.tensor_tensor(u[0:64, :], u[0:64, :], kf[0:64, :], ALU.add),
                  reads=[uk, kk], writes=[uk])
            p.add("dve", lambda e, u=u: e.tensor_scalar(u[0:64, :], u[0:64, :], TWO_PI, float(-np.pi), ALU.mult, ALU.add),
                  reads=[uk], writes=[uk])
            p.add("dve", lambda e, u=u: e.tensor_scalar(u[0:64, :], u[0:64, :], float(np.pi), float(-np.pi), ALU.min, ALU.max),
                  reads=[uk], writes=[uk])
            cx.act(dst[:, cs], u[0:64, :], AF.Sin, reads=[uk], writes=[(dkey, c)])
        p.add("dve", lambda e, cs=cs: e.tensor_scalar(sinS[0:32, cs], sinS[0:32, cs], -1.0, None, ALU.mult),
              reads=[("sinS", c)], writes=[("sinS", c)])


def phase_lb1(cx, tk, io, T, C, t_off):
    p = cx.p
    D = 2048
    XI = io["XI3"]
    RB = (512, 448, 448)
    TL_ = 2048

    def col(s_, tg):
        c0 = s_ * TL_ + t_off + tg * 512
        return slice(c0, c0 + 512)
    xb = cx.sb([128, 16, T], BF16)
    rstd1 = cx.sb([128, T], F32)
    rstd2 = cx.sb([128, T], F32)
    rtm2 = cx.sb([128, 16], F32)
    lat = cx.sb([128, 4, T], BF16)
    qmn = cx.sb([128, 4, T], BF16)
    sgm = cx.sb([128, 4, T], BF16)
    cosF = cx.sb([64, T], F32)
    sinS = cx.sb([64, T], F32)
    stage = [cx.sb([128, T], F32) for _ in range(2)]
    vst = [cx.sb([128, 4, 128], BF16) for _ in range(2)]
    memb = cx.sb([128, 16, 256], BF16)
    rsm_b = cx.sb([128, 256], F32)
    rsm_tm = cx.sb([128, 2], F32)
    mkn = cx.sb([128, 4, 256], BF16)
    mvb = cx.sb([128, 2, 512], BF16)
    mst = [cx.sb([128, 256], F32) for _ in range(2)]
    NTG = T // 512

    HO = [0, 3, 6, 9, 1, 4, 7, 10, 2, 5, 8, 11]
    pl = [(io["b_w_in"], fo * 128, 16, 128) for fo in range(4)]
    for h in HO:
        pl += [(io["b_w_uq"], h * 192, 4, 128), (io["b_w_uq"], h * 192 + 128, 4, 64)]
    pl += [(io["w_dkv"], j * 128, 16, 128) for j in range(4)] + [(io["w_dkv"], 512, 16, 64)]
    for h in HO:
        pl += [(io["w_ukv"], h * 256, 4, 128), (io["w_ukv"], h * 256 + 128, 4, 128)]
    pl += [(io["b_w_in"], fo * 128, 16, 128) for fo in range(4, 24)]
    pl += [(io["w_mem_kv"], h * 128, 16, 128) for h in range(4)] + [(io["w_mem_kv"], 512 + h * 128, 16, 128) for h in range(4)]
    tk.plan(pl)
    rope_tables(cx, tk, io["pos"], C["invf"], cosF, sinS, T)
    tk.load_x(io["X1T"], 16, xb, "xb", rstd1, "rstd1", D, stage)

    def rs(buf, key, tg):
        return buf[:, tg * 512:(tg + 1) * 512], (key, tg)

    def rope_apply(y, yk, tg, dsts):
        cs = slice(tg * 512, (tg + 1) * 512)
        sw, swk = tk.rp[tk.nrp % len(tk.rp)]
        tk.nrp += 1
        cx.mm(sw[0:64, :], C["PERM"][:, :], y[0:64, :], True, True, reads=["PERM", yk], writes=[swk])
        t1, k1 = tk.t_tmp()
        t2, k2 = tk.t_tmp()
        p.add("dve", lambda e: e.tensor_tensor(t1[0:64, :], y[0:64, :], cosF[:, cs], ALU.mult), reads=[yk, ("cosF", tg)], writes=[k1])
        yield
        p.add("dve", lambda e: e.tensor_tensor(t2[0:64, :], sw[0:64, :], sinS[:, cs], ALU.mult), reads=[swk, ("sinS", tg)], writes=[k2])
        o, ok = tk.t_ob()
        p.add("dve", lambda e: e.tensor_tensor(o[0:64, :], t1[0:64, :], t2[0:64, :], ALU.add), reads=[k1, k2], writes=[ok])
        yield
        for dst in dsts:
            tk.xwrite(o[0:64, :], ok, 64, dst)

    def lat_evac(rbuf, rkey, j):
        def evac(tg, acc, akey):
            y, yk = tk.t_tmp()
            ra, rk = rs(rbuf, rkey, tg)
            p.add("dve", lambda e: e.tensor_tensor(y[:], acc[:], ra, ALU.mult), reads=[akey, rk], writes=[yk])
            cx.act(lat[:, j, tg * 512:(tg + 1) * 512], y[:], AF.Copy, reads=[yk], writes=[("lat", j)])
            sq, sk = tk.t_sq()
            cx.act(sq[:], y[:], AF.Square, reads=[yk], writes=[sk])
            cx.mm(tk.ss[tg][:], tk.ON[:], sq[:], j == 0, j == 3, reads=["ON", sk], writes=[("ss", tg)])
        return evac

    def fin_lat_norm():
        tk.flush()
        for tg in range(NTG):
            tk.rstd_from(rstd2[:, tg * 512:(tg + 1) * 512], tk.ss[tg][:], 512, [("ss", tg)], [("rstd2", tg)])

    gB = (C["g_b"], "g_b")
    for j in range(4):
        wb, wkey = tk.weights(io["b_w_in"], 0, j, 16, gain=gB)
        tk.proj_fm(wb, wkey, 16, xb, "xb", lat_evac(rstd1, "rstd1", j))
    fin_lat_norm()
    gQ = (C["g_q_lat"], "g_q_lat")
    for h in HO:
        wb, wkey = tk.weights(io["b_w_uq"], h * 192, 0, 4, gain=gQ)

        def evac_qn(tg, acc, akey, h=h):
            y, yk = tk.t_tmp()
            ra, rk = rs(rstd2, "rstd2", tg)
            p.add("dve", lambda e: e.tensor_tensor(y[:], acc[:], ra, ALU.mult), reads=[akey, rk], writes=[yk])
            o, ok = tk.t_ob()
            yield
            yield from tk.head_norm_g(y, yk, 128, C["g_q_nope"][:, 0:1], "g_q_nope", o[:], ok)
            yield
            r0 = (h // 3) * RB[h % 3]
            tk.xwrite(o[:], ok, 128, lambda s_: XI[h % 3][s_][r0:r0 + 128, t_off + tg * 512:t_off + (tg + 1) * 512])
        tk.proj_fm(wb, wkey, 4, lat, "lat", evac_qn)
        wb, wkey = tk.weights(io["b_w_uq"], h * 192 + 128, 0, 4, gain=gQ, width=64)

        def evac_qr(tg, acc, akey, h=h):
            y, yk = tk.t_tmp()
            ra, rk = rs(rstd2, "rstd2", tg)
            p.add("dve", lambda e: e.tensor_tensor(y[0:64, :], acc[0:64, :], ra[0:64, :], ALU.mult), reads=[akey, rk], writes=[yk])
            yn, ynk = tk.t_tmp()
            yield
            yield from tk.head_norm_g(y, yk, 64, C["g_q_rope"][0:64, 0:1], "g_q_rope", yn[0:64, :], ynk)
            yield
            r0 = (h // 3) * RB[h % 3] + 384
            yield from rope_apply(yn, ynk, tg, [lambda s_: XI[h % 3][s_][r0:r0 + 64, t_off + tg * 512:t_off + (tg + 1) * 512]])
        tk.proj_fm(wb, wkey, 4, lat, "lat", evac_qr, width=64)

    for j in range(4):
        wb, wkey = tk.weights(io["w_dkv"], 0, j, 16, gain=(C["g_kv"], "g_kv"))
        tk.proj_fm(wb, wkey, 16, xb, "xb", lat_evac(rstd1, "rstd1", j))
    wb, wkey = tk.weights(io["w_dkv"], 512, 0, 16, gain=(C["g_kv"], "g_kv"), width=64)

    def evac_kr(tg, acc, akey):
        y, yk = tk.t_tmp()
        ra, rk = rs(rstd1, "rstd1", tg)
        p.add("dve", lambda e: e.tensor_tensor(y[0:64, :], acc[0:64, :], ra[0:64, :], ALU.mult), reads=[akey, rk], writes=[yk])
        yn, ynk = tk.t_tmp()
        yield
        yield from tk.head_norm_g(y, yk, 64, C["g_k_rope"][0:64, 0:1], "g_k_rope", yn[0:64, :], ynk)
        yield
        yield from rope_apply(yn, ynk, tg, [lambda s_, d=d: XI[0][s_][d * 512 + 448:d * 512 + 512, t_off + tg * 512:t_off + (tg + 1) * 512] for d in range(4)])
    tk.proj_fm(wb, wkey, 16, xb, "xb", evac_kr, width=64)
    fin_lat_norm()
    tk.rstd_tm(rstd2, "rstd2", rtm2, "rtm2")

    for n_, h in enumerate(HO):
        if n_ in (4, 8) and io.get("cc"):
            io["cc"](n_ // 4 - 1, list(tk.xops))
        wb, wkey = tk.weights(io["w_ukv"], h * 256, 0, 4, gain=(C["g_ckv"], "g_ckv"))

        def evac_kn(tg, acc, akey, h=h):
            y, yk = tk.t_tmp()
            ra, rk = rs(rstd2, "rstd2", tg)
            p.add("dve", lambda e: e.tensor_tensor(y[:], acc[:], ra, ALU.mult), reads=[akey, rk], writes=[yk])
            o, ok = tk.t_ob()
            yield
            yield from tk.head_norm_g(y, yk, 128, C["g_k_nope"][:, 0:1], "g_k_nope", o[:], ok)
            yield
            r0 = (h // 3) * RB[h % 3] + 128
            tk.xwrite(o[:], ok, 128, lambda s_: XI[h % 3][s_][r0:r0 + 128, t_off + tg * 512:t_off + (tg + 1) * 512])
        tk.proj_fm(wb, wkey, 4, lat, "lat", evac_kn)
        wb, wkey = tk.weights(io["w_ukv"], h * 256 + 128, 0, 4, gain=(C["g_ckv"], "g_ckv"))

        def evac_v(g, acc, akey, h=h):
            i = g % 2
            v = vst[i]
            p.add("dve", lambda e: e.tensor_tensor(
                v[:], acc[:].rearrange("p (j f) -> p j f", j=4),
                rtm2[:, g * 4:(g + 1) * 4].unsqueeze(2).to_broadcast([128, 4, 128]), ALU.mult),
                reads=[akey, "rtm2"], writes=[("vst", i)])
            r0 = (h // 3) * RB[h % 3] + 256
            kb0 = t_off // 128 + g * 4
            tk.xwrite(v[:].rearrange("p j f -> p (j f)"), ("vst", i), 128, lambda s_: XI[h % 3][s_][r0:r0 + 128, kb0 * 128:(kb0 + 4) * 128])
        tk.proj_tm(wb, wkey, 4, lat, "lat", evac_v)

    if io.get("cc"):
        io["cc"](2, list(tk.xops))

    gB = (C["g_b"], "g_b")
    for fo in range(4, 24):
        wb, wkey = tk.weights(io["b_w_in"], 0, fo, 16, gain=gB)
        if fo < 16:
            def evac(tg, acc, akey, fo=fo):
                o, ok = tk.t_ob()
                ra, rk = rs(rstd1, "rstd1", tg)
                tk.silu_gate(acc, akey, ra, rk, o[:], ok)
                cx.dma(io["G2T"][(fo - 4) * 128:(fo - 3) * 128, tg * 512:(tg + 1) * 512], o[:], reads=[ok])
        elif fo < 20:
            def evac(tg, acc, akey, fo=fo):
                y, yk = tk.t_tmp()
                ra, rk = rs(rstd1, "rstd1", tg)
                p.add("dve", lambda e: e.tensor_tensor(y[:], acc[:], ra, ALU.mult), reads=[akey, rk], writes=[yk])
                tk.head_norm(y, yk, 128, C["g_q"][:, 0:1], "g_q", qmn[:, fo - 16, tg * 512:(tg + 1) * 512], ("qmn", fo - 16))
        else:
            def evac(tg, acc, akey, fo=fo):
                ra, rk = rs(rstd1, "rstd1", tg)
                tk.silu_gate(acc, akey, ra, rk, sgm[:, fo - 20, tg * 512:(tg + 1) * 512], ("sgm", fo - 20))
        tk.proj_fm(wb, wkey, 16, xb, "xb", evac)

    tk.flush()
    mem_attention(tk, io["memT"], C["g_mem"], "g_mem", io["w_mem_kv"], C["g_q"], C["g_k"][:, 0:1], "g_k", qmn, "qmn", sgm, "sgm",
                  io["Mm2T"], (memb, rsm_b, rsm_tm, mkn, mvb, mst))


def host_consts():
    c = const_mats()
    i = np.arange(64) % 32
    c["invf"] = (10000.0 ** (-(2.0 * i) / 64.0)).astype(np.float32).reshape(64, 1)
    k = np.arange(64)[:, None]
    m = np.arange(64)[None, :]
    c["PERM"] = (k == (m + 32) % 64).astype(np.float32)
    return c


TL, SEQ, BATCH, NCORE = 2048, 8192, 2, 8
NR1, NR3, NRO = 1152, 1408, 1536
GROUPS = [[0, 1, 2, 3], [4, 5, 6, 7]]

W_SHAPES = dict(a_w_in=[2048, 7168], a_w_out=[2048, 2048], w_dkv=[2048, 576], w_ukv=[512, 3072], b_w_in=[2048, 3072],
                b_w_uq=[512, 2304], b_w_out=[2048, 2048], w_mem_kv0=[2048, 1024], w_mem_kv1=[2048, 1024])
SMALL_F32 = dict(a_norm=[128, 16], mem_norm0=[128, 16], g_mem=[128, 16], g_mem_q0=[128, 1], g_mem_k0=[128, 1], g_q=[128, 1],
                 g_k=[128, 1], g_kv=[128, 16], g_b=[128, 16], g_ckv=[128, 4], g_q_lat=[128, 4], g_k_nope=[128, 1],
                 g_k_rope=[128, 1], g_q_nope=[128, 1], g_q_rope=[128, 1], invf=[64, 1], PERM=[64, 64], sel=[128, 4],
                 MS=[128, 128], MI=[128, 128])
SMALL_BF = dict(ONES=[128, 128], U=[128, 128], Uc=[128, 128], Z=[128, 128])


def attn_out_writer(cx, C, XIs, XOs, nqg):
    p = cx.p
    mk = [cx.sb([128, 4, 512], BF16) for _ in range(3)]
    st = {"n": 0, "ops": {}}

    def out(h, qg, O, okey):
        j = qg // 4
        cs = slice((qg % 4) * 512, (qg % 4 + 1) * 512)
        i = st["n"] % 3
        st["n"] += 1
        m = mk[i]
        for s_ in range(4):
            p.add("dve", lambda e, s_=s_: e.tensor_scalar(m[:, s_, :], O[:], C["sel"][:, s_:s_ + 1], None, ALU.mult),
                  reads=[okey, "sel"], writes=[("omk", i)])
        dst = XIs[h][j * 512:(j + 1) * 512, cs].rearrange("(s p) c -> p s c", p=128)
        st["ops"].setdefault(h, []).append(cx.dma(dst, m[:], reads=[("omk", i)]))
        if qg == nqg - 1:
            xi, xo = XIs[h], XOs[h]
            p.add_cc(lambda e: e.collective_compute("ReduceScatter", ALU.add, replica_groups=GROUPS, ins=[xi], outs=[xo]),
                     writes=[("XO", h)], deps=st["ops"][h])
    return out


def build_fused():
    cx = Ctx()
    p, nc = cx.p, cx.nc
    ext = {}
    ext["xT"] = cx.dram("xT", [2048, TL], F32, "ExternalInput")
    ext["memT"] = cx.dram("memT", [2048, 256], F32, "ExternalInput")
    ext["pos"] = cx.dram("pos", [TL], I32, "ExternalInput")
    for nm, shp in W_SHAPES.items():
        ext[nm] = cx.dram(nm, shp, F32, "ExternalInput")
    Y = cx.dram("Y", [2048, TL], F32, "ExternalOutput")
    C = {}
    for nm, shp in SMALL_F32.items():
        d = cx.dram(nm, shp, F32, "ExternalInput")
        t = cx.gsb(shp, F32)
        cx.dma(t[:], d, writes=[nm])
        C[nm] = t
    for nm, shp in SMALL_BF.items():
        d = cx.dram(nm, shp, BF16, "ExternalInput")
        t = cx.gsb(shp, BF16)
        cx.dma(t[:], d, writes=[nm])
        C[nm] = t
    C["ON"] = C["ONES"]
    p.last_w["ON"] = p.last_w["ONES"]
    C["MSb"] = cx.gsb([128, 128], BF16)
    C["MIb"] = cx.gsb([128, 128], BF16)
    C["ZR"] = cx.gsb([128, 512], BF16)
    C["one"] = cx.gsb([1, 1], F32)
    p.add("dve", lambda e: e.tensor_copy(C["MSb"][:], C["MS"][:]), reads=["MS"], writes=["MSb"])
    p.add("dve", lambda e: e.tensor_copy(C["MIb"][:], C["MI"][:]), reads=["MI"], writes=["MIb"])
    p.add("pool", lambda e: e.memset(C["ZR"][:], 0.0), writes=["ZR"])
    p.add("pool", lambda e: e.memset(C["one"][:], 1.0), writes=["one"])

    def idram(name, shape, dt):
        return nc.dram_tensor(name, list(shape), dt).ap()

    RB = (512, 448, 448)
    XI1 = [[idram("XI1_%d_%d" % (r, s_), [4 * 384, TL], BF16) for s_ in range(4)] for r in range(3)]
    XO1 = [[idram("XO1_%d_%d" % (r, s_), [384, TL], BF16) for s_ in range(4)] for r in range(3)]
    XI2 = [idram("XI2_%d" % r, [4 * 512, TL], BF16) for r in range(3)]
    XO2 = [idram("XO2_%d" % r, [512, TL], BF16) for r in range(3)]
    XI3 = [[idram("XI3_%d_%d" % (r, s_), [4 * RB[r], TL], BF16) for s_ in range(4)] for r in range(3)]
    XO3 = [[idram("XO3_%d_%d" % (r, s_), [RB[r], TL], BF16) for s_ in range(4)] for r in range(3)]
    XI4 = [idram("XI4_%d" % r, [4 * 512, TL], BF16) for r in range(3)]
    XO4 = [idram("XO4_%d" % r, [512, TL], BF16) for r in range(3)]
    GT, MmT = idram("GT", [1536, TL], BF16), idram("MmT", [512, TL], BF16)
    G2T, Mm2T = idram("G2T", [1536, TL], BF16), idram("Mm2T", [512, TL], BF16)
    X1T = idram("X1T", [2048, TL], F32)

    def xchg3(xis, xos):
        prev = []
        for r in range(3):
            for s_ in range(4):
                xi, xo = xis[r][s_], xos[r][s_]
                op = p.add_cc(lambda e, xi=xi, xo=xo: e.collective_compute("ReduceScatter", ALU.add, replica_groups=GROUPS,
                                                                            ins=[xi], outs=[xo]),
                              writes=[("XO", r, s_)], deps=prev)
                prev = [op]

    def mkcc(xis, xos):
        def cc(r, deps):
            xi, xo = xis[r], xos[r]
            p.add_cc(lambda e: e.collective_compute("ReduceScatter", ALU.add, replica_groups=GROUPS, ins=[xi], outs=[xo]),
                     writes=[("XO", r)], deps=deps)
        return cc

    cx.begin_phase()
    tk = Tok(cx, TL, C)
    phase_la1(cx, tk, dict(xT=ext["xT"], a_w_in=ext["a_w_in"], memT=ext["memT"], w_mem_kv0=ext["w_mem_kv0"],
                           XI1=XI1, GT=GT, MmT=MmT), C, TL)
    cx.end_phase()
    xchg3(XI1, XO1)
    cx.begin_phase()
    io = dict(QT=lambda h, sl: XO1[h][sl][0:128, :], KT=lambda h, sl: XO1[h][sl][128:256, :],
              V=lambda h, sl: XO1[h][sl][256:384, :].rearrange("p (k d) -> p k d", d=128),
              out=attn_out_writer(cx, C, XI2, XO2, SEQ // 512))
    phase_sb(cx, io, C, 3, SEQ, 128 ** -0.5)
    cx.end_phase()
    cx.begin_phase()
    tk = Tok(cx, TL, C)
    phase_out(cx, tk, dict(OT=XO2, GT=GT, MmT=MmT, XR=ext["xT"], W=ext["a_w_out"], Y=X1T), TL)
    cx.end_phase()
    TH = 1024
    for hh in range(TL // TH):
        cs = slice(hh * TH, (hh + 1) * TH)
        cx.begin_phase()
        tk = Tok(cx, TH, C)
        phase_lb1(cx, tk, dict(X1T=X1T[:, cs], pos=ext["pos"][cs], w_dkv=ext["w_dkv"], w_ukv=ext["w_ukv"],
                               b_w_in=ext["b_w_in"], b_w_uq=ext["b_w_uq"], memT=ext["memT"], w_mem_kv=ext["w_mem_kv1"],
                               G2T=G2T[:, cs], Mm2T=Mm2T[:, cs], XI3=XI3), TH, C, hh * TH)
        cx.end_phase()
    xchg3(XI3, XO3)
    cx.begin_phase()
    io = dict(QnT=lambda h, sl: XO3[h][sl][0:128, :], KnT=lambda h, sl: XO3[h][sl][128:256, :],
              V=lambda h, sl: XO3[h][sl][256:384, :].rearrange("p (k d) -> p k d", d=128),
              QrT=lambda h, sl: XO3[h][sl][384:448, :], KrT=lambda sl: XO3[0][sl][448:512, :],
              out=attn_out_writer(cx, C, XI4, XO4, SEQ // 512))
    phase_mla(cx, io, C, 3, SEQ, 192 ** -0.5)
    cx.end_phase()
    cx.begin_phase()
    tk = Tok(cx, TL, C)
    phase_out(cx, tk, dict(OT=XO4, GT=G2T, MmT=Mm2T, XR=X1T, W=ext["b_w_out"], Y=Y, final=True), TL)
    cx.end_phase()
    return cx.finish()


def _g16(g):
    return np.ascontiguousarray(np.asarray(g, np.float32).reshape(-1, 128).T)


def _col(g):
    g = np.asarray(g, np.float32)
    o = np.zeros((128, 1), np.float32)
    o[:len(g), 0] = g
    return o


def make_inputs(x, mem, positions, a_norm, a_w_in, a_w_out, kv_norm, w_dkv, g_ckv, w_ukv,
                g_k_nope, g_k_rope, b_norm, b_w_in, b_g_q_lat, b_w_uq, b_g_q_nope, b_g_q_rope,
                b_w_out, mem_norm, w_mem_kv, g_mem_q, g_mem_k):
    f32 = lambda a: np.ascontiguousarray(np.asarray(a, np.float32))
    x, mem = f32(x), f32(mem)
    positions = np.asarray(positions, np.int32)
    c = host_consts()
    shared = dict(a_w_in=f32(a_w_in[0]), a_w_out=f32(a_w_out[0]), w_dkv=f32(w_dkv), w_ukv=f32(w_ukv), b_w_in=f32(b_w_in[0]),
                  b_w_uq=f32(b_w_uq[0]), b_w_out=f32(b_w_out[0]), w_mem_kv0=f32(w_mem_kv[0]), w_mem_kv1=f32(w_mem_kv[1]),
                  a_norm=_g16(a_norm[0]), mem_norm0=_g16(mem_norm[0]), g_mem=_g16(mem_norm[1]), g_mem_q0=_col(g_mem_q[0]),
                  g_mem_k0=_col(g_mem_k[0]), g_q=_col(g_mem_q[1]), g_k=_col(g_mem_k[1]), g_kv=_g16(kv_norm), g_b=_g16(b_norm[0]),
                  g_ckv=_g16(g_ckv), g_q_lat=_g16(b_g_q_lat[0]), g_k_nope=_col(g_k_nope), g_k_rope=_col(g_k_rope),
                  g_q_nope=_col(b_g_q_nope[0]), g_q_rope=_col(b_g_q_rope[0]), invf=c["invf"], PERM=c["PERM"],
                  MS=c["MS"], MI=c["MI"], ONES=c["ONES"], U=c["U"], Uc=c["Uc"], Z=c["Z"])
    ims = []
    for b in range(BATCH):
        memT = np.ascontiguousarray(mem[b].T)
        for j in range(4):
            sel = np.zeros((128, 4), np.float32)
            sel[:, j] = 1.0
            d = dict(shared)
            d.update(xT=np.ascontiguousarray(x[b, j * TL:(j + 1) * TL].T), memT=memT,
                     pos=np.ascontiguousarray(positions[b, j * TL:(j + 1) * TL]), sel=sel)
            ims.append(d)
    return ims


_NC_CACHE = {}


def kernel(**inputs):
    ims = make_inputs(**inputs)
    if "nc" not in _NC_CACHE:
        _NC_CACHE["nc"] = build_fused()
    res = run_spmd(_NC_CACHE["nc"], ims)
    out = np.empty((BATCH, SEQ, 2048), np.float32)
    for i in range(NCORE):
        b, j = divmod(i, 4)
        out[b, j * TL:(j + 1) * TL] = res[i]["Y"].T
    return out
```
